# Optimizing a Trainium2 kernel written in Bass

```python
import math
import jax, jax.numpy as jnp
from jax import lax
import numpy as np

D_MODEL = 2048
BATCH = 4
SEQ = 2048
DEPTH = 2
DEC_BATCH = 128
DEC_SEQ = 1
PAST_LEN = 16384
PAGE_SIZE = 128

N_MIXERS = 2
N_CONV_LAYERS = (DEPTH + 1) // 2
N_SSD_LAYERS = DEPTH // 2
SC_WIDTH = 3
SSD_EXPAND = 2
D_INNER = SSD_EXPAND * D_MODEL
SSD_HEAD_DIM = 64
SSD_HEADS = D_INNER // SSD_HEAD_DIM
SSD_GROUPS = 8
SSD_HPG = SSD_HEADS // SSD_GROUPS
SSD_STATE = 128
SSD_CONV_WIDTH = 4
SSD_CONV_DIM = D_INNER + 2 * SSD_GROUPS * SSD_STATE
SSD_IN_DIM = D_INNER + SSD_CONV_DIM + SSD_HEADS
SSD_CHUNK = 128
MEM_LEN = 256
MEM_HEADS = 4
MEM_HEAD_DIM = D_MODEL // MEM_HEADS
D_FF = -(-8 * D_MODEL // (3 * 256)) * 256
RMS_EPS = 1e-5

kernel_name = "hybrid_shortconv_ssd_memxattn_step"


def rmsnorm(x, g):
    xf = x.astype(jnp.float32)
    y = xf * lax.rsqrt(jnp.mean(xf * xf, axis=-1, keepdims=True) + RMS_EPS)
    return (y * g.astype(jnp.float32)).astype(x.dtype)


def causal_dwconv(u_ext, w):
    c = u_ext.shape[-1]
    return lax.conv_general_dilated(u_ext, w[:, None, :].astype(u_ext.dtype), window_strides=(1,), padding='VALID',
                                    dimension_numbers=('NWC', 'WIO', 'NWC'), feature_group_count=c)


def short_conv_mixer(h, hist, w_in, w_conv, w_out):
    b_gate, c_gate, v = jnp.split(h @ w_in, 3, axis=-1)
    u_ext = jnp.concatenate([hist.astype(h.dtype), c_gate * v], axis=1)
    y = b_gate * causal_dwconv(u_ext, w_conv)
    return y @ w_out, u_ext[:, -(SC_WIDTH - 1):]


def ssd_scan(x, dt, a, b, c, s0):
    f32 = jnp.float32
    bsz, L = x.shape[0], x.shape[1]
    q = min(SSD_CHUNK, L)
    nc = -(-L // q)
    pad = nc * q - L
    x, b, c, dt = (t.astype(f32) for t in (x, b, c, dt))
    if pad:
        padw = lambda t: jnp.pad(t, [(0, 0), (0, pad)] + [(0, 0)] * (t.ndim - 2))
        x, b, c, dt = padw(x), padw(b), padw(c), padw(dt)
    x = x.reshape(bsz, nc, q, SSD_GROUPS, SSD_HPG, SSD_HEAD_DIM)
    dt = dt.reshape(bsz, nc, q, SSD_GROUPS, SSD_HPG)
    b = b.reshape(bsz, nc, q, SSD_GROUPS, SSD_STATE)
    c = c.reshape(bsz, nc, q, SSD_GROUPS, SSD_STATE)
    cs = jnp.cumsum(dt * a.astype(f32).reshape(SSD_GROUPS, SSD_HPG), axis=2)
    tri = jnp.tril(jnp.ones((q, q), dtype=bool))[None, None, :, :, None, None]
    seg = cs[:, :, :, None] - cs[:, :, None, :]
    decay = jnp.exp(jnp.where(tri, seg, -jnp.inf))
    cb = jnp.einsum('bctgn,bcsgn->bctsg', c, b)
    w_ts = cb[..., None] * decay * dt[:, :, None]
    y_diag = jnp.einsum('bctsgj,bcsgjp->bctgjp', w_ts, x)
    to_end = jnp.exp(cs[:, :, -1:] - cs) * dt
    chunk_states = jnp.einsum('bcsgj,bcsgn,bcsgjp->bcgjpn', to_end, b, x)
    chunk_decay = jnp.exp(cs[:, :, -1])
    s_init = s0.astype(f32).reshape(bsz, SSD_GROUPS, SSD_HPG, SSD_HEAD_DIM, SSD_STATE)

    def step(s, inp):
        st, cd = inp
        return s * cd[..., None, None] + st, s

    s_final, s_in = lax.scan(step, s_init, (jnp.swapaxes(chunk_states, 0, 1), jnp.swapaxes(chunk_decay, 0, 1)))
    s_in = jnp.swapaxes(s_in, 0, 1)
    y_off = jnp.einsum('bctgn,bcgjpn,bctgj->bctgjp', c, s_in, jnp.exp(cs))
    y = (y_diag + y_off).reshape(bsz, nc * q, SSD_HEADS, SSD_HEAD_DIM)[:, :L]
    return y, s_final.reshape(bsz, SSD_HEADS, SSD_HEAD_DIM, SSD_STATE)


def ssd_mixer(h, conv_hist, state, w_in, conv_w, conv_b, dt_bias, a_log, d_skip, norm_g, w_out):
    bsz, L, _ = h.shape
    z, xbc, dt_raw = jnp.split(h @ w_in, [D_INNER, D_INNER + SSD_CONV_DIM], axis=-1)
    xbc_ext = jnp.concatenate([conv_hist.astype(h.dtype), xbc], axis=1)
    xbc_c = jax.nn.silu(causal_dwconv(xbc_ext, conv_w) + conv_b.astype(h.dtype))
    xs, b_in, c_in = jnp.split(xbc_c, [D_INNER, D_INNER + SSD_GROUPS * SSD_STATE], axis=-1)
    xs = xs.reshape(bsz, L, SSD_HEADS, SSD_HEAD_DIM)
    b_in = b_in.reshape(bsz, L, SSD_GROUPS, SSD_STATE)
    c_in = c_in.reshape(bsz, L, SSD_GROUPS, SSD_STATE)
    dt = jax.nn.softplus(dt_raw.astype(jnp.float32) + dt_bias.astype(jnp.float32))
    a = -jnp.exp(a_log.astype(jnp.float32))
    y, new_state = ssd_scan(xs, dt, a, b_in, c_in, state)
    y = y + d_skip.astype(jnp.float32)[:, None] * xs.astype(jnp.float32)
    g = (y.reshape(bsz, L, D_INNER) * jax.nn.silu(z.astype(jnp.float32))).reshape(bsz, L, SSD_GROUPS, D_INNER // SSD_GROUPS)
    g = rmsnorm(g, norm_g.reshape(SSD_GROUPS, D_INNER // SSD_GROUPS)).reshape(bsz, L, D_INNER).astype(h.dtype)
    return g @ w_out, xbc_ext[:, -(SSD_CONV_WIDTH - 1):], new_state.astype(state.dtype)


def mem_project(mem, g, w_k, w_v):
    bsz = mem.shape[0]
    m = rmsnorm(mem, g)
    k = (m @ w_k).reshape(bsz, MEM_LEN, MEM_HEADS, MEM_HEAD_DIM)
    v = (m @ w_v).reshape(bsz, MEM_LEN, MEM_HEADS, MEM_HEAD_DIM)
    return k, v


def mem_attend(h, k, v, w_q, w_o):
    bsz, L, _ = h.shape
    q = (h @ w_q).reshape(bsz, L, MEM_HEADS, MEM_HEAD_DIM)
    s = jnp.einsum('blhd,bmhd->bhlm', q, k.astype(q.dtype)).astype(jnp.float32) * (MEM_HEAD_DIM ** -0.5)
    p = jax.nn.softmax(s, axis=-1).astype(h.dtype)
    o = jnp.einsum('bhlm,bmhd->blhd', p, v.astype(h.dtype)).reshape(bsz, L, D_MODEL)
    return o @ w_o


def swiglu(h, w_gate, w_up, w_down):
    return (jax.nn.silu(h @ w_gate) * (h @ w_up)) @ w_down


def trunk(x, sc_hist, ssd_conv_hist, ssd_state, mem_k, mem_v, p):
    new_sc, new_sconv, new_ss = [], [], []
    h = x
    for i in range(DEPTH):
        hn = rmsnorm(h, p['norm_mix'][i])
        if i % N_MIXERS == 0:
            a = i // N_MIXERS
            out, hist = short_conv_mixer(hn, sc_hist[a], p['sc_w_in'][a], p['sc_w_conv'][a], p['sc_w_out'][a])
            new_sc.append(hist)
        else:
            j = i // N_MIXERS
            out, chist, st = ssd_mixer(hn, ssd_conv_hist[j], ssd_state[j], p['ssd_w_in'][j], p['ssd_conv_w'][j],
                                       p['ssd_conv_b'][j], p['ssd_dt_bias'][j], p['ssd_a_log'][j], p['ssd_d'][j],
                                       p['ssd_norm'][j], p['ssd_w_out'][j])
            new_sconv.append(chist)
            new_ss.append(st)
        h = h + out
        h = h + mem_attend(rmsnorm(h, p['norm_mem_q'][i]), mem_k[i], mem_v[i], p['xa_w_q'][i], p['xa_w_o'][i])
        h = h + swiglu(rmsnorm(h, p['norm_ffn'][i]), p['ffn_w_gate'][i], p['ffn_w_up'][i], p['ffn_w_down'][i])
    return rmsnorm(h, p['norm_final']), jnp.stack(new_sc), jnp.stack(new_sconv), jnp.stack(new_ss)


def setup_inputs(seed: int = 0) -> dict:
    key = jax.random.key(seed)
    ks = iter(jax.random.split(key, 64))
    f32 = jnp.float32

    def nrm(shape, scale):
        return jax.random.normal(next(ks), shape, f32) * scale

    def gain(shape):
        return 1.0 + nrm(shape, 0.02)

    n_a, n_b = N_CONV_LAYERS, N_SSD_LAYERS
    dt0 = jnp.exp(jax.random.uniform(next(ks), (n_b, SSD_HEADS), f32, math.log(1e-3), math.log(1e-1)))
    dt_bias = dt0 + jnp.log(-jnp.expm1(-dt0))
    a_log = jnp.log(jax.random.uniform(next(ks), (n_b, SSD_HEADS), f32, 1.0, 16.0))
    sd = D_MODEL ** -0.5
    return {
        "x_prompt": nrm((BATCH, SEQ, D_MODEL), 1.0),
        "x_sample": nrm((DEC_BATCH, DEC_SEQ, D_MODEL), 1.0),
        "mem_prompt": nrm((BATCH, MEM_LEN, D_MODEL), 1.0),
        "cache_sc": nrm((n_a, DEC_BATCH, SC_WIDTH - 1, D_MODEL), 1.0),
        "state_ssd_conv": nrm((n_b, DEC_BATCH, SSD_CONV_WIDTH - 1, SSD_CONV_DIM), 1.0),
        "state_ssd": nrm((n_b, DEC_BATCH, SSD_HEADS, SSD_HEAD_DIM, SSD_STATE), 0.1),
        "cache_mem_k": nrm((DEPTH, DEC_BATCH, MEM_LEN, MEM_HEADS, MEM_HEAD_DIM), 1.0),
        "cache_mem_v": nrm((DEPTH, DEC_BATCH, MEM_LEN, MEM_HEADS, MEM_HEAD_DIM), 1.0),
        "norm_mix": gain((DEPTH, D_MODEL)),
        "norm_mem_q": gain((DEPTH, D_MODEL)),
        "norm_mem_kv": gain((DEPTH, D_MODEL)),
        "norm_ffn": gain((DEPTH, D_MODEL)),
        "norm_final": gain((D_MODEL,)),
        "sc_w_in": nrm((n_a, D_MODEL, 3 * D_MODEL), sd),
        "sc_w_conv": nrm((n_a, SC_WIDTH, D_MODEL), SC_WIDTH ** -0.5),
        "sc_w_out": nrm((n_a, D_MODEL, D_MODEL), sd),
        "ssd_w_in": nrm((n_b, D_MODEL, SSD_IN_DIM), sd),
        "ssd_conv_w": nrm((n_b, SSD_CONV_WIDTH, SSD_CONV_DIM), SSD_CONV_WIDTH ** -0.5),
        "ssd_conv_b": nrm((n_b, SSD_CONV_DIM), 0.01),
        "ssd_dt_bias": dt_bias,
        "ssd_a_log": a_log,
        "ssd_d": 1.0 + nrm((n_b, SSD_HEADS), 0.1),
        "ssd_norm": gain((n_b, D_INNER)),
        "ssd_w_out": nrm((n_b, D_INNER, D_MODEL), D_INNER ** -0.5),
        "xa_w_q": nrm((DEPTH, D_MODEL, D_MODEL), sd),
        "xa_w_k": nrm((DEPTH, D_MODEL, D_MODEL), sd),
        "xa_w_v": nrm((DEPTH, D_MODEL, D_MODEL), sd),
        "xa_w_o": nrm((DEPTH, D_MODEL, D_MODEL), sd),
        "ffn_w_gate": nrm((DEPTH, D_MODEL, D_FF), sd),
        "ffn_w_up": nrm((DEPTH, D_MODEL, D_FF), sd),
        "ffn_w_down": nrm((DEPTH, D_FF, D_MODEL), D_FF ** -0.5),
    }


def reference(x_prompt, x_sample, mem_prompt, cache_sc, state_ssd_conv, state_ssd, cache_mem_k, cache_mem_v,
              norm_mix, norm_mem_q, norm_mem_kv, norm_ffn, norm_final,
              sc_w_in, sc_w_conv, sc_w_out,
              ssd_w_in, ssd_conv_w, ssd_conv_b, ssd_dt_bias, ssd_a_log, ssd_d, ssd_norm, ssd_w_out,
              xa_w_q, xa_w_k, xa_w_v, xa_w_o, ffn_w_gate, ffn_w_up, ffn_w_down):
    p = {'norm_mix': norm_mix, 'norm_mem_q': norm_mem_q, 'norm_ffn': norm_ffn, 'norm_final': norm_final,
         'sc_w_in': sc_w_in, 'sc_w_conv': sc_w_conv, 'sc_w_out': sc_w_out,
         'ssd_w_in': ssd_w_in, 'ssd_conv_w': ssd_conv_w, 'ssd_conv_b': ssd_conv_b, 'ssd_dt_bias': ssd_dt_bias,
         'ssd_a_log': ssd_a_log, 'ssd_d': ssd_d, 'ssd_norm': ssd_norm, 'ssd_w_out': ssd_w_out,
         'xa_w_q': xa_w_q, 'xa_w_o': xa_w_o, 'ffn_w_gate': ffn_w_gate, 'ffn_w_up': ffn_w_up, 'ffn_w_down': ffn_w_down}
    dt = x_prompt.dtype
    kv = [mem_project(mem_prompt, norm_mem_kv[i], xa_w_k[i], xa_w_v[i]) for i in range(DEPTH)]
    mem_k_p = jnp.stack([k for k, _ in kv])
    mem_v_p = jnp.stack([v for _, v in kv])
    zero_sc = jnp.zeros((N_CONV_LAYERS, BATCH, SC_WIDTH - 1, D_MODEL), dt)
    zero_sconv = jnp.zeros((N_SSD_LAYERS, BATCH, SSD_CONV_WIDTH - 1, SSD_CONV_DIM), dt)
    zero_ss = jnp.zeros((N_SSD_LAYERS, BATCH, SSD_HEADS, SSD_HEAD_DIM, SSD_STATE), dt)
    y_prompt, sc_p, sconv_p, ss_p = trunk(x_prompt, zero_sc, zero_sconv, zero_ss, mem_k_p, mem_v_p, p)
    y_sample, sc_s, sconv_s, ss_s = trunk(x_sample, cache_sc, state_ssd_conv, state_ssd, cache_mem_k, cache_mem_v, p)
    return (y_prompt, y_sample, sc_p, sc_s, sconv_p, sconv_s, ss_p, ss_s, mem_k_p, mem_v_p)
```

```python
import numpy as np
import concourse.bass as bass
import concourse.mybir as mybir
from concourse.bass_utils import run_bass_kernel_spmd

F32 = mybir.dt.float32
BF16 = mybir.dt.bfloat16
AF = mybir.ActivationFunctionType
ALU = mybir.AluOpType
AX = mybir.AxisListType

D = 2048
KT = 16
NS, NH, NP = 16, 8, 1024
S0, H0, P0 = 0, 16, 24
NCOL = NS + NH + NP
BLKS = [(0, 24), (24, 512), (536, 512)]
DFF = 5632
DIN = 4096
CONV = 6144
EPS = 1e-5
NCORES = 8


class Op:
    __slots__ = ("eng", "idx", "fn", "waits", "signal", "dma", "rank", "real")

    def __init__(self, eng, idx, fn, dma):
        self.eng, self.idx, self.fn, self.dma = eng, idx, fn, dma
        self.waits = []
        self.signal = False
        self.rank = None
        self.real = True


class Sched:
    ENGS = ("pe", "act", "dve", "pool", "sp")

    def __init__(self):
        self.q = {e: [] for e in self.ENGS}
        self.lastw = {}
        self.readers = {}
        self.seen = {e: {} for e in self.ENGS}
        self.ndma = {"pool": 0, "sp": 0}
        self.NDS = 6

    def op(self, eng, fn, reads=(), writes=(), dma=False, real=True):
        o = Op(eng, len(self.q[eng]), fn, None)
        o.real = real
        deps = []
        for k in reads:
            w = self.lastw.get(k)
            if w is not None:
                deps.append((w, True))
        for k in writes:
            w = self.lastw.get(k)
            if w is not None:
                deps.append((w, False))
            for r in self.readers.get(k, ()):
                deps.append((r, False))
        if dma:
            i = self.ndma[eng]
            self.ndma[eng] += 1
            o.dma = (eng, i % self.NDS, 16 * (i // self.NDS + 1))
            if i >= self.NDS:
                o.waits.append(("dma", eng, i % self.NDS, 16 * (i // self.NDS)))
        for d, raw in deps:
            if d is o:
                continue
            if d.dma is not None:
                key = ("dma", d.dma[0], d.dma[1])
                if self.seen[eng].get(key, 0) >= d.dma[2]:
                    continue
                self.seen[eng][key] = d.dma[2]
                o.waits.append(("dma", d.dma[0], d.dma[1], d.dma[2]))
            else:
                if d.eng == eng and (eng == "pe" or not raw):
                    continue
                if self.seen[eng].get(d.eng, -1) >= d.idx:
                    continue
                self.seen[eng][d.eng] = d.idx
                d.signal = True
                o.waits.append(("op", d))
        for k in reads:
            self.readers.setdefault(k, []).append(o)
        for k in writes:
            self.lastw[k] = o
            self.readers[k] = []
        self.q[eng].append(o)
        return o

    def replay(self, nc, block, sems, dsems):
        for e in self.ENGS:
            r = 0
            for o in self.q[e]:
                if o.signal:
                    r += 1
                    o.rank = r
        names = {"pe": "tensor", "act": "scalar", "dve": "vector", "pool": "gpsimd", "sp": "sync"}
        fin = Op("sp", len(self.q["sp"]), (lambda e: None), None)
        fin.real = False
        for qn, n in self.ndma.items():
            for si in range(min(n, self.NDS)):
                last_i = ((n - 1 - si) // self.NDS) * self.NDS + si
                fin.waits.append(("dma", qn, si, 16 * (last_i // self.NDS + 1)))
        self.q["sp"].append(fin)

        def make(ename):
            def body(eng):
                for o in self.q[ename]:
                    for w in o.waits:
                        if w[0] == "dma":
                            eng.wait_ge(dsems[w[1]][w[2]], w[3])
                        else:
                            eng.wait_ge(sems[w[1].eng], w[1].rank)
                    ins = o.fn(eng)
                    if ins is None:
                        continue
                    if o.dma is not None:
                        ins.then_inc(dsems[o.dma[0]][o.dma[1]], 16)
                    elif o.signal:
                        ins.then_inc(sems[ename], 1)
            return body

        for ename in self.ENGS:
            getattr(block, names[ename])(make(ename))


PV_NMIX, PV_NQ, PV_NKV, PV_NFFN, PV_NFIN = 0, 32, 64, 96, 128
PV_SCW, PV_CW, PV_CB, PV_SNORM = 144, 192, 384, 432
NSLOT = 3
SLOTW = 4096


def build_program():
    import os
    nc = bass.Bass("TRN2", target_bir_lowering=False)
    S = Sched()
    from contextlib import ExitStack
    es = ExitStack()

    def din(name, shape):
        return nc.dram_tensor(name, list(shape), F32, kind="ExternalInput").ap()

    def dout(name, shape):
        return nc.dram_tensor(name, list(shape), F32, kind="ExternalOutput").ap()

    _in_shapes = {
        "xp": [NP, D], "xh": [NH, D], "xs": [NS, D], "mem": [256, D], "csc": [2 * NS, D], "ccv": [3 * NS, CONV],
        "sst": [NS, 4096, 128], "ck": [2, NS, 256, D], "cv": [2, NS, 256, D],
        "pvec": [128, 512], "tokbc": [128, 192], "cst": [128, 1160], "hvec": [64, 8], "selb": [16, 2048], "lg": [64, 140],
        "sc_w_in": [D, 3 * D], "sc_w_out": [D, D], "ssd_w_in": [D, 10304], "ssd_w_out": [DIN, D],
        "xa_w_q": [2, D, D], "xa_w_k": [2, D, D], "xa_w_v": [2, D, D], "xa_w_o": [2, D, D],
        "ffn_w_gate": [2, D, DFF], "ffn_w_up": [2, D, DFF], "ffn_w_down": [2, DFF, D],
    }
    _in_aps = {}

    def IN(name):
        if name not in _in_aps:
            _in_aps[name] = din(name, _in_shapes[name])
        return _in_aps[name]

    yp_o = dout("y_p", [NP, D]); ys_o = dout("y_s", [NS, D]); scp_o = dout("sc_p", [2, D]); scs_o = dout("sc_s", [2 * NS, D])
    cvp_o = dout("cv_p", [3, CONV]); cvs_o = dout("cv_s", [3 * NS, CONV]); ssp_o = dout("ss_p", [4096, 128])
    sss_o = dout("ss_s", [NS, 4096, 128]); mk_o = dout("mk", [2, 256, D]); mv_o = dout("mv", [2, 256, D])

    cc_src = nc.dram_tensor("cc_src", [1024, 512], F32, kind="Internal").ap()
    cc_dst = nc.dram_tensor("cc_dst", [2048, 512], F32, kind="Internal").ap()

    def sb(name, shape, dt=F32):
        return es.enter_context(nc.sbuf_tensor("sb_" + name, list(shape), dt))

    hT = sb("hT", [128, KT, NCOL])
    xn = sb("xn", [128, KT, NCOL], BF16)
    bB = sb("bufB", [128, KT, NCOL], BF16)
    wsl = [sb(f"wslot{i}", [128, SLOTW], BF16) for i in range(NSLOT)]
    pvec = sb("pvec", [128, 512]); tokbc = sb("tokbc", [128, 192]); cst = sb("cst", [128, 1160])
    hvec = sb("hvec", [64, 8]); selb = sb("selb", [16, 2048], BF16); lg = sb("lg", [64, 140])
    identb = sb("identb", [128, 128], BF16); onesb = sb("onesb", [128, 128], BF16)
    ARW = int(os.environ.get("ARW", "10160")) if True else 0
    ar = sb("arena", [128, ARW])
    ps = es.enter_context(nc.psum_tensor("ps", [128, 6, 512], F32))
    psb = es.enter_context(nc.psum_tensor("psb", [128, 2, 1024], BF16))

    xnf = xn[:].rearrange("p k n -> p (k n)")
    bBf = bB[:].rearrange("p k n -> p (k n)")
    XN_ALL = [("xn", k, bi) for k in range(KT) for bi in range(3)]
    BB_ALL = [("bB", k, bi) for k in range(KT) for bi in range(3)]

    def arf(o, n):
        return ar[:, o:o + n]

    def arb(o, n):
        return ar[:, o:o + n].bitcast(BF16)

    sq = [arb(0, 256), arb(256, 256)]
    rstd = arf(512, 512)
    sm = arf(1024, 512)
    stg = [arf(1536, 2048), arf(3584, 2048)]
    PX = 5632
    PXN = ARW - PX

    ident = cst[:, 0:128]; tri = cst[:, 128:256]; ones = cst[:, 256:384]; mneg4 = cst[:, 384:896]
    d16 = cst[:, 896:1152].rearrange("p (b c) -> p b c", b=NS)
    flag = cst[:, 1152:1153]

    state = {"bank": 0, "pb": 0, "slot": 0, "stg": 0}
    outkeys = []

    def bank():
        b = state["bank"]; state["bank"] = (b + 1) % 6
        return b

    def pbank():
        b = state["pb"]; state["pb"] = (b + 1) % 2
        return b

    def mm(out, lhsT, rhs, start, stop, reads, writes):
        S.op("pe", lambda e: e.matmul(out, lhsT=lhsT, rhs=rhs, start=start, stop=stop), reads, writes)

    def tr(out, in_, idn, reads, writes):
        S.op("pe", lambda e: e.transpose(out, in_, idn), reads, writes)

    def act(out, in_, func, reads, writes, bias=None, scale=None, accum=None):
        kw = {}
        if bias is not None: kw["bias"] = bias
        if scale is not None: kw["scale"] = scale
        if accum is not None: kw["accum_out"] = accum
        S.op("act", lambda e: e.activation(out=out, in_=in_, func=func, **kw), reads, writes)

    def tt(out, a, b, op, reads, writes, eng="dve"):
        S.op(eng, lambda e: e.tensor_tensor(out=out, in0=a, in1=b, op=op), reads, writes)

    def ts(out, a, s1, s2, op0, op1, reads, writes, eng="dve"):
        if op1 is None:
            S.op(eng, lambda e: e.tensor_scalar(out=out, in0=a, scalar1=s1, scalar2=None, op0=op0), reads, writes)
        else:
            S.op(eng, lambda e: e.tensor_scalar(out=out, in0=a, scalar1=s1, scalar2=s2, op0=op0, op1=op1), reads, writes)

    def stt(out, a, s, b, op0, op1, reads, writes):
        S.op("dve", lambda e: e.scalar_tensor_tensor(out=out, in0=a, scalar=s, in1=b, op0=op0, op1=op1), reads, writes)

    def cp(out, in_, reads, writes, eng="dve"):
        if eng == "act":
            S.op("act", lambda e: e.copy(out=out, in_=in_), reads, writes)
        else:
            S.op(eng, lambda e: e.tensor_copy(out=out, in_=in_), reads, writes)

    def vop(fn, reads, writes):
        S.op("dve", fn, reads, writes)

    def dma(q, out, in_, reads, writes):
        S.op(q, lambda e: e.dma_start(out=out, in_=in_), reads, writes, dma=True)

    cpi = [0]

    def cp_alt(out, in_, reads, writes):
        cpi[0] += 1
        cp(out, in_, reads, writes, eng=("act" if cpi[0] % 2 else "dve"))

    def barrier():
        engs = ("pe", "act", "dve")
        lasts = {}
        for e in engs:
            for o_ in reversed(S.q[e]):
                if o_.real:
                    lasts[e] = o_
                    break
        for e in engs:
            o = S.op(e, lambda eng: None, real=False)
            for e2, l in lasts.items():
                if e2 != e and S.seen[e].get(e2, -1) < l.idx:
                    S.seen[e][e2] = l.idx
                    l.signal = True
                    o.waits.append(("op", l))

    dma("sp", pvec[:], IN("pvec"), [], ["pvec"]); dma("sp", tokbc[:], IN("tokbc"), [], ["tokbc"]); dma("sp", cst[:], IN("cst"), [], ["cst"])
    dma("sp", hvec[:], IN("hvec"), [], ["hvec"]); dma("sp", lg[:], IN("lg"), [], ["lg"])
    dma("pool", selb[:], IN("selb"), [], ["selb"])
    import os
    KDBG = int(os.environ.get("KDBG", "9"))
    cp(identb[:], ident, ["cst"], ["identb"]); cp(onesb[:], ones, ["cst"], ["onesb"])
    if KDBG >= 1:
        act(hvec[:, 3:4], hvec[:, 1:2], AF.Exp, ["hvec"], ["hvec_a"])
        ts(hvec[:, 3:4], hvec[:, 3:4], -1.0, None, ALU.mult, None, ["hvec_a"], ["hvec_a"])
        act(tokbc[:, 64:128], tokbc[:, 64:128], AF.Exp, ["tokbc"], ["tokbc"])
        ts(tokbc[:, 64:128], tokbc[:, 64:128], -1.0, None, ALU.mult, None, ["tokbc"], ["tokbc"])
    a_bc = tokbc[:, 64:128]; d_bc = tokbc[:, 128:192]

    def in_transpose(rows_ap, R, C, dst, dst_keys_fn):
        si = state["stg"]; state["stg"] ^= 1
        st = stg[si]
        dma("sp", st[0:R, 0:C], rows_ap, [], [("stg", si)])
        nct = C // 128
        for c0 in range(0, nct, 4):
            n = min(4, nct - c0)
            b = bank()
            for j in range(n):
                tr(ps[:, b, j * 128:j * 128 + R], st[0:R, (c0 + j) * 128:(c0 + j + 1) * 128], ident[0:R, 0:R],
                   [("stg", si), "cst"], [("ps", b)])
            cpi[0] += 1
            eng_ = "act" if cpi[0] % 2 else "dve"
            for j in range(n):
                cp(dst(c0 + j), ps[:, b, j * 128:j * 128 + R], [("ps", b)], dst_keys_fn(c0 + j), eng=eng_)

    def out_transpose(src, src_keys_fn, R, C, out_ap, okey):
        si = state["stg"]; state["stg"] ^= 1
        st = stg[si]
        nct = C // 128
        for c0 in range(0, nct, 4):
            n = min(4, nct - c0)
            b = bank()
            for j in range(n):
                tr(ps[0:R, b, j * 128:(j + 1) * 128], src(c0 + j), ident, list(src_keys_fn(c0 + j)) + ["cst"], [("ps", b)])
            cp_alt(st[0:R, c0 * 128:(c0 + n) * 128], ps[0:R, b, 0:n * 128], [("ps", b)], [("stg", si)])
        dma("sp", out_ap, st[0:R, 0:C], [("stg", si)], [okey])
        outkeys.append(okey)

    def hkeys(kt, bi):
        return ("hT", kt, bi)

    def rmsnorm_cols(src, src_key, dstf, dst_key, gcol, ntile, blocks, scale_div):
        for bi, (c0, n) in blocks:
            b = bank()
            for kt in range(ntile):
                si = kt % 2
                act(sq[si][:, 0:n], src(kt, c0, n), AF.Square, [src_key(kt, bi)], [("sq", si)])
                mm(ps[:, b, 0:n], onesb[:], sq[si][:, 0:n], kt == 0, kt == ntile - 1, [("sq", si), "onesb"], [("ps", b)])
            ts(rstd[:, 0:n], ps[:, b, 0:n], 1.0 / scale_div, EPS, ALU.mult, ALU.add, [("ps", b)], ["rstd"])
            act(rstd[:, 0:n], rstd[:, 0:n], AF.Ln, ["rstd"], ["rstd"])
            act(rstd[:, 0:n], rstd[:, 0:n], AF.Exp, ["rstd"], ["rstd"], scale=-0.5)
            for kt in range(ntile):
                stt(dstf(kt, c0, n), src(kt, c0, n), pvec[:, gcol + kt:gcol + kt + 1], rstd[:, 0:n], ALU.mult, ALU.mult,
                    [src_key(kt, bi), "rstd", "pvec"], [dst_key(kt, bi)])

    EB = list(enumerate(BLKS))

    def norm_h(gcol):
        rmsnorm_cols(lambda kt, c0, n: hT[:, kt, c0:c0 + n], hkeys, lambda kt, c0, n: xn[:, kt, c0:c0 + n],
                     lambda kt, bi: ("xn", kt, bi), gcol, KT, EB, float(D))

    def load_w(pieces, nk):
        si = state["slot"]; state["slot"] = (si + 1) % NSLOT
        tot = sum(p.shape[1] for p in pieces)
        assert nk * tot <= SLOTW, (nk, tot)
        view = wsl[si][:, 0:nk * tot].rearrange("p (k n) -> p k n", k=nk)
        off = 0
        offs = []
        for p in pieces:
            n = p.shape[1]
            dma("pool", view[:, :, off:off + n], p.rearrange("(k p) n -> p k n", p=128), [], [("w", si)])
            offs.append(off); off += n
        return si, view, offs

    def proj_fm(view, si, col0, nk, rhs, rhs_keys, blocks, evac, M=128):
        for bi, (c0, n) in blocks:
            b = bank()
            for k in range(nk):
                mm(ps[0:M, b, 0:n], view[:, k, col0:col0 + M], rhs(k, c0, n), k == 0, k == nk - 1,
                   [("w", si)] + list(rhs_keys(k, bi)), [("ps", b)])
            evac(bi, c0, n, ps[0:M, b, 0:n], ("ps", b))

    def xn_rhs(k, c0, n):
        return xn[:, k, c0:c0 + n]

    def xn_keys(k, bi):
        return [("xn", k, bi)]

    def bB_rhs(k, c0, n):
        return bB[:, k, c0:c0 + n]

    def bB_keys(k, bi):
        return [("bB", k, bi)]

    def add_to_h(nt):
        def ev(bi, c0, n, p, pk):
            tt(hT[:, nt, c0:c0 + n], hT[:, nt, c0:c0 + n], p, ALU.add, [hkeys(nt, bi), pk], [hkeys(nt, bi)])
        return ev

    def out_proj(wfn, rhs, rhs_keys, nk, k0=0):
        ncc = SLOTW // nk // 128
        ncc = min(ncc, 4)
        for c in range(0, 16, ncc):
            si, view, offs = load_w([wfn(c * 128, ncc * 128)], nk)
            for t in range(ncc):
                proj_fm(view, si, t * 128, nk, rhs, rhs_keys, EB, add_to_h(c + t))

    def all_h_keys(ct):
        return [hkeys(ct, 0), hkeys(ct, 1), hkeys(ct, 2)]

    if KDBG >= 2:
        in_transpose(IN("xs"), NS, D, lambda ct: hT[:, ct, S0:S0 + NS], all_h_keys)
    if KDBG >= 3:
        in_transpose(IN("xh"), NH, D, lambda ct: hT[:, ct, H0:H0 + NH], all_h_keys)
    for i in range(8 if KDBG >= 4 else 0):
        in_transpose(IN("xp")[i * 128:(i + 1) * 128, :], 128, D, (lambda ct, i=i: hT[:, ct, P0 + i * 128:P0 + (i + 1) * 128]), all_h_keys)

    def mixer_sc():
        norm_h(PV_NMIX + 0)
        hist = arf(PX, 512).rearrange("p (k r) -> p k r", k=KT)
        uo_p = arf(PX + 512, 32).rearrange("p (k r) -> p k r", k=KT)
        uo_s = arf(PX + 544, 512).rearrange("p (k r) -> p k r", k=KT)
        u = arf(PX + 1056, NCOL); bg = arb(PX + 2104, NCOL // 2); acc = arf(PX + 2628, NCOL)
        assert 2628 + NCOL <= PXN
        in_transpose(IN("csc"), 2 * NS, D, lambda ct: hist[:, ct, :], lambda ct: ["hist"])
        for j in range(KT):
            for which in range(3):
                si, view, offs = load_w([IN("sc_w_in")[:, which * D + j * 128: which * D + (j + 1) * 128]], KT)
                if which == 0:
                    proj_fm(view, si, 0, KT, xn_rhs, xn_keys, EB, lambda bi, c0, n, p, pk: cp(bg[:, c0:c0 + n], p, [pk], ["bg"], eng="act"))
                elif which == 1:
                    proj_fm(view, si, 0, KT, xn_rhs, xn_keys, EB, lambda bi, c0, n, p, pk: cp(u[:, c0:c0 + n], p, [pk], ["u"], eng="act"))
                else:
                    proj_fm(view, si, 0, KT, xn_rhs, xn_keys, EB, lambda bi, c0, n, p, pk: tt(u[:, c0:c0 + n], u[:, c0:c0 + n], p, ALU.mult, ["u", pk], ["u"]))
            w0 = pvec[:, PV_SCW + j:PV_SCW + j + 1]; w1 = pvec[:, PV_SCW + 16 + j:PV_SCW + 17 + j]; w2 = pvec[:, PV_SCW + 32 + j:PV_SCW + 33 + j]
            L = NCOL - 18
            ts(acc[:, 18:NCOL], u[:, 16:16 + L], w0, None, ALU.mult, None, ["u", "pvec"], ["acc"])
            stt(acc[:, 18:NCOL], u[:, 17:17 + L], w1, acc[:, 18:NCOL], ALU.mult, ALU.add, ["u", "acc"], ["acc"])
            stt(acc[:, 18:NCOL], u[:, 18:18 + L], w2, acc[:, 18:NCOL], ALU.mult, ALU.add, ["u", "acc"], ["acc"])
            hv = hist[:, j, :].rearrange("p (b r) -> p b r", r=2)
            ts(acc[:, 0:NS], hv[:, :, 0], w0, None, ALU.mult, None, ["hist", "pvec"], ["acc"])
            stt(acc[:, 0:NS], hv[:, :, 1], w1, acc[:, 0:NS], ALU.mult, ALU.add, ["hist", "acc"], ["acc"])
            stt(acc[:, 0:NS], u[:, 0:NS], w2, acc[:, 0:NS], ALU.mult, ALU.add, ["u", "acc"], ["acc"])
            vop(lambda e, j=j: e.memset(bB[:, j, 16:18], 0.0), [], [("bB", j, 0)])
            tt(bB[:, j, 18:NCOL], acc[:, 18:NCOL], bg[:, 18:NCOL], ALU.mult, ["acc", "bg"], [("bB", j, 0), ("bB", j, 1), ("bB", j, 2)])
            tt(bB[:, j, 0:NS], acc[:, 0:NS], bg[:, 0:NS], ALU.mult, ["acc", "bg"], [("bB", j, 0)])
            cp(uo_p[:, j, :], u[:, NCOL - 2:NCOL], ["u"], ["uo_p"], eng="act")
            uov = uo_s[:, j, :].rearrange("p (b r) -> p b r", r=2)
            cp(uov[:, :, 0], hv[:, :, 1], ["hist"], ["uo_s"], eng="act")
            cp(uov[:, :, 1], u[:, 0:NS], ["u"], ["uo_s"], eng="act")
        out_transpose(lambda ct: uo_p[:, ct, :], lambda ct: ["uo_p"], 2, D, scp_o, "o_scp")
        out_transpose(lambda ct: uo_s[:, ct, :], lambda ct: ["uo_s"], 2 * NS, D, scs_o, "o_scs")
        out_proj(lambda c0, n: IN("sc_w_out")[:, c0:c0 + n], bB_rhs, bB_keys, KT)

    def xattn(layer, with_halo):
        barrier()
        memT = bBf[:, 0:8192].bitcast(F32).rearrange("p (k m) -> p k m", k=KT)
        mn = xnf[:, 0:4096].rearrange("p (k m) -> p k m", k=KT)
        KTb = arb(PX, 2048).rearrange("p (k m) -> p k m", k=KT)
        Vt = arb(PX + 2048, 2048).rearrange("p (m n) -> p m n", m=2)
        for mt in range(2):
            in_transpose(IN("mem")[mt * 128:(mt + 1) * 128, :], 128, D, (lambda ct, mt=mt: memT[:, ct, mt * 128:(mt + 1) * 128]),
                         lambda ct: [("memT", ct)] + BB_ALL[0:1])
        rmsnorm_cols(lambda kt, c0, n: memT[:, kt, c0:c0 + n], lambda kt, bi: ("memT", kt),
                     lambda kt, c0, n: mn[:, kt, c0:c0 + n], lambda kt, bi: ("mn", kt),
                     PV_NKV + 16 * layer, KT, [(0, (0, 256))], float(D))
        for which, wmat, oap in (("k", IN("xa_w_k"), mk_o), ("v", IN("xa_w_v"), mv_o)):
            for c in range(8):
                si, view, offs = load_w([wmat[layer, :, c * 256:(c + 1) * 256]], KT)
                for mt in range(2):
                    b = bank()
                    for k in range(KT):
                        mm(ps[:, b, 0:256], mn[:, k, mt * 128:(mt + 1) * 128], view[:, k, :], k == 0, k == KT - 1,
                           [("w", si), ("mn", k)], [("ps", b)])
                    sti = state["stg"]; state["stg"] ^= 1
                    cp(stg[sti][:, 0:256], ps[:, b, 0:256], [("ps", b)], [("stg", sti)], eng="act")
                    okey = ("o_m" + which, layer, c, mt)
                    dma("sp", oap[layer, mt * 128:(mt + 1) * 128, c * 256:(c + 1) * 256], stg[sti][:, 0:256], [("stg", sti)], [okey])
                    outkeys.append(okey)
                    if which == "v":
                        cp(Vt[:, mt, c * 256:(c + 1) * 256], stg[sti][:, 0:256], [("stg", sti)], ["Vt"])
                if which == "k":
                    for t in range(2):
                        b = bank()
                        for k in range(KT):
                            mm(ps[:, b, 0:256], view[:, k, t * 128:(t + 1) * 128], mn[:, k, :], k == 0, k == KT - 1,
                               [("w", si), ("mn", k)], [("ps", b)])
                        cp(KTb[:, c * 2 + t, :], ps[:, b, 0:256], [("ps", b)], [("KTb", c * 2 + t)])
        barrier()
        norm_h(PV_NQ + 16 * layer)
        qs = float(512 ** -0.5)
        for c in range(8):
            si, view, offs = load_w([IN("xa_w_q")[layer, :, c * 256:(c + 1) * 256]], KT)
            for t in range(2):
                nt = c * 2 + t
                proj_fm(view, si, t * 128, KT, xn_rhs, xn_keys, EB,
                        lambda bi, c0, n, p, pk, nt=nt: S.op("act", lambda e: e.mul(bB[:, nt, c0:c0 + n], p, qs),
                                                             [pk], [("bB", nt, bi)]))
        barrier()
        tiles = [(P0 + 128 * i, 128, 1 + (i // 4)) for i in range(8)]
        if with_halo:
            tiles.append((H0, NH, 0))
        Pf = sm[:, 0:256]; Pn = sm[:, 256:384].bitcast(BF16)
        PT = sm[:, 384:500].bitcast(BF16)[:, 0:256].rearrange("p (m t) -> p m t", m=2) if False else xnf[:, 0:256].rearrange("p (m t) -> p m t", m=2)
        st1 = sm[:, 500:504]
        for h in range(4):
            for (c0, nt_, bi) in tiles:
                b = bank()
                for j in range(4):
                    mm(ps[0:nt_, b, 0:256], bB[:, 4 * h + j, c0:c0 + nt_], KTb[:, 4 * h + j, :], j == 0, j == 3,
                       [("bB", 4 * h + j, bi), ("KTb", 4 * h + j)], [("ps", b)])
                vop(lambda e, b=b, nt_=nt_: e.tensor_reduce(out=st1[0:nt_, 0:1], in_=ps[0:nt_, b, 0:256], axis=AX.X, op=ALU.max), [("ps", b)], ["st1a"])
                ts(st1[0:nt_, 1:2], st1[0:nt_, 0:1], -1.0, None, ALU.mult, None, ["st1a"], ["st1b"])
                act(Pf[0:nt_, :], ps[0:nt_, b, 0:256], AF.Exp, [("ps", b), "st1b"], ["Pf", "st1c"], bias=st1[0:nt_, 1:2], accum=st1[0:nt_, 2:3])
                vop(lambda e, nt_=nt_: e.reciprocal(out=st1[0:nt_, 3:4], in_=st1[0:nt_, 2:3]), ["st1c"], ["st1d"])
                ts(Pn[0:nt_, :], Pf[0:nt_, :], st1[0:nt_, 3:4], None, ALU.mult, None, ["Pf", "st1d"], ["Pn"])
                pb = pbank()
                for mt in range(2):
                    tr(psb[:, pb, mt * 128:mt * 128 + nt_], Pn[0:nt_, mt * 128:(mt + 1) * 128], identb[0:nt_, 0:nt_], ["Pn", "identb"], [("psb", pb)])
                for mt in range(2):
                    cp(PT[:, mt, 0:nt_], psb[:, pb, mt * 128:mt * 128 + nt_], [("psb", pb)], ["PT"], eng="act")
                for j in range(4):
                    b2 = bank()
                    for mt in range(2):
                        mm(ps[:, b2, 0:nt_], Vt[:, mt, (4 * h + j) * 128:(4 * h + j + 1) * 128], PT[:, mt, 0:nt_], mt == 0, mt == 1,
                           ["Vt", "PT"], [("ps", b2)])
                    cp_alt(bB[:, 4 * h + j, c0:c0 + nt_], ps[:, b2, 0:nt_], [("ps", b2)], [("bB", 4 * h + j, bi)])
        qtok = xnf[0:16, 512:512 + 2048]
        kvst = [xnf[:, 4096 + i * 4096:4096 + (i + 1) * 4096].bitcast(F32) for i in range(2)]
        Pz = xnf[:, 12288:12288 + 4096].bitcast(F32).rearrange("p (f h b c) -> p f h b c", f=2, h=4, b=NS)
        scs = arf(PX, 128).rearrange("p (f b h) -> p f b h", f=2, b=NS)
        junk = arf(PX + 128, 512)
        sp_ = ar[0:64, PX + 640:PX + 896]
        Pm = arf(PX + 896, 128).rearrange("p (f b h) -> p f b h", f=2, b=NS)
        otok = ar[0:16, PX + 1024:PX + 3072]
        st2 = sm[0:64, 504:508]
        barrier()
        for c4 in range(4):
            pb = pbank()
            for j in range(4):
                tr(psb[0:16, pb, j * 128:(j + 1) * 128], bB[:, c4 * 4 + j, 0:NS], identb[:], [("bB", c4 * 4 + j, 0), "identb"], [("psb", pb)])
            cp(qtok[:, c4 * 512:(c4 + 1) * 512], psb[0:16, pb, 0:512], [("psb", pb)], ["qtok"])
        kv = [stg[0], stg[1], kvst[0], kvst[1]]
        kvk = [("stg", 0), ("stg", 1), "kvst0", "kvst1"]
        for bsmp in range(NS):
            o2 = (bsmp % 2) * 2
            for f in range(2):
                dma("sp", kv[o2 + f], IN("ck")[layer, bsmp, f * 128:(f + 1) * 128, :], [], [kvk[o2 + f]] + (XN_ALL if (bsmp < 2 and o2 == 2) else []))
            for h in range(4):
                b = bank()
                mm(ps[:, b, :], selb[:, bsmp * 128:(bsmp + 1) * 128], qtok[:, h * 512:(h + 1) * 512], True, True, ["selb", "qtok"], [("ps", b)])
                for f in range(2):
                    vop(lambda e, b=b, f=f, h=h, o2=o2, bsmp=bsmp: e.scalar_tensor_tensor(
                        out=junk, in0=kv[o2 + f][:, h * 512:(h + 1) * 512], scalar=1.0, in1=ps[:, b, :],
                        op0=ALU.mult, op1=ALU.mult, accum_out=scs[:, f, bsmp, h:h + 1]),
                        [kvk[o2 + f], ("ps", b)], ["junk", "scs"])
        b = bank()
        for f in range(2):
            tr(ps[0:64, b, f * 128:(f + 1) * 128], arf(PX + 64 * f, 64), ident, ["scs", "cst"], [("ps", b)])
        vop(lambda e, b=b: e.tensor_reduce(out=st2[:, 0:1], in_=ps[0:64, b, 0:256], axis=AX.X, op=ALU.max), [("ps", b)], ["st2a"])
        ts(st2[:, 1:2], st2[:, 0:1], -1.0, None, ALU.mult, None, ["st2a"], ["st2b"])
        act(sp_, ps[0:64, b, 0:256], AF.Exp, [("ps", b), "st2b"], ["sp_", "st2c"], bias=st2[:, 1:2], accum=st2[:, 2:3])
        vop(lambda e: e.reciprocal(out=st2[:, 3:4], in_=st2[:, 2:3]), ["st2c"], ["st2d"])
        ts(sp_, sp_, st2[:, 3:4], None, ALU.mult, None, ["sp_", "st2d"], ["sp_"])
        b = bank()
        for f in range(2):
            tr(ps[:, b, f * 64:(f + 1) * 64], sp_[:, f * 128:(f + 1) * 128], ident[0:64, 0:64], ["sp_", "cst"], [("ps", b)])
        cp(arf(PX + 896, 128), ps[:, b, 0:128], [("ps", b)], ["Pm"])
        for f in range(2):
            for h in range(4):
                tt(Pz[:, f, h, :, :], Pm[:, f, :, h].unsqueeze(2).to_broadcast([128, NS, NS]), d16, ALU.mult, ["Pm", "cst"], ["Pz"])
        ob = [bank() for _ in range(4)]
        for bsmp in range(NS):
            o2 = (bsmp % 2) * 2
            for f in range(2):
                dma("sp", kv[o2 + f], IN("cv")[layer, bsmp, f * 128:(f + 1) * 128, :], [], [kvk[o2 + f]])
            for h in range(4):
                for f in range(2):
                    mm(ps[0:16, ob[h], :], Pz[:, f, h, bsmp, :], kv[o2 + f][:, h * 512:(h + 1) * 512], bsmp == 0 and f == 0, bsmp == NS - 1 and f == 1,
                       ["Pz", kvk[o2 + f]], [("ps", ob[h])])
        for h in range(4):
            cp_alt(otok[:, h * 512:(h + 1) * 512], ps[0:16, ob[h], :], [("ps", ob[h])], ["otok"])
        for c4 in range(4):
            b = bank()
            for j in range(4):
                tr(ps[:, b, j * 16:(j + 1) * 16], otok[:, (c4 * 4 + j) * 128:(c4 * 4 + j + 1) * 128], ident[0:16, 0:16], ["otok", "cst"], [("ps", b)])
            for j in range(4):
                cp(bB[:, c4 * 4 + j, 0:NS], ps[:, b, j * 16:(j + 1) * 16], [("ps", b)], [("bB", c4 * 4 + j, 0)])
        out_proj(lambda c0, n: IN("xa_w_o")[layer, :, c0:c0 + n], bB_rhs, bB_keys, KT)
        barrier()

    def ffn(layer):
        norm_h(PV_NFFN + 16 * layer)
        gs = [arb(PX, NCOL // 2), arb(PX + 600, NCOL // 2)]
        f0 = 0
        while f0 < 44:
            nf = min(8, 44 - f0)
            for j0 in range(0, nf, 2):
                cg = (f0 + j0) * 128
                sg, vg, _ = load_w([IN("ffn_w_gate")[layer, :, cg:cg + 256]], KT)
                su, vu, _ = load_w([IN("ffn_w_up")[layer, :, cg:cg + 256]], KT)
                for t in range(2):
                    jj = j0 + t
                    gbuf = gs[jj % 2]; gk = ("gs", jj % 2)
                    proj_fm(vg, sg, t * 128, KT, xn_rhs, xn_keys, EB,
                            lambda bi, c0, n, p, pk, gbuf=gbuf, gk=gk: act(gbuf[:, c0:c0 + n], p, AF.Silu, [pk], [gk]))
                    proj_fm(vu, su, t * 128, KT, xn_rhs, xn_keys, EB,
                            lambda bi, c0, n, p, pk, gbuf=gbuf, gk=gk, jj=jj: tt(bB[:, jj, c0:c0 + n], gbuf[:, c0:c0 + n], p, ALU.mult, [gk, pk], [("bB", jj, bi)]))
            out_proj(lambda c0, n, f0=f0, nf=nf: IN("ffn_w_down")[layer, f0 * 128:(f0 + nf) * 128, c0:c0 + n], bB_rhs, bB_keys, nf)
            f0 += nf

    def mixer_ssd():
        barrier()
        norm_h(PV_NMIX + 16)
        for kt in range(KT):
            ts(xn[:, kt, H0:H0 + NH], xn[:, kt, H0:H0 + NH], flag, None, ALU.mult, None, [("xn", kt, 0), "cst"], [("xn", kt, 0)])
        xT = bB[:, 0:4, :]; BT = bB[:, 4, :]; CT = bB[:, 5, :]
        xtok = bBf[:, 6 * NCOL:6 * NCOL + 4096].rearrange("p (c n) -> p c n", c=8)
        Btok = bB[:, 10, 0:1024].rearrange("p (c n) -> p c n", c=8)
        Eb = [bB[:, 11 + i, 0:1024].rearrange("p (h t) -> p h t", h=8) for i in range(2)]
        Wb = [bB[:, 13 + i, 0:1024].rearrange("p (h t) -> p h t", h=8) for i in range(2)]
        xdtb = [bB[:, 15, i * 512:(i + 1) * 512].rearrange("p (h d) -> p h d", h=8) for i in range(2)]
        o = PX
        raw = arf(o, NCOL); o += NCOL
        R = arf(o, 1024).rearrange("p (h t) -> p h t", h=8); o += 1024
        Sst = arf(o, 512); o += 512
        Sbf = arb(o, 256); o += 256
        tmpA = arf(o, 512); o += 512
        xdte = arb(o, 256).rearrange("p (h d) -> p h d", h=8); o += 256
        CBTb = [arb(o, 64), arb(o + 64, 64)]; o += 128
        xs_conv = arb(o, 256).rearrange("p (j b) -> p j b", j=32); o += 256
        y_s = arf(o, 512).rearrange("p (j b) -> p j b", j=32); o += 512
        assert o <= ARW, o
        o = 1024
        tstat = arf(o, 16); o += 16
        cs_sb = [arf(o, 32), arf(o + 32, 32)]; o += 64
        ncs = [arf(o, 8), arf(o + 8, 8)]; o += 16
        dq = arf(o, 132); o += 132
        xdts = arf(o, 64).rearrange("p (b j) -> p b j", b=NS); o += 64
        dAs = arf(o, 16); o += 16
        bcs = arb(o, 128); o += 128
        assert o <= 1536, o
        o = 1536
        dt_tok = arf(o, 512).rearrange("p (c h) -> p c h", c=8); o += 512
        dta_tok = arf(o, 512).rearrange("p (c h) -> p c h", c=8); o += 512
        gg = arf(o, 512); o += 512
        gn = arb(o, 256); o += 256
        histg = arf(o, 288).rearrange("p (t r) -> p t r", t=6); o += 288
        cvo_s = arf(o, 288).rearrange("p (t r) -> p t r", t=6); o += 288
        cvo_p = arf(o, 18).rearrange("p (t r) -> p t r", t=6); o += 18
        BsT = arb(o, 64).rearrange("p (g b) -> p g b", g=8); o += 64
        CsT = arb(o, 64).rearrange("p (g b) -> p g b", g=8); o += 64
        rallb = arf(o, 132); o += 132
        rg = arf(o, 132); o += 132
        stS = [arf(o, 512).rearrange("p (j n) -> p j n", j=4), arf(o + 512, 512).rearrange("p (j n) -> p j n", j=4)]
        cstg = arf(o, 512)
        o += 1024
        assert o <= 5632, o
        smid = arf(PX + NCOL, 512)
        t1 = tmpA.rearrange("p (j n) -> p j n", j=4)
        def fence():
            vop(lambda e: e.memset(tstat[:, 0:1], 0.0), [], [("stg", 0), ("stg", 1), "kvst0", "kvst1", "fence"])
            barrier()

        fence()

        dtT = ar[0:64, 3584 + 200:3584 + 200 + NCOL] if False else raw[0:64, :]
        sdt, vdt, _ = load_w([IN("ssd_w_in")[:, 10240:10304]], KT)
        for bi, (c0, n) in EB:
            b = bank()
            for k in range(KT):
                mm(ps[0:64, b, 0:n], vdt[:, k, 0:64], xn[:, k, c0:c0 + n], k == 0, k == KT - 1, [("w", sdt), ("xn", k, bi)], [("ps", b)])
            act(dtT[:, c0:c0 + n], ps[0:64, b, 0:n], AF.Exp, [("ps", b), "hvec"], ["dtT"], bias=hvec[:, 0:1])
        ts(dtT, dtT, 1.0, None, ALU.add, None, ["dtT"], ["dtT"])
        act(dtT, dtT, AF.Ln, ["dtT"], ["dtT"])
        for c4 in range(2):
            b = bank()
            for j in range(4):
                c = c4 * 4 + j
                tr(ps[:, b, j * 64:(j + 1) * 64], dtT[:, P0 + c * 128:P0 + (c + 1) * 128], ident[0:64, 0:64], ["dtT", "cst"], [("ps", b)])
            cp(ar[:, 1536 + c4 * 256:1536 + (c4 + 1) * 256], ps[:, b, 0:256], [("ps", b)], ["dt_tok"])
        tt(dta_tok, dt_tok, a_bc.unsqueeze(1).to_broadcast([128, 8, 64]), ALU.mult, ["dt_tok", "tokbc"], ["dta_tok"])
        M2 = lg[:, 128:132]
        dts = dtT[:, 0:NS]
        act(dAs[0:64, :], dts, AF.Exp, ["dtT", "hvec_a"], ["dAs"], scale=hvec[:, 3:4])
        tt(rallb[0:64, 0:64].rearrange("p (b j) -> p b j", b=NS), dts.unsqueeze(2).to_broadcast([64, NS, 4]), M2.unsqueeze(1).to_broadcast([64, NS, 4]), ALU.mult, ["dtT", "lg"], ["rall"])
        tt(rallb[0:64, 64:128].rearrange("p (b j) -> p b j", b=NS), dAs[0:64, :].unsqueeze(2).to_broadcast([64, NS, 4]), M2.unsqueeze(1).to_broadcast([64, NS, 4]), ALU.mult, ["dAs", "lg"], ["rall"])
        ts(rallb[0:64, 128:132], M2, hvec[:, 2:3], None, ALU.mult, None, ["lg", "hvec"], ["rall"])
        barrier()

        XCH0 = 0; BCH0 = 4096; CCH0 = 5120

        def do_conv(dst_tile, ct, hist_t, full, ck_):
            w = [pvec[:, PV_CW + 48 * r + ct:PV_CW + 48 * r + ct + 1] for r in range(4)]
            bia = pvec[:, PV_CB + ct:PV_CB + ct + 1]
            Rf = arf(PX + NCOL, 1024)
            for (c0, n) in ((19, 517), (536, 512)):
                a = Rf[:, 0:n]
                ts(a, raw[:, c0 - 3:c0 - 3 + n], w[0], None, ALU.mult, None, ["raw", "pvec"], ["R"])
                stt(a, raw[:, c0 - 2:c0 - 2 + n], w[1], a, ALU.mult, ALU.add, ["raw", "R"], ["R"])
                stt(a, raw[:, c0 - 1:c0 - 1 + n], w[2], a, ALU.mult, ALU.add, ["raw", "R"], ["R"])
                stt(a, raw[:, c0:c0 + n], w[3], a, ALU.mult, ALU.add, ["raw", "R"], ["R"])
                act(dst_tile[:, c0:c0 + n], a, AF.Silu, ["R", "pvec"], [ck_], bias=bia)
            hv = histg[:, hist_t, :].rearrange("p (b r) -> p b r", r=3)
            a = Rf[:, 0:NS]
            ts(a, hv[:, :, 0], w[0], None, ALU.mult, None, ["histg", "pvec"], ["R"])
            stt(a, hv[:, :, 1], w[1], a, ALU.mult, ALU.add, ["histg", "R"], ["R"])
            stt(a, hv[:, :, 2], w[2], a, ALU.mult, ALU.add, ["histg", "R"], ["R"])
            stt(a, raw[:, 0:NS], w[3], a, ALU.mult, ALU.add, ["raw", "R"], ["R"])
            act(dst_tile[:, 0:NS], a, AF.Silu, ["R", "pvec"], [ck_], bias=bia)
            if full:
                cp(cvo_p[:, hist_t, :], raw[:, NCOL - 3:NCOL], ["raw"], ["cvo_p"])
                ov = cvo_s[:, hist_t, :].rearrange("p (b r) -> p b r", r=3)
                cp(ov[:, :, 0:2], hv[:, :, 1:3], ["histg"], ["cvo_s"])
                cp(ov[:, :, 2], raw[:, 0:NS], ["raw"], ["cvo_s"])

        def load_hist(g):
            for (c0, ncol, t0) in ((XCH0 + 512 * g, 512, 0), (BCH0 + 128 * g, 128, 4), (CCH0 + 128 * g, 128, 5)):
                dma("sp", cstg[0:48, 0:ncol], IN("ccv")[:, c0:c0 + ncol], [], ["cstg"])
                b = bank()
                nct = ncol // 128
                for j in range(nct):
                    tr(ps[:, b, j * 48:(j + 1) * 48], cstg[0:48, j * 128:(j + 1) * 128], ident[0:48, 0:48], ["cstg", "cst"], [("ps", b)])
                cp(histg[:, t0:t0 + nct, :], ps[:, b, 0:nct * 48].rearrange("p (t r) -> p t r", t=nct), [("ps", b)], ["histg"])

        def store_conv(g):
            for (c0, ncol, t0) in ((XCH0 + 512 * g, 512, 0), (BCH0 + 128 * g, 128, 4), (CCH0 + 128 * g, 128, 5)):
                nct = ncol // 128
                b = bank()
                for j in range(nct):
                    tr(ps[0:48, b, j * 128:(j + 1) * 128], cvo_s[:, t0 + j, :], ident, ["cvo_s", "cst"], [("ps", b)])
                cp(cstg[0:48, 0:ncol], ps[0:48, b, 0:ncol], [("ps", b)], ["cstg"])
                ok = ("o_cvs", g, t0); outkeys.append(ok)
                dma("sp", cvs_o[:, c0:c0 + ncol], cstg[0:48, 0:ncol], ["cstg"], [ok])
                b = bank()
                for j in range(nct):
                    tr(ps[0:3, b, j * 128:(j + 1) * 128], cvo_p[:, t0 + j, :], ident, ["cvo_p", "cst"], [("ps", b)])
                cp(cstg[0:3, 0:ncol], ps[0:3, b, 0:ncol], [("ps", b)], ["cstg"])
                ok = ("o_cvp", g, t0); outkeys.append(ok)
                dma("sp", cvp_o[:, c0:c0 + ncol], cstg[0:3, 0:ncol], ["cstg"], [ok])

        def inproj_conv(g, full):
            load_hist(g)
            specs = [(4096 + 512 * g + 0, 2), (4096 + 512 * g + 256, 2), (8192 + 128 * g, 1)]
            specs.append((9216 + 128 * g, 1))
            tix = 0
            for (wc0, ntl) in specs:
                si, view, _ = load_w([IN("ssd_w_in")[:, wc0:wc0 + ntl * 128]], KT)
                for t in range(ntl):
                    proj_fm(view, si, t * 128, KT, xn_rhs, xn_keys, EB,
                            lambda bi, c0, n, p, pk: cp_alt(raw[:, c0:c0 + n], p, [pk], ["raw"]))
                    ch = wc0 - 4096 + t * 128
                    if tix < 4:
                        dst = xT[:, tix, :]; ck_ = "xTk"
                    elif tix == 4:
                        dst = BT; ck_ = "BTk"
                    else:
                        dst = CT; ck_ = "CTk"
                    do_conv(dst, ch // 128, tix, full, ck_)
                    tix += 1
            for c in range(8):
                pb = pbank()
                for j in range(4):
                    tr(psb[:, pb, j * 128:(j + 1) * 128], xT[:, j, P0 + c * 128:P0 + (c + 1) * 128], identb[:], ["xTk", "identb"], [("psb", pb)])
                cp_alt(xtok[:, c, :], psb[:, pb, 0:512], [("psb", pb)], ["xtok"])
            for c4 in range(2):
                pb = pbank()
                for j in range(4):
                    c = c4 * 4 + j
                    tr(psb[:, pb, j * 128:(j + 1) * 128], BT[:, P0 + c * 128:P0 + (c + 1) * 128], identb[:], ["BTk", "identb"], [("psb", pb)])
                cp_alt(Btok[:, c4 * 4:(c4 + 1) * 4, :], psb[:, pb, 0:512].rearrange("p (c n) -> p c n", c=4), [("psb", pb)], ["Btok"])
            cp(xs_conv[:, 4 * g:4 * g + 4, :], xT[:, :, 0:NS], ["xTk"], ["xs_conv"])
            cp(BsT[:, g, :], BT[:, 0:NS], ["BTk"], ["BsT"])
            cp(CsT[:, g, :], CT[:, 0:NS], ["CTk"], ["CsT"])

        def stageA(g, c, full):
            i = c % 2
            hs = slice(8 * g, 8 * g + 8)
            b = bank()
            mm(ps[:, b, 0:8], tri, dta_tok[:, c, hs], True, True, ["cst", "dta_tok"], [("ps", b)])
            mm(ps[:, b, 8:16], ones, dta_tok[:, c, hs], True, True, ["cst", "dta_tok"], [("ps", b)])
            cp(cs_sb[i][:, 0:16], ps[:, b, 0:16], [("ps", b)], [("cs", i)])
            tt(cs_sb[i][:, 16:24], cs_sb[i][:, 8:16], cs_sb[i][:, 0:8], ALU.subtract, [("cs", i)], [("cs", i)])
            act(cs_sb[i][:, 16:24], cs_sb[i][:, 16:24], AF.Exp, [("cs", i)], [("cs", i)])
            act(cs_sb[i][:, 24:32], cs_sb[i][:, 8:16], AF.Exp, [("cs", i)], [("cs", i)])
            tt(xdtb[i], xtok[:, c, :].rearrange("p (h d) -> p h d", h=8), dt_tok[:, c, hs].unsqueeze(2).to_broadcast([128, 8, 64]), ALU.mult,
               ["xtok", "dt_tok"], [("xdt", i)])
            if full:
                ts(ncs[i], cs_sb[i][:, 0:8], -1.0, None, ALU.mult, None, [("cs", i)], [("ncs", i)])
                tt(R, tri.unsqueeze(1).to_broadcast([128, 8, 128]), dta_tok[:, c, hs].unsqueeze(2).to_broadcast([128, 8, 128]), ALU.mult,
                   ["cst", "dta_tok"], ["R"])
                pbk = [bank(), bank()]
                Rf = arf(PX + NCOL, 1024)
                for hh in range(2):
                    mm(ps[:, pbk[hh], :], ones, Rf[:, hh * 512:(hh + 1) * 512], True, False, ["cst", "R"], [("ps", pbk[hh])])
                    mm(ps[:, pbk[hh], :], ident, mneg4, False, True, ["cst"], [("ps", pbk[hh])])
                for h in range(8):
                    act(Eb[i][:, h, :], ps[:, pbk[h // 4], (h % 4) * 128:(h % 4 + 1) * 128], AF.Exp, [("ps", pbk[h // 4]), ("ncs", i)], [("E", i)], bias=ncs[i][:, h:h + 1])
                b2 = bank()
                mm(ps[:, b2, 0:128], BT[:, P0 + c * 128:P0 + (c + 1) * 128], CT[:, P0 + c * 128:P0 + (c + 1) * 128], True, True, ["BTk", "CTk"], [("ps", b2)])
                cp(CBTb[i], ps[:, b2, 0:128], [("ps", b2)], [("CBT", i)], eng="act")
                tt(Wb[i], Eb[i], CBTb[i].unsqueeze(1).to_broadcast([128, 8, 128]), ALU.mult, [("E", i), ("CBT", i)], [("W", i)])

        def stageB(g, c, full):
            i = c % 2
            if full:
                by = bank()
                for h in range(8):
                    mm(ps[:, by, h * 64:(h + 1) * 64], Wb[i][:, h, :], xdtb[i][:, h, :], True, True, [("W", i), ("xdt", i)], [("ps", by)])
                bo = bank()
                mm(ps[:, bo, :], CT[:, P0 + c * 128:P0 + (c + 1) * 128], Sbf, True, True, ["CTk", "Sbf"], [("ps", bo)])
                act(tstat[:, 0:8], cs_sb[i][:, 0:8], AF.Exp, [("cs", i)], ["expcs"])
                tt(gg.rearrange("p (h d) -> p h d", h=8), ps[:, bo, :].rearrange("p (h d) -> p h d", h=8),
                   tstat[:, 0:8].unsqueeze(2).to_broadcast([128, 8, 64]), ALU.mult, [("ps", bo), "expcs"], ["gg"])
                tt(gg, gg, ps[:, by, :], ALU.add, ["gg", ("ps", by)], ["gg"])
                tt(tmpA.rearrange("p (h d) -> p h d", h=8), xtok[:, c, :].rearrange("p (h d) -> p h d", h=8),
                   d_bc[:, 8 * g:8 * g + 8].unsqueeze(2).to_broadcast([128, 8, 64]), ALU.mult, ["xtok", "tokbc"], ["tmpA"])
                tt(gg, gg, tmpA, ALU.add, ["gg", "tmpA"], ["gg"])
            tt(xdte, xdtb[i], cs_sb[i][:, 16:24].unsqueeze(2).to_broadcast([128, 8, 64]), ALU.mult, [("xdt", i), ("cs", i)], ["xdte"])
            bs_ = bank()
            mm(ps[:, bs_, :], Btok[:, c, :], xdte.rearrange("p h d -> p (h d)"), True, True, ["Btok", "xdte"], [("ps", bs_)])
            tt(Sst.rearrange("p (h d) -> p h d", h=8), Sst.rearrange("p (h d) -> p h d", h=8),
               cs_sb[i][:, 24:32].unsqueeze(2).to_broadcast([128, 8, 64]), ALU.mult, ["Sst", ("cs", i)], ["Sst"])
            tt(Sst, Sst, ps[:, bs_, :], ALU.add, ["Sst", ("ps", bs_)], ["Sst"])
            cp(Sbf, Sst, ["Sst"], ["Sbf"], eng="act")
            if full:
                cp(gn, gg, ["gg"], ["gn"], eng="act")
                pb = pbank()
                for j in range(4):
                    tr(psb[:, pb, j * 128:(j + 1) * 128], gn[:, j * 128:(j + 1) * 128], identb[:], ["gn", "identb"], [("psb", pb)])
                cpi[0] += 1
                for j in range(4):
                    cp(xT[:, j, P0 + c * 128:P0 + (c + 1) * 128], psb[:, pb, j * 128:(j + 1) * 128], [("psb", pb)], ["xTk"], eng=("act" if cpi[0] % 2 else "dve"))


        def scan(g, full):
            if full:
                dma("sp", smid, cc_dst[128 * g:128 * (g + 1), :], ["cc_dst"], ["smid", "R"])
                ts(Sst, smid, flag, None, ALU.mult, None, ["smid", "R", "cst"], ["Sst"])
            else:
                vop(lambda e: e.memset(Sst, 0.0), [], ["Sst"])
            cp(Sbf, Sst, ["Sst"], ["Sbf"], eng="act")
            stageA(g, 0, full)
            for c in range(8):
                if c + 1 < 8:
                    stageA(g, c + 1, full)
                stageB(g, c, full)

        for g in range(8):
            inproj_conv(g, False)
            scan(g, False)
            dma("sp", cc_src[128 * g:128 * (g + 1), :], Sst, ["Sst"], [("cc_src", g)])
        S.op("pool", lambda e: e.collective_compute("AllGather", ALU.bypass, replica_groups=[[0, 1], [2, 3], [4, 5], [6, 7]],
                                                     ins=[cc_src.opt()], outs=[cc_dst.opt()]),
             [("cc_src", g) for g in range(8)], ["cc_dst"])
        barrier()

        L64 = lg[:, 0:128]
        for g in range(8):
            ts(rg[0:64, :], rallb[0:64, :], lg[:, 132 + g:133 + g], None, ALU.mult, None, ["rall", "lg"], ["rg"])
            b = bank()
            mm(ps[:, b, 0:132], L64, rg[0:64, :], True, True, ["lg", "rg"], [("ps", b)])
            cp(dq, ps[:, b, 0:132], [("ps", b)], ["dq"])
            tt(xdts, xs_conv[:, 4 * g:4 * g + 4, :].rearrange("p j b -> p b j"), dq[:, 0:64].rearrange("p (b j) -> p b j", b=NS), ALU.mult,
               ["xs_conv", "dq"], ["xdts"])
            pb = pbank()
            tr(psb[0:16, pb, 0:128], BsT[:, g, :], identb[:], ["BsT", "identb"], [("psb", pb)])
            tr(psb[0:16, pb, 128:256], CsT[:, g, :], identb[:], ["CsT", "identb"], [("psb", pb)])
            cp(bcs[0:16, :], psb[0:16, pb, 0:256], [("psb", pb)], ["bcs"])
            for bsm in range(NS):
                st = stS[bsm % 2]; sk = ("stS", bsm % 2)
                dma("sp", st, IN("sst")[bsm, 512 * g:512 * (g + 1), :].rearrange("(j q) n -> q j n", q=128), [], [sk])
                b = bank()
                mm(ps[:, b, 0:256], selb[:, bsm * 128:(bsm + 1) * 128], bcs[0:16, :], True, True, ["selb", "bcs"], [("ps", b)])
                tt(t1, xdts[:, bsm, :].unsqueeze(2).to_broadcast([128, 4, 128]), ps[:, b, 0:128].unsqueeze(1).to_broadcast([128, 4, 128]), ALU.mult,
                   ["xdts", ("ps", b)], ["t1"])
                tt(st, st, dq[:, 64 + 4 * bsm:64 + 4 * bsm + 4].unsqueeze(2).to_broadcast([128, 4, 128]), ALU.mult, [sk, "dq"], [sk])
                tt(st, st, t1, ALU.add, [sk, "t1"], [sk])
                ok = ("o_sss", g, bsm); outkeys.append(ok)
                dma("sp", sss_o[bsm, 512 * g:512 * (g + 1), :].rearrange("(j q) n -> q j n", q=128), st, [sk], [ok])
                tt(t1, st, ps[:, b, 128:256].unsqueeze(1).to_broadcast([128, 4, 128]), ALU.mult, [sk, ("ps", b)], ["t1"])
                vop(lambda e: e.tensor_reduce(out=tstat[:, 12:16], in_=t1, axis=AX.X, op=ALU.add), ["t1"], ["ysr"])
                tt(tstat[:, 4:8], xs_conv[:, 4 * g:4 * g + 4, bsm], dq[:, 128:132], ALU.mult, ["xs_conv", "dq"], ["ydx"])
                tt(y_s[:, 4 * g:4 * g + 4, bsm], tstat[:, 12:16], tstat[:, 4:8], ALU.add, ["ysr", "ydx"], ["y_s"])
        barrier()

        for g in range(8):
            inproj_conv(g, True)
            store_conv(g)
            scan(g, True)
            b = bank()
            for j in range(4):
                tr(ps[:, b, j * 128:(j + 1) * 128], Sst[:, j * 128:(j + 1) * 128], ident, ["Sst", "cst"], [("ps", b)])
            cp(tmpA, ps[:, b, :], [("ps", b)], ["tmpA"])
            ok = ("o_ssp", g); outkeys.append(ok)
            dma("sp", ssp_o[512 * g:512 * (g + 1), :].rearrange("(j q) n -> q j n", q=128), tmpA.rearrange("p (j n) -> p j n", j=4), ["tmpA"], [ok])
            cp(xT[:, :, 0:NS], y_s[:, 4 * g:4 * g + 4, :], ["y_s"], ["xTk"])
            for half in range(2):
                si, view, _ = load_w([IN("ssd_w_in")[:, 512 * g + 256 * half:512 * g + 256 * (half + 1)]], KT)
                for t in range(2):
                    j = half * 2 + t
                    def ev(bi, c0, n, p, pk, j=j):
                        act(gg[:, 0:n], p, AF.Silu, [pk], ["gg"])
                        tt(xT[:, j, c0:c0 + n], xT[:, j, c0:c0 + n], gg[:, 0:n], ALU.mult, ["xTk", "gg"], [("gT", j, bi)])
                    proj_fm(view, si, t * 128, KT, xn_rhs, xn_keys, EB, ev)
            rmsnorm_cols(lambda kt, c0, n: xT[:, kt, c0:c0 + n], lambda kt, bi: ("gT", kt, bi),
                         lambda kt, c0, n: xT[:, kt, c0:c0 + n], lambda kt, bi: ("gT", kt, bi),
                         PV_SNORM + 4 * g, 4, EB, 512.0)
            out_proj(lambda c0, n, g=g: IN("ssd_w_out")[512 * g:512 * (g + 1), c0:c0 + n],
                     lambda k, c0, n: xT[:, k, c0:c0 + n], lambda k, bi: [("gT", k, bi), "xTk"], 4)
            vop(lambda e: e.memset(tstat[:, 1:2], 0.0), [("gT", k, bi) for k in range(4) for bi in range(3)], ["xTk"])
        barrier()

    import os
    kstop = int(os.environ.get("KSTOP", "7"))

    def ssd_phase():
        mixer_ssd()
        S.op("dve", lambda e: e.memset(sm[:, 508:509], 0.0), [], [("stg", 0), ("stg", 1), "fence2"])
        barrier()

    phases = [mixer_sc, lambda: xattn(0, True), lambda: ffn(0), ssd_phase, lambda: xattn(1, False), lambda: ffn(1)]
    for ph in phases[:kstop]:
        ph()
    if kstop >= 7:
        rmsnorm_cols(lambda kt, c0, n: hT[:, kt, c0:c0 + n], hkeys, lambda kt, c0, n: hT[:, kt, c0:c0 + n], hkeys, PV_NFIN, KT, EB, float(D))
    if KDBG >= 5:
        barrier()
    if KDBG >= 6:
        out_transpose(lambda ct: hT[:, ct, S0:S0 + NS], all_h_keys, NS, D, ys_o, "o_ys")
    for i in range(8 if KDBG >= 7 else 0):
        out_transpose((lambda ct, i=i: hT[:, ct, P0 + i * 128:P0 + (i + 1) * 128]), all_h_keys, 128, D, yp_o[i * 128:(i + 1) * 128, :], ("o_yp", i))
    S.op("sp", lambda e: None, outkeys, [], real=False)

    sems = {e: es.enter_context(nc.semaphore("sem_" + e)) for e in Sched.ENGS}
    dsems = {q: [es.enter_context(nc.semaphore(f"dsem_{q}{i}")) for i in range(S.NDS)] for q in ("pool", "sp")}
    block = es.enter_context(nc.Block())
    S.replay(nc, block, sems, dsems)
    es.close()
    nc._used_inputs = sorted(_in_aps.keys())
    return nc


def _fm(v, nt):
    return np.ascontiguousarray(np.asarray(v, np.float32).reshape(nt, 128).T)


def _consts():
    cst = np.zeros((128, 1160), np.float32)
    cst[:, 0:128] = np.eye(128, dtype=np.float32)
    s = np.arange(128)
    tri = (s[:, None] <= s[None, :]).astype(np.float32)
    cst[:, 128:256] = tri
    cst[:, 256:384] = 1.0
    cst[:, 384:896] = np.tile((1.0 - tri) * -30000.0, (1, 4))
    cst[:, 896:1152] = np.tile(np.eye(16, dtype=np.float32).reshape(1, 256), (128, 1))
    sel = np.zeros((16, 16, 128), np.float32)
    for b in range(16):
        sel[b, b, :] = 1.0
    lg = np.zeros((64, 140), np.float32)
    k = np.arange(64)
    q = np.arange(128)
    lg[:, 0:128] = ((k[:, None] % 2) == (q[None, :] // 64)).astype(np.float32)
    lg[:, 128:132] = (((k[:, None] % 8) // 2) == np.arange(4)[None, :]).astype(np.float32)
    lg[:, 132:140] = ((k[:, None] // 8) == np.arange(8)[None, :]).astype(np.float32)
    return cst, sel.reshape(16, 2048), lg


_PROG = {}


def kernel(x_prompt, x_sample, mem_prompt, cache_sc, state_ssd_conv, state_ssd, cache_mem_k, cache_mem_v,
           norm_mix, norm_mem_q, norm_mem_kv, norm_ffn, norm_final,
           sc_w_in, sc_w_conv, sc_w_out,
           ssd_w_in, ssd_conv_w, ssd_conv_b, ssd_dt_bias, ssd_a_log, ssd_d, ssd_norm, ssd_w_out,
           xa_w_q, xa_w_k, xa_w_v, xa_w_o, ffn_w_gate, ffn_w_up, ffn_w_down):
    f32 = np.float32
    A = lambda a: np.ascontiguousarray(np.asarray(a, f32))
    x_prompt = A(x_prompt); x_sample = A(x_sample); mem_prompt = A(mem_prompt)
    cache_sc = A(cache_sc); state_ssd_conv = A(state_ssd_conv); state_ssd = A(state_ssd)
    cache_mem_k = A(cache_mem_k); cache_mem_v = A(cache_mem_v)

    pvec = np.zeros((128, 512), f32)
    for l in range(2):
        pvec[:, PV_NMIX + 16 * l:PV_NMIX + 16 * (l + 1)] = _fm(norm_mix[l], 16)
        pvec[:, PV_NQ + 16 * l:PV_NQ + 16 * (l + 1)] = _fm(norm_mem_q[l], 16)
        pvec[:, PV_NKV + 16 * l:PV_NKV + 16 * (l + 1)] = _fm(norm_mem_kv[l], 16)
        pvec[:, PV_NFFN + 16 * l:PV_NFFN + 16 * (l + 1)] = _fm(norm_ffn[l], 16)
    pvec[:, PV_NFIN:PV_NFIN + 16] = _fm(norm_final, 16)
    for r in range(3):
        pvec[:, PV_SCW + 16 * r:PV_SCW + 16 * (r + 1)] = _fm(np.asarray(sc_w_conv)[0, r], 16)
    for r in range(4):
        pvec[:, PV_CW + 48 * r:PV_CW + 48 * (r + 1)] = _fm(np.asarray(ssd_conv_w)[0, r], 48)
    pvec[:, PV_CB:PV_CB + 48] = _fm(np.asarray(ssd_conv_b)[0], 48)
    pvec[:, PV_SNORM:PV_SNORM + 32] = _fm(np.asarray(ssd_norm)[0], 32)
    tokbc = np.zeros((128, 192), f32)
    tokbc[:, 0:64] = np.asarray(ssd_dt_bias, f32)[0][None, :]
    tokbc[:, 64:128] = np.asarray(ssd_a_log, f32)[0][None, :]
    tokbc[:, 128:192] = np.asarray(ssd_d, f32)[0][None, :]
    hvec = np.zeros((64, 8), f32)
    hvec[:, 0] = np.asarray(ssd_dt_bias, f32)[0]; hvec[:, 1] = np.asarray(ssd_a_log, f32)[0]; hvec[:, 2] = np.asarray(ssd_d, f32)[0]
    cst0, sel, lg = _consts()

    shared = {
        "pvec": pvec, "tokbc": tokbc, "hvec": hvec, "selb": sel, "lg": lg,
        "sc_w_in": A(sc_w_in)[0], "sc_w_out": A(sc_w_out)[0], "ssd_w_in": A(ssd_w_in)[0], "ssd_w_out": A(ssd_w_out)[0],
        "xa_w_q": A(xa_w_q), "xa_w_k": A(xa_w_k), "xa_w_v": A(xa_w_v), "xa_w_o": A(xa_w_o),
        "ffn_w_gate": A(ffn_w_gate), "ffn_w_up": A(ffn_w_up), "ffn_w_down": A(ffn_w_down),
    }
    in_maps = []
    for c in range(NCORES):
        seq, half = c // 2, c % 2
        st = half * NP
        cst = cst0.copy()
        cst[:, 1152] = float(half)
        xh = x_prompt[seq, st - NH:st] if half else np.zeros((NH, D), f32)
        sl = slice(NS * c, NS * (c + 1))
        m = dict(shared)
        m.update({
            "xp": np.ascontiguousarray(x_prompt[seq, st:st + NP]), "xh": np.ascontiguousarray(xh),
            "xs": np.ascontiguousarray(x_sample[sl, 0]), "mem": np.ascontiguousarray(mem_prompt[seq]),
            "csc": np.ascontiguousarray(cache_sc[0, sl].reshape(2 * NS, D)),
            "ccv": np.ascontiguousarray(state_ssd_conv[0, sl].reshape(3 * NS, CONV)),
            "sst": np.ascontiguousarray(state_ssd[0, sl].reshape(NS, 4096, 128)),
            "ck": np.ascontiguousarray(cache_mem_k[:, sl].reshape(2, NS, 256, D)),
            "cv": np.ascontiguousarray(cache_mem_v[:, sl].reshape(2, NS, 256, D)),
            "cst": cst,
        })
        in_maps.append(m)

    if "nc" not in _PROG:
        _PROG["nc"] = build_program()
    used = _PROG["nc"]._used_inputs
    in_maps = [{k: m[k] for k in used} for m in in_maps]
    res = run_bass_kernel_spmd(_PROG["nc"], in_maps, core_ids=list(range(NCORES)))
    R = res.results

    y_prompt = np.zeros((4, 2048, D), f32); y_sample = np.zeros((128, 1, D), f32)
    sc_p = np.zeros((1, 4, 2, D), f32); sc_s = np.zeros((1, 128, 2, D), f32)
    cv_p = np.zeros((1, 4, 3, CONV), f32); cv_s = np.zeros((1, 128, 3, CONV), f32)
    ss_p = np.zeros((1, 4, 64, 64, 128), f32); ss_s = np.zeros((1, 128, 64, 64, 128), f32)
    mk = np.zeros((2, 4, 256, 4, 512), f32); mv = np.zeros((2, 4, 256, 4, 512), f32)
    for c in range(NCORES):
        seq, half = c // 2, c % 2
        sl = slice(NS * c, NS * (c + 1))
        r = R[c]
        y_prompt[seq, half * NP:(half + 1) * NP] = r["y_p"]
        y_sample[sl, 0] = r["y_s"]
        sc_s[0, sl] = r["sc_s"].reshape(NS, 2, D)
        cv_s[0, sl] = r["cv_s"].reshape(NS, 3, CONV)
        ss_s[0, sl] = r["ss_s"].reshape(NS, 64, 64, 128)
        if half == 1:
            sc_p[0, seq] = r["sc_p"]
            cv_p[0, seq] = r["cv_p"]
            ss_p[0, seq] = r["ss_p"].reshape(64, 64, 128)
        else:
            mk[:, seq] = r["mk"].reshape(2, 256, 4, 512)
            mv[:, seq] = r["mv"].reshape(2, 256, 4, 512)
    return (y_prompt, y_sample, sc_p, sc_s, cv_p, cv_s, ss_p, ss_s, mk, mv)
```

```python
import numpy as np
import concourse.bass as bass
import concourse.mybir as mybir
from concourse.bass_utils import run_bass_kernel_spmd

F32 = mybir.dt.float32
BF16 = mybir.dt.bfloat16
AF = mybir.ActivationFunctionType
ALU = mybir.AluOpType
AX = mybir.AxisListType

D = 2048
KT = 16
NS, NH, NP = 16, 8, 1024
S0, H0, P0 = 0, 16, 24
NCOL = NS + NH + NP
BLKS = [(0, 24), (24, 512), (536, 512)]
DFF = 5632
DIN = 4096
CONV = 6144
EPS = 1e-5
NCORES = 8


class Op:
    __slots__ = ("eng", "idx", "fn", "waits", "signal", "dma", "rank", "real")

    def __init__(self, eng, idx, fn, dma):
        self.eng, self.idx, self.fn, self.dma = eng, idx, fn, dma
        self.waits = []
        self.signal = False
        self.rank = None
        self.real = True


class Sched:
    ENGS = ("pe", "act", "dve", "pool", "sp")

    def __init__(self):
        self.q = {e: [] for e in self.ENGS}
        self.lastw = {}
        self.readers = {}
        self.seen = {e: {} for e in self.ENGS}
        self.ndma = {"pool": 0, "sp": 0}
        self.NDS = 6

    def op(self, eng, fn, reads=(), writes=(), dma=False, real=True):
        o = Op(eng, len(self.q[eng]), fn, None)
        o.real = real
        deps = []
        for k in reads:
            w = self.lastw.get(k)
            if w is not None:
                deps.append((w, True))
        for k in writes:
            w = self.lastw.get(k)
            if w is not None:
                deps.append((w, False))
            for r in self.readers.get(k, ()):
                deps.append((r, False))
        if dma:
            i = self.ndma[eng]
            self.ndma[eng] += 1
            o.dma = (eng, i % self.NDS, 16 * (i // self.NDS + 1))
            if i >= self.NDS:
                o.waits.append(("dma", eng, i % self.NDS, 16 * (i // self.NDS)))
        for d, raw in deps:
            if d is o:
                continue
            if d.dma is not None:
                key = ("dma", d.dma[0], d.dma[1])
                if self.seen[eng].get(key, 0) >= d.dma[2]:
                    continue
                self.seen[eng][key] = d.dma[2]
                o.waits.append(("dma", d.dma[0], d.dma[1], d.dma[2]))
            else:
                if d.eng == eng and (eng == "pe" or not raw):
                    continue
                if self.seen[eng].get(d.eng, -1) >= d.idx:
                    continue
                self.seen[eng][d.eng] = d.idx
                d.signal = True
                o.waits.append(("op", d))
        for k in reads:
            self.readers.setdefault(k, []).append(o)
        for k in writes:
            self.lastw[k] = o
            self.readers[k] = []
        self.q[eng].append(o)
        return o

    def replay(self, nc, block, sems, dsems):
        for e in self.ENGS:
            r = 0
            for o in self.q[e]:
                if o.signal:
                    r += 1
                    o.rank = r
        names = {"pe": "tensor", "act": "scalar", "dve": "vector", "pool": "gpsimd", "sp": "sync"}
        fin = Op("sp", len(self.q["sp"]), (lambda e: None), None)
        fin.real = False
        for qn, n in self.ndma.items():
            for si in range(min(n, self.NDS)):
                last_i = ((n - 1 - si) // self.NDS) * self.NDS + si
                fin.waits.append(("dma", qn, si, 16 * (last_i // self.NDS + 1)))
        self.q["sp"].append(fin)

        def make(ename):
            def body(eng):
                for o in self.q[ename]:
                    for w in o.waits:
                        if w[0] == "dma":
                            eng.wait_ge(dsems[w[1]][w[2]], w[3])
                        else:
                            eng.wait_ge(sems[w[1].eng], w[1].rank)
                    ins = o.fn(eng)
                    if ins is None:
                        continue
                    if o.dma is not None:
                        ins.then_inc(dsems[o.dma[0]][o.dma[1]], 16)
                    elif o.signal:
                        ins.then_inc(sems[ename], 1)
            return body

        for ename in self.ENGS:
            getattr(block, names[ename])(make(ename))


PV_NMIX, PV_NQ, PV_NKV, PV_NFFN, PV_NFIN = 0, 32, 64, 96, 128
PV_SCW, PV_CW, PV_CB, PV_SNORM = 144, 192, 384, 432
NSLOT = 3
SLOTW = 4096


def build_program():
    import os
    nc = bass.Bass("TRN2", target_bir_lowering=False)
    S = Sched()
    from contextlib import ExitStack
    es = ExitStack()

    def din(name, shape):
        return nc.dram_tensor(name, list(shape), F32, kind="ExternalInput").ap()

    def dout(name, shape):
        return nc.dram_tensor(name, list(shape), F32, kind="ExternalOutput").ap()

    _in_shapes = {
        "xp": [NP, D], "xh": [NH, D], "xs": [NS, D], "mem": [256, D], "csc": [2 * NS, D], "ccv": [3 * NS, CONV],
        "sst": [NS, 4096, 128], "ck": [2, NS, 256, D], "cv": [2, NS, 256, D],
        "pvec": [128, 512], "tokbc": [128, 192], "cst": [128, 1160], "hvec": [64, 8], "selb": [16, 2048], "lg": [64, 140],
        "sc_w_in": [D, 3 * D], "sc_w_out": [D, D], "ssd_w_in": [D, 10304], "ssd_w_out": [DIN, D],
        "xa_w_q": [2, D, D], "xa_w_k": [2, D, D], "xa_w_v": [2, D, D], "xa_w_o": [2, D, D],
        "ffn_w_gate": [2, D, DFF], "ffn_w_up": [2, D, DFF], "ffn_w_down": [2, DFF, D],
    }
    _in_aps = {}

    def IN(name):
        if name not in _in_aps:
            _in_aps[name] = din(name, _in_shapes[name])
        return _in_aps[name]

    yp_o = dout("y_p", [NP, D]); ys_o = dout("y_s", [NS, D]); scp_o = dout("sc_p", [2, D]); scs_o = dout("sc_s", [2 * NS, D])
    cvp_o = dout("cv_p", [3, CONV]); cvs_o = dout("cv_s", [3 * NS, CONV]); ssp_o = dout("ss_p", [4096, 128])
    sss_o = dout("ss_s", [NS, 4096, 128]); mk_o = dout("mk", [2, 256, D]); mv_o = dout("mv", [2, 256, D])

    cc_src = nc.dram_tensor("cc_src", [1024, 512], F32, kind="Internal").ap()
    cc_dst = nc.dram_tensor("cc_dst", [2048, 512], F32, kind="Internal").ap()

    def sb(name, shape, dt=F32):
        return es.enter_context(nc.sbuf_tensor("sb_" + name, list(shape), dt))

    hT = sb("hT", [128, KT, NCOL])
    xn = sb("xn", [128, KT, NCOL], BF16)
    bB = sb("bufB", [128, KT, NCOL], BF16)
    wsl = [sb(f"wslot{i}", [128, SLOTW], BF16) for i in range(NSLOT)]
    pvec = sb("pvec", [128, 512]); tokbc = sb("tokbc", [128, 192]); cst = sb("cst", [128, 1160])
    hvec = sb("hvec", [64, 8]); selb = sb("selb", [16, 2048], BF16); lg = sb("lg", [64, 140])
    identb = sb("identb", [128, 128], BF16); onesb = sb("onesb", [128, 128], BF16)
    ARW = int(os.environ.get("ARW", "10160")) if True else 0
    ar = sb("arena", [128, ARW])
    ps = es.enter_context(nc.psum_tensor("ps", [128, 6, 512], F32))
    psb = es.enter_context(nc.psum_tensor("psb", [128, 2, 1024], BF16))

    xnf = xn[:].rearrange("p k n -> p (k n)")
    bBf = bB[:].rearrange("p k n -> p (k n)")
    XN_ALL = [("xn", k, bi) for k in range(KT) for bi in range(3)]
    BB_ALL = [("bB", k, bi) for k in range(KT) for bi in range(3)]

    def arf(o, n):
        return ar[:, o:o + n]

    def arb(o, n):
        return ar[:, o:o + n].bitcast(BF16)

    sq = [arb(0, 256), arb(256, 256)]
    rstd = arf(512, 512)
    sm = arf(1024, 512)
    stg = [arf(1536, 2048), arf(3584, 2048)]
    PX = 5632
    PXN = ARW - PX

    ident = cst[:, 0:128]; tri = cst[:, 128:256]; ones = cst[:, 256:384]; mneg4 = cst[:, 384:896]
    d16 = cst[:, 896:1152].rearrange("p (b c) -> p b c", b=NS)
    flag = cst[:, 1152:1153]

    state = {"bank": 0, "pb": 0, "slot": 0, "stg": 0}
    outkeys = []

    def bank():
        b = state["bank"]; state["bank"] = (b + 1) % 6
        return b

    def pbank():
        b = state["pb"]; state["pb"] = (b + 1) % 2
        return b

    def mm(out, lhsT, rhs, start, stop, reads, writes):
        S.op("pe", lambda e: e.matmul(out, lhsT=lhsT, rhs=rhs, start=start, stop=stop), reads, writes)

    def tr(out, in_, idn, reads, writes):
        S.op("pe", lambda e: e.transpose(out, in_, idn), reads, writes)

    def act(out, in_, func, reads, writes, bias=None, scale=None, accum=None):
        kw = {}
        if bias is not None: kw["bias"] = bias
        if scale is not None: kw["scale"] = scale
        if accum is not None: kw["accum_out"] = accum
        S.op("act", lambda e: e.activation(out=out, in_=in_, func=func, **kw), reads, writes)

    def tt(out, a, b, op, reads, writes, eng="dve"):
        S.op(eng, lambda e: e.tensor_tensor(out=out, in0=a, in1=b, op=op), reads, writes)

    def ts(out, a, s1, s2, op0, op1, reads, writes, eng="dve"):
        if op1 is None:
            S.op(eng, lambda e: e.tensor_scalar(out=out, in0=a, scalar1=s1, scalar2=None, op0=op0), reads, writes)
        else:
            S.op(eng, lambda e: e.tensor_scalar(out=out, in0=a, scalar1=s1, scalar2=s2, op0=op0, op1=op1), reads, writes)

    def stt(out, a, s, b, op0, op1, reads, writes):
        S.op("dve", lambda e: e.scalar_tensor_tensor(out=out, in0=a, scalar=s, in1=b, op0=op0, op1=op1), reads, writes)

    def cp(out, in_, reads, writes, eng="dve"):
        if eng == "act":
            S.op("act", lambda e: e.copy(out=out, in_=in_), reads, writes)
        else:
            S.op(eng, lambda e: e.tensor_copy(out=out, in_=in_), reads, writes)

    def vop(fn, reads, writes):
        S.op("dve", fn, reads, writes)

    def dma(q, out, in_, reads, writes):
        S.op(q, lambda e: e.dma_start(out=out, in_=in_), reads, writes, dma=True)

    cpi = [0]

    def cp_alt(out, in_, reads, writes):
        cpi[0] += 1
        cp(out, in_, reads, writes, eng=("act" if cpi[0] % 2 else "dve"))

    def barrier():
        engs = ("pe", "act", "dve")
        lasts = {}
        for e in engs:
            for o_ in reversed(S.q[e]):
                if o_.real:
                    lasts[e] = o_
                    break
        for e in engs:
            o = S.op(e, lambda eng: None, real=False)
            for e2, l in lasts.items():
                if e2 != e and S.seen[e].get(e2, -1) < l.idx:
                    S.seen[e][e2] = l.idx
                    l.signal = True
                    o.waits.append(("op", l))

    dma("sp", pvec[:], IN("pvec"), [], ["pvec"]); dma("sp", tokbc[:], IN("tokbc"), [], ["tokbc"]); dma("sp", cst[:], IN("cst"), [], ["cst"])
    dma("sp", hvec[:], IN("hvec"), [], ["hvec"]); dma("sp", lg[:], IN("lg"), [], ["lg"])
    dma("pool", selb[:], IN("selb"), [], ["selb"])
    import os
    KDBG = int(os.environ.get("KDBG", "9"))
    cp(identb[:], ident, ["cst"], ["identb"]); cp(onesb[:], ones, ["cst"], ["onesb"])
    if KDBG >= 1:
        act(hvec[:, 3:4], hvec[:, 1:2], AF.Exp, ["hvec"], ["hvec_a"])
        ts(hvec[:, 3:4], hvec[:, 3:4], -1.0, None, ALU.mult, None, ["hvec_a"], ["hvec_a"])
        act(tokbc[:, 64:128], tokbc[:, 64:128], AF.Exp, ["tokbc"], ["tokbc"])
        ts(tokbc[:, 64:128], tokbc[:, 64:128], -1.0, None, ALU.mult, None, ["tokbc"], ["tokbc"])
    a_bc = tokbc[:, 64:128]; d_bc = tokbc[:, 128:192]

    def in_transpose(rows_ap, R, C, dst, dst_keys_fn):
        si = state["stg"]; state["stg"] ^= 1
        st = stg[si]
        dma("sp", st[0:R, 0:C], rows_ap, [], [("stg", si)])
        nct = C // 128
        for c0 in range(0, nct, 4):
            n = min(4, nct - c0)
            b = bank()
            for j in range(n):
                tr(ps[:, b, j * 128:j * 128 + R], st[0:R, (c0 + j) * 128:(c0 + j + 1) * 128], ident[0:R, 0:R],
                   [("stg", si), "cst"], [("ps", b)])
            cpi[0] += 1
            eng_ = "act" if cpi[0] % 2 else "dve"
            for j in range(n):
                cp(dst(c0 + j), ps[:, b, j * 128:j * 128 + R], [("ps", b)], dst_keys_fn(c0 + j), eng=eng_)

    def out_transpose(src, src_keys_fn, R, C, out_ap, okey):
        si = state["stg"]; state["stg"] ^= 1
        st = stg[si]
        nct = C // 128
        for c0 in range(0, nct, 4):
            n = min(4, nct - c0)
            b = bank()
            for j in range(n):
                tr(ps[0:R, b, j * 128:(j + 1) * 128], src(c0 + j), ident, list(src_keys_fn(c0 + j)) + ["cst"], [("ps", b)])
            cp_alt(st[0:R, c0 * 128:(c0 + n) * 128], ps[0:R, b, 0:n * 128], [("ps", b)], [("stg", si)])
        dma("sp", out_ap, st[0:R, 0:C], [("stg", si)], [okey])
        outkeys.append(okey)

    def hkeys(kt, bi):
        return ("hT", kt, bi)

    def rmsnorm_cols(src, src_key, dstf, dst_key, gcol, ntile, blocks, scale_div):
        for bi, (c0, n) in blocks:
            b = bank()
            for kt in range(ntile):
                si = kt % 2
                act(sq[si][:, 0:n], src(kt, c0, n), AF.Square, [src_key(kt, bi)], [("sq", si)])
                mm(ps[:, b, 0:n], onesb[:], sq[si][:, 0:n], kt == 0, kt == ntile - 1, [("sq", si), "onesb"], [("ps", b)])
            ts(rstd[:, 0:n], ps[:, b, 0:n], 1.0 / scale_div, EPS, ALU.mult, ALU.add, [("ps", b)], ["rstd"])
            act(rstd[:, 0:n], rstd[:, 0:n], AF.Ln, ["rstd"], ["rstd"])
            act(rstd[:, 0:n], rstd[:, 0:n], AF.Exp, ["rstd"], ["rstd"], scale=-0.5)
            for kt in range(ntile):
                stt(dstf(kt, c0, n), src(kt, c0, n), pvec[:, gcol + kt:gcol + kt + 1], rstd[:, 0:n], ALU.mult, ALU.mult,
                    [src_key(kt, bi), "rstd", "pvec"], [dst_key(kt, bi)])

    EB = list(enumerate(BLKS))

    def norm_h(gcol):
        rmsnorm_cols(lambda kt, c0, n: hT[:, kt, c0:c0 + n], hkeys, lambda kt, c0, n: xn[:, kt, c0:c0 + n],
                     lambda kt, bi: ("xn", kt, bi), gcol, KT, EB, float(D))

    def load_w(pieces, nk):
        si = state["slot"]; state["slot"] = (si + 1) % NSLOT
        tot = sum(p.shape[1] for p in pieces)
        assert nk * tot <= SLOTW, (nk, tot)
        view = wsl[si][:, 0:nk * tot].rearrange("p (k n) -> p k n", k=nk)
        off = 0
        offs = []
        for p in pieces:
            n = p.shape[1]
            dma("pool", view[:, :, off:off + n], p.rearrange("(k p) n -> p k n", p=128), [], [("w", si)])
            offs.append(off); off += n
        return si, view, offs

    def proj_fm(view, si, col0, nk, rhs, rhs_keys, blocks, evac, M=128):
        for bi, (c0, n) in blocks:
            b = bank()
            for k in range(nk):
                mm(ps[0:M, b, 0:n], view[:, k, col0:col0 + M], rhs(k, c0, n), k == 0, k == nk - 1,
                   [("w", si)] + list(rhs_keys(k, bi)), [("ps", b)])
            evac(bi, c0, n, ps[0:M, b, 0:n], ("ps", b))

    def xn_rhs(k, c0, n):
        return xn[:, k, c0:c0 + n]

    def xn_keys(k, bi):
        return [("xn", k, bi)]

    def bB_rhs(k, c0, n):
        return bB[:, k, c0:c0 + n]

    def bB_keys(k, bi):
        return [("bB", k, bi)]

    def add_to_h(nt):
        def ev(bi, c0, n, p, pk):
            tt(hT[:, nt, c0:c0 + n], hT[:, nt, c0:c0 + n], p, ALU.add, [hkeys(nt, bi), pk], [hkeys(nt, bi)])
        return ev

    def out_proj(wfn, rhs, rhs_keys, nk, k0=0):
        ncc = SLOTW // nk // 128
        ncc = min(ncc, 4)
        for c in range(0, 16, ncc):
            si, view, offs = load_w([wfn(c * 128, ncc * 128)], nk)
            for t in range(ncc):
                proj_fm(view, si, t * 128, nk, rhs, rhs_keys, EB, add_to_h(c + t))

    def all_h_keys(ct):
        return [hkeys(ct, 0), hkeys(ct, 1), hkeys(ct, 2)]

    if KDBG >= 2:
        in_transpose(IN("xs"), NS, D, lambda ct: hT[:, ct, S0:S0 + NS], all_h_keys)
    if KDBG >= 3:
        in_transpose(IN("xh"), NH, D, lambda ct: hT[:, ct, H0:H0 + NH], all_h_keys)
    for i in range(8 if KDBG >= 4 else 0):
        in_transpose(IN("xp")[i * 128:(i + 1) * 128, :], 128, D, (lambda ct, i=i: hT[:, ct, P0 + i * 128:P0 + (i + 1) * 128]), all_h_keys)

    def mixer_sc():
        norm_h(PV_NMIX + 0)
        hist = arf(PX, 512).rearrange("p (k r) -> p k r", k=KT)
        uo_p = arf(PX + 512, 32).rearrange("p (k r) -> p k r", k=KT)
        uo_s = arf(PX + 544, 512).rearrange("p (k r) -> p k r", k=KT)
        u = arf(PX + 1056, NCOL); bg = arb(PX + 2104, NCOL // 2); acc = arf(PX + 2628, NCOL)
        assert 2628 + NCOL <= PXN
        in_transpose(IN("csc"), 2 * NS, D, lambda ct: hist[:, ct, :], lambda ct: ["hist"])
        for j in range(KT):
            for which in range(3):
                si, view, offs = load_w([IN("sc_w_in")[:, which * D + j * 128: which * D + (j + 1) * 128]], KT)
                if which == 0:
                    proj_fm(view, si, 0, KT, xn_rhs, xn_keys, EB, lambda bi, c0, n, p, pk: cp(bg[:, c0:c0 + n], p, [pk], ["bg"], eng="act"))
                elif which == 1:
                    proj_fm(view, si, 0, KT, xn_rhs, xn_keys, EB, lambda bi, c0, n, p, pk: cp(u[:, c0:c0 + n], p, [pk], ["u"], eng="act"))
                else:
                    proj_fm(view, si, 0, KT, xn_rhs, xn_keys, EB, lambda bi, c0, n, p, pk: tt(u[:, c0:c0 + n], u[:, c0:c0 + n], p, ALU.mult, ["u", pk], ["u"]))
            w0 = pvec[:, PV_SCW + j:PV_SCW + j + 1]; w1 = pvec[:, PV_SCW + 16 + j:PV_SCW + 17 + j]; w2 = pvec[:, PV_SCW + 32 + j:PV_SCW + 33 + j]
            L = NCOL - 18
            ts(acc[:, 18:NCOL], u[:, 16:16 + L], w0, None, ALU.mult, None, ["u", "pvec"], ["acc"])
            stt(acc[:, 18:NCOL], u[:, 17:17 + L], w1, acc[:, 18:NCOL], ALU.mult, ALU.add, ["u", "acc"], ["acc"])
            stt(acc[:, 18:NCOL], u[:, 18:18 + L], w2, acc[:, 18:NCOL], ALU.mult, ALU.add, ["u", "acc"], ["acc"])
            hv = hist[:, j, :].rearrange("p (b r) -> p b r", r=2)
            ts(acc[:, 0:NS], hv[:, :, 0], w0, None, ALU.mult, None, ["hist", "pvec"], ["acc"])
            stt(acc[:, 0:NS], hv[:, :, 1], w1, acc[:, 0:NS], ALU.mult, ALU.add, ["hist", "acc"], ["acc"])
            stt(acc[:, 0:NS], u[:, 0:NS], w2, acc[:, 0:NS], ALU.mult, ALU.add, ["u", "acc"], ["acc"])
            vop(lambda e, j=j: e.memset(bB[:, j, 16:18], 0.0), [], [("bB", j, 0)])
            tt(bB[:, j, 18:NCOL], acc[:, 18:NCOL], bg[:, 18:NCOL], ALU.mult, ["acc", "bg"], [("bB", j, 0), ("bB", j, 1), ("bB", j, 2)])
            tt(bB[:, j, 0:NS], acc[:, 0:NS], bg[:, 0:NS], ALU.mult, ["acc", "bg"], [("bB", j, 0)])
            cp(uo_p[:, j, :], u[:, NCOL - 2:NCOL], ["u"], ["uo_p"], eng="act")
            uov = uo_s[:, j, :].rearrange("p (b r) -> p b r", r=2)
            cp(uov[:, :, 0], hv[:, :, 1], ["hist"], ["uo_s"], eng="act")
            cp(uov[:, :, 1], u[:, 0:NS], ["u"], ["uo_s"], eng="act")
        out_transpose(lambda ct: uo_p[:, ct, :], lambda ct: ["uo_p"], 2, D, scp_o, "o_scp")
        out_transpose(lambda ct: uo_s[:, ct, :], lambda ct: ["uo_s"], 2 * NS, D, scs_o, "o_scs")
        out_proj(lambda c0, n: IN("sc_w_out")[:, c0:c0 + n], bB_rhs, bB_keys, KT)

    def xattn(layer, with_halo):
        barrier()
        memT = bBf[:, 0:8192].bitcast(F32).rearrange("p (k m) -> p k m", k=KT)
        mn = xnf[:, 0:4096].rearrange("p (k m) -> p k m", k=KT)
        KTb = arb(PX, 2048).rearrange("p (k m) -> p k m", k=KT)
        Vt = arb(PX + 2048, 2048).rearrange("p (m n) -> p m n", m=2)
        for mt in range(2):
            in_transpose(IN("mem")[mt * 128:(mt + 1) * 128, :], 128, D, (lambda ct, mt=mt: memT[:, ct, mt * 128:(mt + 1) * 128]),
                         lambda ct: [("memT", ct)] + BB_ALL[0:1])
        rmsnorm_cols(lambda kt, c0, n: memT[:, kt, c0:c0 + n], lambda kt, bi: ("memT", kt),
                     lambda kt, c0, n: mn[:, kt, c0:c0 + n], lambda kt, bi: ("mn", kt),
                     PV_NKV + 16 * layer, KT, [(0, (0, 256))], float(D))
        for which, wmat, oap in (("k", IN("xa_w_k"), mk_o), ("v", IN("xa_w_v"), mv_o)):
            for c in range(8):
                si, view, offs = load_w([wmat[layer, :, c * 256:(c + 1) * 256]], KT)
                for mt in range(2):
                    b = bank()
                    for k in range(KT):
                        mm(ps[:, b, 0:256], mn[:, k, mt * 128:(mt + 1) * 128], view[:, k, :], k == 0, k == KT - 1,
                           [("w", si), ("mn", k)], [("ps", b)])
                    sti = state["stg"]; state["stg"] ^= 1
                    cp(stg[sti][:, 0:256], ps[:, b, 0:256], [("ps", b)], [("stg", sti)], eng="act")
                    okey = ("o_m" + which, layer, c, mt)
                    dma("sp", oap[layer, mt * 128:(mt + 1) * 128, c * 256:(c + 1) * 256], stg[sti][:, 0:256], [("stg", sti)], [okey])
                    outkeys.append(okey)
                    if which == "v":
                        cp(Vt[:, mt, c * 256:(c + 1) * 256], stg[sti][:, 0:256], [("stg", sti)], ["Vt"])
                if which == "k":
                    for t in range(2):
                        b = bank()
                        for k in range(KT):
                            mm(ps[:, b, 0:256], view[:, k, t * 128:(t + 1) * 128], mn[:, k, :], k == 0, k == KT - 1,
                               [("w", si), ("mn", k)], [("ps", b)])
                        cp(KTb[:, c * 2 + t, :], ps[:, b, 0:256], [("ps", b)], [("KTb", c * 2 + t)])
        barrier()
        norm_h(PV_NQ + 16 * layer)
        qs = float(512 ** -0.5)
        for c in range(8):
            si, view, offs = load_w([IN("xa_w_q")[layer, :, c * 256:(c + 1) * 256]], KT)
            for t in range(2):
                nt = c * 2 + t
                proj_fm(view, si, t * 128, KT, xn_rhs, xn_keys, EB,
                        lambda bi, c0, n, p, pk, nt=nt: S.op("act", lambda e: e.mul(bB[:, nt, c0:c0 + n], p, qs),
                                                             [pk], [("bB", nt, bi)]))
        barrier()
        tiles = [(P0 + 128 * i, 128, 1 + (i // 4)) for i in range(8)]
        if with_halo:
            tiles.append((H0, NH, 0))
        Pf = sm[:, 0:256]; Pn = sm[:, 256:384].bitcast(BF16)
        PT = sm[:, 384:500].bitcast(BF16)[:, 0:256].rearrange("p (m t) -> p m t", m=2) if False else xnf[:, 0:256].rearrange("p (m t) -> p m t", m=2)
        st1 = sm[:, 500:504]
        for h in range(4):
            for (c0, nt_, bi) in tiles:
                b = bank()
                for j in range(4):
                    mm(ps[0:nt_, b, 0:256], bB[:, 4 * h + j, c0:c0 + nt_], KTb[:, 4 * h + j, :], j == 0, j == 3,
                       [("bB", 4 * h + j, bi), ("KTb", 4 * h + j)], [("ps", b)])
                vop(lambda e, b=b, nt_=nt_: e.tensor_reduce(out=st1[0:nt_, 0:1], in_=ps[0:nt_, b, 0:256], axis=AX.X, op=ALU.max), [("ps", b)], ["st1a"])
                ts(st1[0:nt_, 1:2], st1[0:nt_, 0:1], -1.0, None, ALU.mult, None, ["st1a"], ["st1b"])
                act(Pf[0:nt_, :], ps[0:nt_, b, 0:256], AF.Exp, [("ps", b), "st1b"], ["Pf", "st1c"], bias=st1[0:nt_, 1:2], accum=st1[0:nt_, 2:3])
                vop(lambda e, nt_=nt_: e.reciprocal(out=st1[0:nt_, 3:4], in_=st1[0:nt_, 2:3]), ["st1c"], ["st1d"])
                ts(Pn[0:nt_, :], Pf[0:nt_, :], st1[0:nt_, 3:4], None, ALU.mult, None, ["Pf", "st1d"], ["Pn"])
                pb = pbank()
                for mt in range(2):
                    tr(psb[:, pb, mt * 128:mt * 128 + nt_], Pn[0:nt_, mt * 128:(mt + 1) * 128], identb[0:nt_, 0:nt_], ["Pn", "identb"], [("psb", pb)])
                for mt in range(2):
                    cp(PT[:, mt, 0:nt_], psb[:, pb, mt * 128:mt * 128 + nt_], [("psb", pb)], ["PT"], eng="act")
                for j in range(4):
                    b2 = bank()
                    for mt in range(2):
                        mm(ps[:, b2, 0:nt_], Vt[:, mt, (4 * h + j) * 128:(4 * h + j + 1) * 128], PT[:, mt, 0:nt_], mt == 0, mt == 1,
                           ["Vt", "PT"], [("ps", b2)])
                    cp_alt(bB[:, 4 * h + j, c0:c0 + nt_], ps[:, b2, 0:nt_], [("ps", b2)], [("bB", 4 * h + j, bi)])
        qtok = xnf[0:16, 512:512 + 2048]
        kvst = [xnf[:, 4096 + i * 4096:4096 + (i + 1) * 4096].bitcast(F32) for i in range(2)]
        Pz = xnf[:, 12288:12288 + 4096].bitcast(F32).rearrange("p (f h b c) -> p f h b c", f=2, h=4, b=NS)
        scs = arf(PX, 128).rearrange("p (f b h) -> p f b h", f=2, b=NS)
        junk = arf(PX + 128, 512)
        sp_ = ar[0:64, PX + 640:PX + 896]
        Pm = arf(PX + 896, 128).rearrange("p (f b h) -> p f b h", f=2, b=NS)
        otok = ar[0:16, PX + 1024:PX + 3072]
        st2 = sm[0:64, 504:508]
        barrier()
        for c4 in range(4):
            pb = pbank()
            for j in range(4):
                tr(psb[0:16, pb, j * 128:(j + 1) * 128], bB[:, c4 * 4 + j, 0:NS], identb[:], [("bB", c4 * 4 + j, 0), "identb"], [("psb", pb)])
            cp(qtok[:, c4 * 512:(c4 + 1) * 512], psb[0:16, pb, 0:512], [("psb", pb)], ["qtok"])
        kv = [stg[0], stg[1], kvst[0], kvst[1]]
        kvk = [("stg", 0), ("stg", 1), "kvst0", "kvst1"]
        for bsmp in range(NS):
            o2 = (bsmp % 2) * 2
            for f in range(2):
                dma("sp", kv[o2 + f], IN("ck")[layer, bsmp, f * 128:(f + 1) * 128, :], [], [kvk[o2 + f]] + (XN_ALL if (bsmp < 2 and o2 == 2) else []))
            for h in range(4):
                b = bank()
                mm(ps[:, b, :], selb[:, bsmp * 128:(bsmp + 1) * 128], qtok[:, h * 512:(h + 1) * 512], True, True, ["selb", "qtok"], [("ps", b)])
                for f in range(2):
                    vop(lambda e, b=b, f=f, h=h, o2=o2, bsmp=bsmp: e.scalar_tensor_tensor(
                        out=junk, in0=kv[o2 + f][:, h * 512:(h + 1) * 512], scalar=1.0, in1=ps[:, b, :],
                        op0=ALU.mult, op1=ALU.mult, accum_out=scs[:, f, bsmp, h:h + 1]),
                        [kvk[o2 + f], ("ps", b)], ["junk", "scs"])
        b = bank()
        for f in range(2):
            tr(ps[0:64, b, f * 128:(f + 1) * 128], arf(PX + 64 * f, 64), ident, ["scs", "cst"], [("ps", b)])
        vop(lambda e, b=b: e.tensor_reduce(out=st2[:, 0:1], in_=ps[0:64, b, 0:256], axis=AX.X, op=ALU.max), [("ps", b)], ["st2a"])
        ts(st2[:, 1:2], st2[:, 0:1], -1.0, None, ALU.mult, None, ["st2a"], ["st2b"])
        act(sp_, ps[0:64, b, 0:256], AF.Exp, [("ps", b), "st2b"], ["sp_", "st2c"], bias=st2[:, 1:2], accum=st2[:, 2:3])
        vop(lambda e: e.reciprocal(out=st2[:, 3:4], in_=st2[:, 2:3]), ["st2c"], ["st2d"])
        ts(sp_, sp_, st2[:, 3:4], None, ALU.mult, None, ["sp_", "st2d"], ["sp_"])
        b = bank()
        for f in range(2):
            tr(ps[:, b, f * 64:(f + 1) * 64], sp_[:, f * 128:(f + 1) * 128], ident[0:64, 0:64], ["sp_", "cst"], [("ps", b)])
        cp(arf(PX + 896, 128), ps[:, b, 0:128], [("ps", b)], ["Pm"])
        for f in range(2):
            for h in range(4):
                tt(Pz[:, f, h, :, :], Pm[:, f, :, h].unsqueeze(2).to_broadcast([128, NS, NS]), d16, ALU.mult, ["Pm", "cst"], ["Pz"])
        ob = [bank() for _ in range(4)]
        for bsmp in range(NS):
            o2 = (bsmp % 2) * 2
            for f in range(2):
                dma("sp", kv[o2 + f], IN("cv")[layer, bsmp, f * 128:(f + 1) * 128, :], [], [kvk[o2 + f]])
            for h in range(4):
                for f in range(2):
                    mm(ps[0:16, ob[h], :], Pz[:, f, h, bsmp, :], kv[o2 + f][:, h * 512:(h + 1) * 512], bsmp == 0 and f == 0, bsmp == NS - 1 and f == 1,
                       ["Pz", kvk[o2 + f]], [("ps", ob[h])])
        for h in range(4):
            cp_alt(otok[:, h * 512:(h + 1) * 512], ps[0:16, ob[h], :], [("ps", ob[h])], ["otok"])
        for c4 in range(4):
            b = bank()
            for j in range(4):
                tr(ps[:, b, j * 16:(j + 1) * 16], otok[:, (c4 * 4 + j) * 128:(c4 * 4 + j + 1) * 128], ident[0:16, 0:16], ["otok", "cst"], [("ps", b)])
            for j in range(4):
                cp(bB[:, c4 * 4 + j, 0:NS], ps[:, b, j * 16:(j + 1) * 16], [("ps", b)], [("bB", c4 * 4 + j, 0)])
        out_proj(lambda c0, n: IN("xa_w_o")[layer, :, c0:c0 + n], bB_rhs, bB_keys, KT)
        barrier()

    def ffn(layer):
        norm_h(PV_NFFN + 16 * layer)
        gs = [arb(PX, NCOL // 2), arb(PX + 600, NCOL // 2)]
        f0 = 0
        while f0 < 44:
            nf = min(8, 44 - f0)
            for j0 in range(0, nf, 2):
                cg = (f0 + j0) * 128
                sg, vg, _ = load_w([IN("ffn_w_gate")[layer, :, cg:cg + 256]], KT)
                su, vu, _ = load_w([IN("ffn_w_up")[layer, :, cg:cg + 256]], KT)
                for t in range(2):
                    jj = j0 + t
                    gbuf = gs[jj % 2]; gk = ("gs", jj % 2)
                    proj_fm(vg, sg, t * 128, KT, xn_rhs, xn_keys, EB,
                            lambda bi, c0, n, p, pk, gbuf=gbuf, gk=gk: act(gbuf[:, c0:c0 + n], p, AF.Silu, [pk], [gk]))
                    proj_fm(vu, su, t * 128, KT, xn_rhs, xn_keys, EB,
                            lambda bi, c0, n, p, pk, gbuf=gbuf, gk=gk, jj=jj: tt(bB[:, jj, c0:c0 + n], gbuf[:, c0:c0 + n], p, ALU.mult, [gk, pk], [("bB", jj, bi)]))
            out_proj(lambda c0, n, f0=f0, nf=nf: IN("ffn_w_down")[layer, f0 * 128:(f0 + nf) * 128, c0:c0 + n], bB_rhs, bB_keys, nf)
            f0 += nf

    def mixer_ssd():
        barrier()
        norm_h(PV_NMIX + 16)
        for kt in range(KT):
            ts(xn[:, kt, H0:H0 + NH], xn[:, kt, H0:H0 + NH], flag, None, ALU.mult, None, [("xn", kt, 0), "cst"], [("xn", kt, 0)])
        xT = bB[:, 0:4, :]; BT = bB[:, 4, :]; CT = bB[:, 5, :]
        xtok = bBf[:, 6 * NCOL:6 * NCOL + 4096].rearrange("p (c n) -> p c n", c=8)
        Btok = bB[:, 10, 0:1024].rearrange("p (c n) -> p c n", c=8)
        EW = [bB[:, 11 + i, 0:1024].rearrange("p (h t) -> p h t", h=8) for i in range(3)]
        xdtb = [bB[:, 15, 0:512].rearrange("p (h d) -> p h d", h=8), bB[:, 15, 512:1024].rearrange("p (h d) -> p h d", h=8),
                bB[:, 14, 0:512].rearrange("p (h d) -> p h d", h=8)]
        o = PX
        raw = arf(o, NCOL); o += NCOL
        R = arf(o, 1024).rearrange("p (h t) -> p h t", h=8); o += 1024
        Sst = arf(o, 512); o += 512
        Sbf = arb(o, 256); o += 256
        tmpA = arf(o, 512); o += 512
        xdte = arb(o, 256).rearrange("p (h d) -> p h d", h=8); o += 256
        CBTb = [arb(o, 64), arb(o + 64, 64), bB[:, 14, 512:640]]; o += 128
        xs_conv = arb(o, 256).rearrange("p (j b) -> p j b", j=32); o += 256
        y_s = arf(o, 512).rearrange("p (j b) -> p j b", j=32); o += 512
        assert o <= ARW, o
        o = 1024
        tstat = arf(o, 16); o += 16
        cs_sb = [arf(o, 32), arf(o + 32, 32), arf(o + 64, 32)]; o += 96
        ncs = [arf(o, 8), arf(o + 8, 8), arf(o + 16, 8)]; o += 24
        dq = arf(o, 132); o += 132
        xdts = arf(o, 64).rearrange("p (b j) -> p b j", b=NS); o += 64
        dAs = arf(o, 16); o += 16
        bcs = arb(o, 128); o += 128
        assert o <= 1536, o
        o = 1536
        dt_tok = arf(o, 512).rearrange("p (c h) -> p c h", c=8); o += 512
        dta_tok = arf(o, 512).rearrange("p (c h) -> p c h", c=8); o += 512
        gg = arf(o, 512); o += 512
        gnb = [arb(o, 256), arb(o + 256, 256)]; o += 512
        histg = arf(o, 288).rearrange("p (t r) -> p t r", t=6); o += 288
        cvo_s = arf(o, 288).rearrange("p (t r) -> p t r", t=6); o += 288
        cvo_p = arf(o, 18).rearrange("p (t r) -> p t r", t=6); o += 18
        BsT = arb(o, 64).rearrange("p (g b) -> p g b", g=8); o += 64
        CsT = arb(o, 64).rearrange("p (g b) -> p g b", g=8); o += 64
        rallb = arf(o, 132); o += 132
        rg = arf(o, 132); o += 132
        stS = [arf(o, 512).rearrange("p (j n) -> p j n", j=4), arf(o + 512, 512).rearrange("p (j n) -> p j n", j=4)]
        cstg = arf(o, 512)
        o += 1024
        assert o <= 5632, o
        smid = arf(PX + NCOL, 512)
        t1 = tmpA.rearrange("p (j n) -> p j n", j=4)
        def fence():
            vop(lambda e: e.memset(tstat[:, 0:1], 0.0), [], [("stg", 0), ("stg", 1), "kvst0", "kvst1", "fence"])
            barrier()

        fence()

        dtT = ar[0:64, 3584 + 200:3584 + 200 + NCOL] if False else raw[0:64, :]
        sdt, vdt, _ = load_w([IN("ssd_w_in")[:, 10240:10304]], KT)
        for bi, (c0, n) in EB:
            b = bank()
            for k in range(KT):
                mm(ps[0:64, b, 0:n], vdt[:, k, 0:64], xn[:, k, c0:c0 + n], k == 0, k == KT - 1, [("w", sdt), ("xn", k, bi)], [("ps", b)])
            act(dtT[:, c0:c0 + n], ps[0:64, b, 0:n], AF.Exp, [("ps", b), "hvec"], ["dtT"], bias=hvec[:, 0:1])
        ts(dtT, dtT, 1.0, None, ALU.add, None, ["dtT"], ["dtT"])
        act(dtT, dtT, AF.Ln, ["dtT"], ["dtT"])
        for c4 in range(2):
            b = bank()
            for j in range(4):
                c = c4 * 4 + j
                tr(ps[:, b, j * 64:(j + 1) * 64], dtT[:, P0 + c * 128:P0 + (c + 1) * 128], ident[0:64, 0:64], ["dtT", "cst"], [("ps", b)])
            cp(ar[:, 1536 + c4 * 256:1536 + (c4 + 1) * 256], ps[:, b, 0:256], [("ps", b)], ["dt_tok"])
        tt(dta_tok, dt_tok, a_bc.unsqueeze(1).to_broadcast([128, 8, 64]), ALU.mult, ["dt_tok", "tokbc"], ["dta_tok"])
        M2 = lg[:, 128:132]
        dts = dtT[:, 0:NS]
        act(dAs[0:64, :], dts, AF.Exp, ["dtT", "hvec_a"], ["dAs"], scale=hvec[:, 3:4])
        tt(rallb[0:64, 0:64].rearrange("p (b j) -> p b j", b=NS), dts.unsqueeze(2).to_broadcast([64, NS, 4]), M2.unsqueeze(1).to_broadcast([64, NS, 4]), ALU.mult, ["dtT", "lg"], ["rall"])
        tt(rallb[0:64, 64:128].rearrange("p (b j) -> p b j", b=NS), dAs[0:64, :].unsqueeze(2).to_broadcast([64, NS, 4]), M2.unsqueeze(1).to_broadcast([64, NS, 4]), ALU.mult, ["dAs", "lg"], ["rall"])
        ts(rallb[0:64, 128:132], M2, hvec[:, 2:3], None, ALU.mult, None, ["lg", "hvec"], ["rall"])
        barrier()

        XCH0 = 0; BCH0 = 4096; CCH0 = 5120

        def do_conv(dst_tile, ct, hist_t, full, ck_):
            w = [pvec[:, PV_CW + 48 * r + ct:PV_CW + 48 * r + ct + 1] for r in range(4)]
            bia = pvec[:, PV_CB + ct:PV_CB + ct + 1]
            Rf = arf(PX + NCOL, 1024)
            for (c0, n) in ((19, 517), (536, 512)):
                a = Rf[:, 0:n]
                ts(a, raw[:, c0 - 3:c0 - 3 + n], w[0], None, ALU.mult, None, ["raw", "pvec"], ["R"])
                stt(a, raw[:, c0 - 2:c0 - 2 + n], w[1], a, ALU.mult, ALU.add, ["raw", "R"], ["R"])
                stt(a, raw[:, c0 - 1:c0 - 1 + n], w[2], a, ALU.mult, ALU.add, ["raw", "R"], ["R"])
                stt(a, raw[:, c0:c0 + n], w[3], a, ALU.mult, ALU.add, ["raw", "R"], ["R"])
                act(dst_tile[:, c0:c0 + n], a, AF.Silu, ["R", "pvec"], [ck_], bias=bia)
            hv = histg[:, hist_t, :].rearrange("p (b r) -> p b r", r=3)
            a = Rf[:, 0:NS]
            ts(a, hv[:, :, 0], w[0], None, ALU.mult, None, ["histg", "pvec"], ["R"])
            stt(a, hv[:, :, 1], w[1], a, ALU.mult, ALU.add, ["histg", "R"], ["R"])
            stt(a, hv[:, :, 2], w[2], a, ALU.mult, ALU.add, ["histg", "R"], ["R"])
            stt(a, raw[:, 0:NS], w[3], a, ALU.mult, ALU.add, ["raw", "R"], ["R"])
            act(dst_tile[:, 0:NS], a, AF.Silu, ["R", "pvec"], [ck_], bias=bia)
            if full:
                cp(cvo_p[:, hist_t, :], raw[:, NCOL - 3:NCOL], ["raw"], ["cvo_p"])
                ov = cvo_s[:, hist_t, :].rearrange("p (b r) -> p b r", r=3)
                cp(ov[:, :, 0:2], hv[:, :, 1:3], ["histg"], ["cvo_s"])
                cp(ov[:, :, 2], raw[:, 0:NS], ["raw"], ["cvo_s"])

        def load_hist(g):
            for (c0, ncol, t0) in ((XCH0 + 512 * g, 512, 0), (BCH0 + 128 * g, 128, 4), (CCH0 + 128 * g, 128, 5)):
                dma("sp", cstg[0:48, 0:ncol], IN("ccv")[:, c0:c0 + ncol], [], ["cstg"])
                b = bank()
                nct = ncol // 128
                for j in range(nct):
                    tr(ps[:, b, j * 48:(j + 1) * 48], cstg[0:48, j * 128:(j + 1) * 128], ident[0:48, 0:48], ["cstg", "cst"], [("ps", b)])
                cp(histg[:, t0:t0 + nct, :], ps[:, b, 0:nct * 48].rearrange("p (t r) -> p t r", t=nct), [("ps", b)], ["histg"])

        def store_conv(g):
            for (c0, ncol, t0) in ((XCH0 + 512 * g, 512, 0), (BCH0 + 128 * g, 128, 4), (CCH0 + 128 * g, 128, 5)):
                nct = ncol // 128
                b = bank()
                for j in range(nct):
                    tr(ps[0:48, b, j * 128:(j + 1) * 128], cvo_s[:, t0 + j, :], ident, ["cvo_s", "cst"], [("ps", b)])
                cp(cstg[0:48, 0:ncol], ps[0:48, b, 0:ncol], [("ps", b)], ["cstg"])
                ok = ("o_cvs", g, t0); outkeys.append(ok)
                dma("sp", cvs_o[:, c0:c0 + ncol], cstg[0:48, 0:ncol], ["cstg"], [ok])
                b = bank()
                for j in range(nct):
                    tr(ps[0:3, b, j * 128:(j + 1) * 128], cvo_p[:, t0 + j, :], ident, ["cvo_p", "cst"], [("ps", b)])
                cp(cstg[0:3, 0:ncol], ps[0:3, b, 0:ncol], [("ps", b)], ["cstg"])
                ok = ("o_cvp", g, t0); outkeys.append(ok)
                dma("sp", cvp_o[:, c0:c0 + ncol], cstg[0:3, 0:ncol], ["cstg"], [ok])

        def inproj_conv(g, full):
            load_hist(g)
            specs = [(4096 + 512 * g + 0, 2), (4096 + 512 * g + 256, 2), (8192 + 128 * g, 1)]
            specs.append((9216 + 128 * g, 1))
            tix = 0
            for (wc0, ntl) in specs:
                si, view, _ = load_w([IN("ssd_w_in")[:, wc0:wc0 + ntl * 128]], KT)
                for t in range(ntl):
                    proj_fm(view, si, t * 128, KT, xn_rhs, xn_keys, EB,
                            lambda bi, c0, n, p, pk: cp_alt(raw[:, c0:c0 + n], p, [pk], ["raw"]))
                    ch = wc0 - 4096 + t * 128
                    if tix < 4:
                        dst = xT[:, tix, :]; ck_ = "xTk"
                    elif tix == 4:
                        dst = BT; ck_ = "BTk"
                    else:
                        dst = CT; ck_ = "CTk"
                    do_conv(dst, ch // 128, tix, full, ck_)
                    tix += 1
            for c in range(8):
                pb = pbank()
                for j in range(4):
                    tr(psb[:, pb, j * 128:(j + 1) * 128], xT[:, j, P0 + c * 128:P0 + (c + 1) * 128], identb[:], ["xTk", "identb"], [("psb", pb)])
                cp_alt(xtok[:, c, :], psb[:, pb, 0:512], [("psb", pb)], ["xtok"])
            for c4 in range(2):
                pb = pbank()
                for j in range(4):
                    c = c4 * 4 + j
                    tr(psb[:, pb, j * 128:(j + 1) * 128], BT[:, P0 + c * 128:P0 + (c + 1) * 128], identb[:], ["BTk", "identb"], [("psb", pb)])
                cp_alt(Btok[:, c4 * 4:(c4 + 1) * 4, :], psb[:, pb, 0:512].rearrange("p (c n) -> p c n", c=4), [("psb", pb)], ["Btok"])
            cp(xs_conv[:, 4 * g:4 * g + 4, :], xT[:, :, 0:NS], ["xTk"], ["xs_conv"])
            cp(BsT[:, g, :], BT[:, 0:NS], ["BTk"], ["BsT"])
            cp(CsT[:, g, :], CT[:, 0:NS], ["CTk"], ["CsT"])

        def stageA(g, c, full):
            i = c % 3
            hs = slice(8 * g, 8 * g + 8)
            b = bank()
            mm(ps[:, b, 0:8], tri, dta_tok[:, c, hs], True, True, ["cst", "dta_tok"], [("ps", b)])
            mm(ps[:, b, 8:16], ones, dta_tok[:, c, hs], True, True, ["cst", "dta_tok"], [("ps", b)])
            if full:
                mm(ps[:, b, 128:256], BT[:, P0 + c * 128:P0 + (c + 1) * 128], CT[:, P0 + c * 128:P0 + (c + 1) * 128], True, True, ["BTk", "CTk"], [("ps", b)])
            cp(cs_sb[i][:, 0:16], ps[:, b, 0:16], [("ps", b)], [("cs", i)], eng="act")
            if full:
                cp(CBTb[i], ps[:, b, 128:256], [("ps", b)], [("CBT", i)], eng="act")
            tt(cs_sb[i][:, 16:24], cs_sb[i][:, 8:16], cs_sb[i][:, 0:8], ALU.subtract, [("cs", i)], [("cs", i)])
            act(cs_sb[i][:, 16:24], cs_sb[i][:, 16:24], AF.Exp, [("cs", i)], [("cs", i)])
            act(cs_sb[i][:, 24:32], cs_sb[i][:, 8:16], AF.Exp, [("cs", i)], [("cs", i)])
            tt(xdtb[i], xtok[:, c, :].rearrange("p (h d) -> p h d", h=8), dt_tok[:, c, hs].unsqueeze(2).to_broadcast([128, 8, 64]), ALU.mult,
               ["xtok", "dt_tok"], [("xdt", i)])
            if full:
                ts(ncs[i], cs_sb[i][:, 0:8], -1.0, None, ALU.mult, None, [("cs", i)], [("ncs", i)])
                tt(R, tri.unsqueeze(1).to_broadcast([128, 8, 128]), dta_tok[:, c, hs].unsqueeze(2).to_broadcast([128, 8, 128]), ALU.mult,
                   ["cst", "dta_tok"], ["R"])
                pbk = [bank(), bank()]
                Rf = arf(PX + NCOL, 1024)
                for hh in range(2):
                    mm(ps[:, pbk[hh], :], ones, Rf[:, hh * 512:(hh + 1) * 512], True, False, ["cst", "R"], [("ps", pbk[hh])])
                    mm(ps[:, pbk[hh], :], ident, mneg4, False, True, ["cst"], [("ps", pbk[hh])])
                for h in range(8):
                    act(EW[i][:, h, :], ps[:, pbk[h // 4], (h % 4) * 128:(h % 4 + 1) * 128], AF.Exp, [("ps", pbk[h // 4]), ("ncs", i)], [("EW", i)], bias=ncs[i][:, h:h + 1])

        def stageW(c):
            i = c % 3
            tt(EW[i], EW[i], CBTb[i].unsqueeze(1).to_broadcast([128, 8, 128]), ALU.mult, [("EW", i), ("CBT", i)], [("EW", i)])

        def stageT(c):
            k2 = c % 2
            pb = pbank()
            for j in range(4):
                tr(psb[:, pb, j * 128:(j + 1) * 128], gnb[k2][:, j * 128:(j + 1) * 128], identb[:], [("gn", k2), "identb"], [("psb", pb)])
            for j in range(4):
                cp(xT[:, j, P0 + c * 128:P0 + (c + 1) * 128], psb[:, pb, j * 128:(j + 1) * 128], [("psb", pb)], ["xTk"], eng="act")

        def stageB(g, c, full):
            i = c % 3
            if full:
                by = bank()
                for h in range(8):
                    mm(ps[:, by, h * 64:(h + 1) * 64], EW[i][:, h, :], xdtb[i][:, h, :], True, True, [("EW", i), ("xdt", i)], [("ps", by)])
                bo = bank()
                mm(ps[:, bo, :], CT[:, P0 + c * 128:P0 + (c + 1) * 128], Sbf, True, True, ["CTk", "Sbf"], [("ps", bo)])
                act(tstat[:, 0:8], cs_sb[i][:, 0:8], AF.Exp, [("cs", i)], ["expcs"])
                tt(gg.rearrange("p (h d) -> p h d", h=8), ps[:, bo, :].rearrange("p (h d) -> p h d", h=8),
                   tstat[:, 0:8].unsqueeze(2).to_broadcast([128, 8, 64]), ALU.mult, [("ps", bo), "expcs"], ["gg"])
                tt(gg, gg, ps[:, by, :], ALU.add, ["gg", ("ps", by)], ["gg"])
                tt(tmpA.rearrange("p (h d) -> p h d", h=8), xtok[:, c, :].rearrange("p (h d) -> p h d", h=8),
                   d_bc[:, 8 * g:8 * g + 8].unsqueeze(2).to_broadcast([128, 8, 64]), ALU.mult, ["xtok", "tokbc"], ["tmpA"])
                tt(gnb[c % 2], gg, tmpA, ALU.add, ["gg", "tmpA"], [("gn", c % 2)])
            tt(xdte, xdtb[i], cs_sb[i][:, 16:24].unsqueeze(2).to_broadcast([128, 8, 64]), ALU.mult, [("xdt", i), ("cs", i)], ["xdte"])
            bs_ = bank()
            mm(ps[:, bs_, :], Btok[:, c, :], xdte.rearrange("p h d -> p (h d)"), True, True, ["Btok", "xdte"], [("ps", bs_)])
            tt(Sst.rearrange("p (h d) -> p h d", h=8), Sst.rearrange("p (h d) -> p h d", h=8),
               cs_sb[i][:, 24:32].unsqueeze(2).to_broadcast([128, 8, 64]), ALU.mult, ["Sst", ("cs", i)], ["Sst"])
            tt(Sst, Sst, ps[:, bs_, :], ALU.add, ["Sst", ("ps", bs_)], ["Sst"])
            cp(Sbf, Sst, ["Sst"], ["Sbf"], eng="act")

        def scan(g, full):
            if full:
                dma("sp", smid, cc_dst[128 * g:128 * (g + 1), :], ["cc_dst"], ["smid", "R"])
                ts(Sst, smid, flag, None, ALU.mult, None, ["smid", "R", "cst"], ["Sst"])
            else:
                vop(lambda e: e.memset(Sst, 0.0), [], ["Sst"])
            cp(Sbf, Sst, ["Sst"], ["Sbf"], eng="act")
            if full:
                stageA(g, 0, True); stageA(g, 1, True); stageW(0)
                for c in range(8):
                    if c + 2 < 8:
                        stageA(g, c + 2, True)
                    stageB(g, c, True)
                    if c + 1 < 8:
                        stageW(c + 1)
                    if c >= 1:
                        stageT(c - 1)
                stageT(7)
            else:
                stageA(g, 0, False)
                for c in range(8):
                    if c + 1 < 8:
                        stageA(g, c + 1, False)
                    stageB(g, c, False)

        L64 = lg[:, 0:128]

        def sample_group(g):
            ts(rg[0:64, :], rallb[0:64, :], lg[:, 132 + g:133 + g], None, ALU.mult, None, ["rall", "lg"], ["rg"])
            b = bank()
            mm(ps[:, b, 0:132], L64, rg[0:64, :], True, True, ["lg", "rg"], [("ps", b)])
            cp(dq, ps[:, b, 0:132], [("ps", b)], ["dq"])
            tt(xdts, xs_conv[:, 4 * g:4 * g + 4, :].rearrange("p j b -> p b j"), dq[:, 0:64].rearrange("p (b j) -> p b j", b=NS), ALU.mult,
               ["xs_conv", "dq"], ["xdts"])
            pb = pbank()
            tr(psb[0:16, pb, 0:128], BsT[:, g, :], identb[:], ["BsT", "identb"], [("psb", pb)])
            tr(psb[0:16, pb, 128:256], CsT[:, g, :], identb[:], ["CsT", "identb"], [("psb", pb)])
            cp(bcs[0:16, :], psb[0:16, pb, 0:256], [("psb", pb)], ["bcs"])
            for bsm in range(NS):
                st = stS[bsm % 2]; sk = "cstg" if bsm % 2 == 0 else ("stS", 1)
                dma("sp", st, IN("sst")[bsm, 512 * g:512 * (g + 1), :].rearrange("(j q) n -> q j n", q=128), [], [sk])
                b = bank()
                mm(ps[:, b, 0:256], selb[:, bsm * 128:(bsm + 1) * 128], bcs[0:16, :], True, True, ["selb", "bcs"], [("ps", b)])
                tt(t1, xdts[:, bsm, :].unsqueeze(2).to_broadcast([128, 4, 128]), ps[:, b, 0:128].unsqueeze(1).to_broadcast([128, 4, 128]), ALU.mult,
                   ["xdts", ("ps", b)], ["t1"])
                tt(st, st, dq[:, 64 + 4 * bsm:64 + 4 * bsm + 4].unsqueeze(2).to_broadcast([128, 4, 128]), ALU.mult, [sk, "dq"], [sk])
                tt(st, st, t1, ALU.add, [sk, "t1"], [sk])
                ok = ("o_sss", g, bsm); outkeys.append(ok)
                dma("sp", sss_o[bsm, 512 * g:512 * (g + 1), :].rearrange("(j q) n -> q j n", q=128), st, [sk], [ok])
                tt(t1, st, ps[:, b, 128:256].unsqueeze(1).to_broadcast([128, 4, 128]), ALU.mult, [sk, ("ps", b)], ["t1"])
                vop(lambda e: e.tensor_reduce(out=tstat[:, 12:16], in_=t1, axis=AX.X, op=ALU.add), ["t1"], ["ysr"])
                tt(tstat[:, 4:8], xs_conv[:, 4 * g:4 * g + 4, bsm], dq[:, 128:132], ALU.mult, ["xs_conv", "dq"], ["ydx"])
                tt(y_s[:, 4 * g:4 * g + 4, bsm], tstat[:, 12:16], tstat[:, 4:8], ALU.add, ["ysr", "ydx"], ["y_s"])

        for g in range(8):
            inproj_conv(g, False)
            if g > 0:
                sample_group(g - 1)
            scan(g, False)
            dma("sp", cc_src[128 * g:128 * (g + 1), :], Sst, ["Sst"], [("cc_src", g)])
        S.op("pool", lambda e: e.collective_compute("AllGather", ALU.bypass, replica_groups=[[0, 1], [2, 3], [4, 5], [6, 7]],
                                                     ins=[cc_src.opt()], outs=[cc_dst.opt()]),
             [("cc_src", g) for g in range(8)], ["cc_dst"])
        sample_group(7)
        barrier()

        for g in range(8):
            inproj_conv(g, True)
            store_conv(g)
            scan(g, True)
            b = bank()
            for j in range(4):
                tr(ps[:, b, j * 128:(j + 1) * 128], Sst[:, j * 128:(j + 1) * 128], ident, ["Sst", "cst"], [("ps", b)])
            cp(tmpA, ps[:, b, :], [("ps", b)], ["tmpA"])
            ok = ("o_ssp", g); outkeys.append(ok)
            dma("sp", ssp_o[512 * g:512 * (g + 1), :].rearrange("(j q) n -> q j n", q=128), tmpA.rearrange("p (j n) -> p j n", j=4), ["tmpA"], [ok])
            cp(xT[:, :, 0:NS], y_s[:, 4 * g:4 * g + 4, :], ["y_s"], ["xTk"])
            for half in range(2):
                si, view, _ = load_w([IN("ssd_w_in")[:, 512 * g + 256 * half:512 * g + 256 * (half + 1)]], KT)
                for t in range(2):
                    j = half * 2 + t
                    def ev(bi, c0, n, p, pk, j=j):
                        act(gg[:, 0:n], p, AF.Silu, [pk], ["gg"])
                        tt(xT[:, j, c0:c0 + n], xT[:, j, c0:c0 + n], gg[:, 0:n], ALU.mult, ["xTk", "gg"], [("gT", j, bi)])
                    proj_fm(view, si, t * 128, KT, xn_rhs, xn_keys, EB, ev)
            rmsnorm_cols(lambda kt, c0, n: xT[:, kt, c0:c0 + n], lambda kt, bi: ("gT", kt, bi),
                         lambda kt, c0, n: xT[:, kt, c0:c0 + n], lambda kt, bi: ("gT", kt, bi),
                         PV_SNORM + 4 * g, 4, EB, 512.0)
            out_proj(lambda c0, n, g=g: IN("ssd_w_out")[512 * g:512 * (g + 1), c0:c0 + n],
                     lambda k, c0, n: xT[:, k, c0:c0 + n], lambda k, bi: [("gT", k, bi), "xTk"], 4)
            vop(lambda e: e.memset(tstat[:, 1:2], 0.0), [("gT", k, bi) for k in range(4) for bi in range(3)], ["xTk"])
        barrier()

    import os
    kstop = int(os.environ.get("KSTOP", "7"))

    def ssd_phase():
        mixer_ssd()
        S.op("dve", lambda e: e.memset(sm[:, 508:509], 0.0), [], [("stg", 0), ("stg", 1), "fence2"])
        barrier()

    phases = [mixer_sc, lambda: xattn(0, True), lambda: ffn(0), ssd_phase, lambda: xattn(1, False), lambda: ffn(1)]
    for ph in phases[:kstop]:
        ph()
    if kstop >= 7:
        rmsnorm_cols(lambda kt, c0, n: hT[:, kt, c0:c0 + n], hkeys, lambda kt, c0, n: hT[:, kt, c0:c0 + n], hkeys, PV_NFIN, KT, EB, float(D))
    if KDBG >= 5:
        barrier()
    if KDBG >= 6:
        out_transpose(lambda ct: hT[:, ct, S0:S0 + NS], all_h_keys, NS, D, ys_o, "o_ys")
    for i in range(8 if KDBG >= 7 else 0):
        out_transpose((lambda ct, i=i: hT[:, ct, P0 + i * 128:P0 + (i + 1) * 128]), all_h_keys, 128, D, yp_o[i * 128:(i + 1) * 128, :], ("o_yp", i))
    S.op("sp", lambda e: None, outkeys, [], real=False)

    sems = {e: es.enter_context(nc.semaphore("sem_" + e)) for e in Sched.ENGS}
    dsems = {q: [es.enter_context(nc.semaphore(f"dsem_{q}{i}")) for i in range(S.NDS)] for q in ("pool", "sp")}
    block = es.enter_context(nc.Block())
    S.replay(nc, block, sems, dsems)
    es.close()
    nc._used_inputs = sorted(_in_aps.keys())
    return nc


def _fm(v, nt):
    return np.ascontiguousarray(np.asarray(v, np.float32).reshape(nt, 128).T)


def _consts():
    cst = np.zeros((128, 1160), np.float32)
    cst[:, 0:128] = np.eye(128, dtype=np.float32)
    s = np.arange(128)
    tri = (s[:, None] <= s[None, :]).astype(np.float32)
    cst[:, 128:256] = tri
    cst[:, 256:384] = 1.0
    cst[:, 384:896] = np.tile((1.0 - tri) * -30000.0, (1, 4))
    cst[:, 896:1152] = np.tile(np.eye(16, dtype=np.float32).reshape(1, 256), (128, 1))
    sel = np.zeros((16, 16, 128), np.float32)
    for b in range(16):
        sel[b, b, :] = 1.0
    lg = np.zeros((64, 140), np.float32)
    k = np.arange(64)
    q = np.arange(128)
    lg[:, 0:128] = ((k[:, None] % 2) == (q[None, :] // 64)).astype(np.float32)
    lg[:, 128:132] = (((k[:, None] % 8) // 2) == np.arange(4)[None, :]).astype(np.float32)
    lg[:, 132:140] = ((k[:, None] // 8) == np.arange(8)[None, :]).astype(np.float32)
    return cst, sel.reshape(16, 2048), lg


_PROG = {}


def kernel(x_prompt, x_sample, mem_prompt, cache_sc, state_ssd_conv, state_ssd, cache_mem_k, cache_mem_v,
           norm_mix, norm_mem_q, norm_mem_kv, norm_ffn, norm_final,
           sc_w_in, sc_w_conv, sc_w_out,
           ssd_w_in, ssd_conv_w, ssd_conv_b, ssd_dt_bias, ssd_a_log, ssd_d, ssd_norm, ssd_w_out,
           xa_w_q, xa_w_k, xa_w_v, xa_w_o, ffn_w_gate, ffn_w_up, ffn_w_down):
    f32 = np.float32
    A = lambda a: np.ascontiguousarray(np.asarray(a, f32))
    x_prompt = A(x_prompt); x_sample = A(x_sample); mem_prompt = A(mem_prompt)
    cache_sc = A(cache_sc); state_ssd_conv = A(state_ssd_conv); state_ssd = A(state_ssd)
    cache_mem_k = A(cache_mem_k); cache_mem_v = A(cache_mem_v)

    pvec = np.zeros((128, 512), f32)
    for l in range(2):
        pvec[:, PV_NMIX + 16 * l:PV_NMIX + 16 * (l + 1)] = _fm(norm_mix[l], 16)
        pvec[:, PV_NQ + 16 * l:PV_NQ + 16 * (l + 1)] = _fm(norm_mem_q[l], 16)
        pvec[:, PV_NKV + 16 * l:PV_NKV + 16 * (l + 1)] = _fm(norm_mem_kv[l], 16)
        pvec[:, PV_NFFN + 16 * l:PV_NFFN + 16 * (l + 1)] = _fm(norm_ffn[l], 16)
    pvec[:, PV_NFIN:PV_NFIN + 16] = _fm(norm_final, 16)
    for r in range(3):
        pvec[:, PV_SCW + 16 * r:PV_SCW + 16 * (r + 1)] = _fm(np.asarray(sc_w_conv)[0, r], 16)
    for r in range(4):
        pvec[:, PV_CW + 48 * r:PV_CW + 48 * (r + 1)] = _fm(np.asarray(ssd_conv_w)[0, r], 48)
    pvec[:, PV_CB:PV_CB + 48] = _fm(np.asarray(ssd_conv_b)[0], 48)
    pvec[:, PV_SNORM:PV_SNORM + 32] = _fm(np.asarray(ssd_norm)[0], 32)
    tokbc = np.zeros((128, 192), f32)
    tokbc[:, 0:64] = np.asarray(ssd_dt_bias, f32)[0][None, :]
    tokbc[:, 64:128] = np.asarray(ssd_a_log, f32)[0][None, :]
    tokbc[:, 128:192] = np.asarray(ssd_d, f32)[0][None, :]
    hvec = np.zeros((64, 8), f32)
    hvec[:, 0] = np.asarray(ssd_dt_bias, f32)[0]; hvec[:, 1] = np.asarray(ssd_a_log, f32)[0]; hvec[:, 2] = np.asarray(ssd_d, f32)[0]
    cst0, sel, lg = _consts()

    shared = {
        "pvec": pvec, "tokbc": tokbc, "hvec": hvec, "selb": sel, "lg": lg,
        "sc_w_in": A(sc_w_in)[0], "sc_w_out": A(sc_w_out)[0], "ssd_w_in": A(ssd_w_in)[0], "ssd_w_out": A(ssd_w_out)[0],
        "xa_w_q": A(xa_w_q), "xa_w_k": A(xa_w_k), "xa_w_v": A(xa_w_v), "xa_w_o": A(xa_w_o),
        "ffn_w_gate": A(ffn_w_gate), "ffn_w_up": A(ffn_w_up), "ffn_w_down": A(ffn_w_down),
    }
    in_maps = []
    for c in range(NCORES):
        seq, half = c // 2, c % 2
        st = half * NP
        cst = cst0.copy()
        cst[:, 1152] = float(half)
        xh = x_prompt[seq, st - NH:st] if half else np.zeros((NH, D), f32)
        sl = slice(NS * c, NS * (c + 1))
        m = dict(shared)
        m.update({
            "xp": np.ascontiguousarray(x_prompt[seq, st:st + NP]), "xh": np.ascontiguousarray(xh),
            "xs": np.ascontiguousarray(x_sample[sl, 0]), "mem": np.ascontiguousarray(mem_prompt[seq]),
            "csc": np.ascontiguousarray(cache_sc[0, sl].reshape(2 * NS, D)),
            "ccv": np.ascontiguousarray(state_ssd_conv[0, sl].reshape(3 * NS, CONV)),
            "sst": np.ascontiguousarray(state_ssd[0, sl].reshape(NS, 4096, 128)),
            "ck": np.ascontiguousarray(cache_mem_k[:, sl].reshape(2, NS, 256, D)),
            "cv": np.ascontiguousarray(cache_mem_v[:, sl].reshape(2, NS, 256, D)),
            "cst": cst,
        })
        in_maps.append(m)

    if "nc" not in _PROG:
        _PROG["nc"] = build_program()
    used = _PROG["nc"]._used_inputs
    in_maps = [{k: m[k] for k in used} for m in in_maps]
    res = run_bass_kernel_spmd(_PROG["nc"], in_maps, core_ids=list(range(NCORES)))
    R = res.results

    y_prompt = np.zeros((4, 2048, D), f32); y_sample = np.zeros((128, 1, D), f32)
    sc_p = np.zeros((1, 4, 2, D), f32); sc_s = np.zeros((1, 128, 2, D), f32)
    cv_p = np.zeros((1, 4, 3, CONV), f32); cv_s = np.zeros((1, 128, 3, CONV), f32)
    ss_p = np.zeros((1, 4, 64, 64, 128), f32); ss_s = np.zeros((1, 128, 64, 64, 128), f32)
    mk = np.zeros((2, 4, 256, 4, 512), f32); mv = np.zeros((2, 4, 256, 4, 512), f32)
    for c in range(NCORES):
        seq, half = c // 2, c % 2
        sl = slice(NS * c, NS * (c + 1))
        r = R[c]
        y_prompt[seq, half * NP:(half + 1) * NP] = r["y_p"]
        y_sample[sl, 0] = r["y_s"]
        sc_s[0, sl] = r["sc_s"].reshape(NS, 2, D)
        cv_s[0, sl] = r["cv_s"].reshape(NS, 3, CONV)
        ss_s[0, sl] = r["ss_s"].reshape(NS, 64, 64, 128)
        if half == 1:
            sc_p[0, seq] = r["sc_p"]
            cv_p[0, seq] = r["cv_p"]
            ss_p[0, seq] = r["ss_p"].reshape(64, 64, 128)
        else:
            mk[:, seq] = r["mk"].reshape(2, 256, 4, 512)
            mv[:, seq] = r["mv"].reshape(2, 256, 4, 512)
    return (y_prompt, y_sample, sc_p, sc_s, cv_p, cv_s, ss_p, ss_s, mk, mv)
```

```python
import numpy as np
import concourse.bass as bass
import concourse.mybir as mybir
from concourse.bass_utils import run_bass_kernel_spmd

F32 = mybir.dt.float32
BF16 = mybir.dt.bfloat16
AF = mybir.ActivationFunctionType
ALU = mybir.AluOpType
AX = mybir.AxisListType

D = 2048
KT = 16
NS, NH, NP = 16, 8, 1024
S0, H0, P0 = 0, 16, 24
NCOL = NS + NH + NP
BLKS = [(0, 24), (24, 512), (536, 512)]
DFF = 5632
DIN = 4096
CONV = 6144
EPS = 1e-5
NCORES = 8


class Op:
    __slots__ = ("eng", "idx", "fn", "waits", "signal", "dma", "rank", "real")

    def __init__(self, eng, idx, fn, dma):
        self.eng, self.idx, self.fn, self.dma = eng, idx, fn, dma
        self.waits = []
        self.signal = False
        self.rank = None
        self.real = True


class Sched:
    ENGS = ("pe", "act", "dve", "pool", "sp")

    def __init__(self):
        self.q = {e: [] for e in self.ENGS}
        self.lastw = {}
        self.readers = {}
        self.seen = {e: {} for e in self.ENGS}
        self.ndma = {"pool": 0, "sp": 0}
        self.NDS = 6

    def op(self, eng, fn, reads=(), writes=(), dma=False, real=True):
        o = Op(eng, len(self.q[eng]), fn, None)
        o.real = real
        deps = []
        for k in reads:
            w = self.lastw.get(k)
            if w is not None:
                deps.append((w, True))
        for k in writes:
            w = self.lastw.get(k)
            if w is not None:
                deps.append((w, False))
            for r in self.readers.get(k, ()):
                deps.append((r, False))
        if dma:
            i = self.ndma[eng]
            self.ndma[eng] += 1
            o.dma = (eng, i % self.NDS, 16 * (i // self.NDS + 1))
            if i >= self.NDS:
                o.waits.append(("dma", eng, i % self.NDS, 16 * (i // self.NDS)))
        for d, raw in deps:
            if d is o:
                continue
            if d.dma is not None:
                key = ("dma", d.dma[0], d.dma[1])
                if self.seen[eng].get(key, 0) >= d.dma[2]:
                    continue
                self.seen[eng][key] = d.dma[2]
                o.waits.append(("dma", d.dma[0], d.dma[1], d.dma[2]))
            else:
                if d.eng == eng and (eng == "pe" or not raw):
                    continue
                if self.seen[eng].get(d.eng, -1) >= d.idx:
                    continue
                self.seen[eng][d.eng] = d.idx
                d.signal = True
                o.waits.append(("op", d))
        for k in reads:
            self.readers.setdefault(k, []).append(o)
        for k in writes:
            self.lastw[k] = o
            self.readers[k] = []
        self.q[eng].append(o)
        return o

    def replay(self, nc, block, sems, dsems):
        for e in self.ENGS:
            r = 0
            for o in self.q[e]:
                if o.signal:
                    r += 1
                    o.rank = r
        names = {"pe": "tensor", "act": "scalar", "dve": "vector", "pool": "gpsimd", "sp": "sync"}
        fin = Op("sp", len(self.q["sp"]), (lambda e: None), None)
        fin.real = False
        for qn, n in self.ndma.items():
            for si in range(min(n, self.NDS)):
                last_i = ((n - 1 - si) // self.NDS) * self.NDS + si
                fin.waits.append(("dma", qn, si, 16 * (last_i // self.NDS + 1)))
        self.q["sp"].append(fin)

        def make(ename):
            def body(eng):
                for o in self.q[ename]:
                    for w in o.waits:
                        if w[0] == "dma":
                            eng.wait_ge(dsems[w[1]][w[2]], w[3])
                        else:
                            eng.wait_ge(sems[w[1].eng], w[1].rank)
                    ins = o.fn(eng)
                    if ins is None:
                        continue
                    if o.dma is not None:
                        ins.then_inc(dsems[o.dma[0]][o.dma[1]], 16)
                    elif o.signal:
                        ins.then_inc(sems[ename], 1)
            return body

        for ename in self.ENGS:
            getattr(block, names[ename])(make(ename))


PV_NMIX, PV_NQ, PV_NKV, PV_NFFN, PV_NFIN = 0, 32, 64, 96, 128
PV_SCW, PV_CW, PV_CB, PV_SNORM = 144, 192, 384, 432
NSLOT = 3
SLOTW = 4096


def build_program():
    import os
    nc = bass.Bass("TRN2", target_bir_lowering=False)
    S = Sched()
    from contextlib import ExitStack
    es = ExitStack()

    def din(name, shape):
        return nc.dram_tensor(name, list(shape), F32, kind="ExternalInput").ap()

    def dout(name, shape):
        return nc.dram_tensor(name, list(shape), F32, kind="ExternalOutput").ap()

    _in_shapes = {
        "xp": [NP, D], "xh": [NH, D], "xs": [NS, D], "mem": [256, D], "csc": [2 * NS, D], "ccv": [3 * NS, CONV],
        "sst": [NS, 4096, 128], "ck": [2, NS, 256, D], "cv": [2, NS, 256, D],
        "pvec": [128, 512], "tokbc": [128, 192], "cst": [128, 1160], "hvec": [64, 8], "selb": [16, 2048], "lg": [64, 140],
        "sc_w_in": [D, 3 * D], "sc_w_out": [D, D], "ssd_w_in": [D, 10304], "ssd_w_out": [DIN, D],
        "xa_w_q": [2, D, D], "xa_w_k": [2, D, D], "xa_w_v": [2, D, D], "xa_w_o": [2, D, D],
        "ffn_w_gate": [2, D, DFF], "ffn_w_up": [2, D, DFF], "ffn_w_down": [2, DFF, D],
    }
    _in_aps = {}

    def IN(name):
        if name not in _in_aps:
            _in_aps[name] = din(name, _in_shapes[name])
        return _in_aps[name]

    yp_o = dout("y_p", [NP, D]); ys_o = dout("y_s", [NS, D]); scp_o = dout("sc_p", [2, D]); scs_o = dout("sc_s", [2 * NS, D])
    cvp_o = dout("cv_p", [3, CONV]); cvs_o = dout("cv_s", [3 * NS, CONV]); ssp_o = dout("ss_p", [4096, 128])
    sss_o = dout("ss_s", [NS, 4096, 128]); mk_o = dout("mk", [2, 256, D]); mv_o = dout("mv", [2, 256, D])

    cc_src = nc.dram_tensor("cc_src", [1024, 512], F32, kind="Internal").ap()
    cc_dst = nc.dram_tensor("cc_dst", [2048, 512], F32, kind="Internal").ap()

    def sb(name, shape, dt=F32):
        return es.enter_context(nc.sbuf_tensor("sb_" + name, list(shape), dt))

    hT = sb("hT", [128, KT, NCOL])
    xn = sb("xn", [128, KT, NCOL], BF16)
    bB = sb("bufB", [128, KT, NCOL], BF16)
    wsl = [sb(f"wslot{i}", [128, SLOTW], BF16) for i in range(NSLOT)]
    pvec = sb("pvec", [128, 512]); tokbc = sb("tokbc", [128, 192]); cst = sb("cst", [128, 1160])
    hvec = sb("hvec", [64, 8]); selb = sb("selb", [16, 2048], BF16); lg = sb("lg", [64, 140])
    identb = sb("identb", [128, 128], BF16); onesb = sb("onesb", [128, 128], BF16)
    ARW = int(os.environ.get("ARW", "10160")) if True else 0
    ar = sb("arena", [128, ARW])
    ps = es.enter_context(nc.psum_tensor("ps", [128, 6, 512], F32))
    psb = es.enter_context(nc.psum_tensor("psb", [128, 2, 1024], BF16))

    xnf = xn[:].rearrange("p k n -> p (k n)")
    bBf = bB[:].rearrange("p k n -> p (k n)")
    XN_ALL = [("xn", k, bi) for k in range(KT) for bi in range(3)]
    BB_ALL = [("bB", k, bi) for k in range(KT) for bi in range(3)]

    def arf(o, n):
        return ar[:, o:o + n]

    def arb(o, n):
        return ar[:, o:o + n].bitcast(BF16)

    sq = [arb(0, 256), arb(256, 256)]
    rstd = arf(512, 512)
    sm = arf(1024, 512)
    stg = [arf(1536, 2048), arf(3584, 2048)]
    PX = 5632
    PXN = ARW - PX

    ident = cst[:, 0:128]; tri = cst[:, 128:256]; ones = cst[:, 256:384]; mneg4 = cst[:, 384:896]
    d16 = cst[:, 896:1152].rearrange("p (b c) -> p b c", b=NS)
    flag = cst[:, 1152:1153]

    state = {"bank": 0, "pb": 0, "slot": 0, "stg": 0}
    outkeys = []

    def bank():
        b = state["bank"]; state["bank"] = (b + 1) % 6
        return b

    def pbank():
        b = state["pb"]; state["pb"] = (b + 1) % 2
        return b

    def mm(out, lhsT, rhs, start, stop, reads, writes):
        S.op("pe", lambda e: e.matmul(out, lhsT=lhsT, rhs=rhs, start=start, stop=stop), reads, writes)

    def tr(out, in_, idn, reads, writes):
        S.op("pe", lambda e: e.transpose(out, in_, idn), reads, writes)

    def act(out, in_, func, reads, writes, bias=None, scale=None, accum=None):
        kw = {}
        if bias is not None: kw["bias"] = bias
        if scale is not None: kw["scale"] = scale
        if accum is not None: kw["accum_out"] = accum
        S.op("act", lambda e: e.activation(out=out, in_=in_, func=func, **kw), reads, writes)

    def tt(out, a, b, op, reads, writes, eng="dve"):
        S.op(eng, lambda e: e.tensor_tensor(out=out, in0=a, in1=b, op=op), reads, writes)

    def ts(out, a, s1, s2, op0, op1, reads, writes, eng="dve"):
        if op1 is None:
            S.op(eng, lambda e: e.tensor_scalar(out=out, in0=a, scalar1=s1, scalar2=None, op0=op0), reads, writes)
        else:
            S.op(eng, lambda e: e.tensor_scalar(out=out, in0=a, scalar1=s1, scalar2=s2, op0=op0, op1=op1), reads, writes)

    def stt(out, a, s, b, op0, op1, reads, writes):
        S.op("dve", lambda e: e.scalar_tensor_tensor(out=out, in0=a, scalar=s, in1=b, op0=op0, op1=op1), reads, writes)

    def cp(out, in_, reads, writes, eng="dve"):
        if eng == "act":
            S.op("act", lambda e: e.copy(out=out, in_=in_), reads, writes)
        else:
            S.op(eng, lambda e: e.tensor_copy(out=out, in_=in_), reads, writes)

    def vop(fn, reads, writes):
        S.op("dve", fn, reads, writes)

    def dma(q, out, in_, reads, writes):
        S.op(q, lambda e: e.dma_start(out=out, in_=in_), reads, writes, dma=True)

    cpi = [0]

    def cp_alt(out, in_, reads, writes):
        cpi[0] += 1
        cp(out, in_, reads, writes, eng=("act" if cpi[0] % 2 else "dve"))

    def barrier():
        engs = ("pe", "act", "dve")
        lasts = {}
        for e in engs:
            for o_ in reversed(S.q[e]):
                if o_.real:
                    lasts[e] = o_
                    break
        for e in engs:
            o = S.op(e, lambda eng: None, real=False)
            for e2, l in lasts.items():
                if e2 != e and S.seen[e].get(e2, -1) < l.idx:
                    S.seen[e][e2] = l.idx
                    l.signal = True
                    o.waits.append(("op", l))

    dma("sp", pvec[:], IN("pvec"), [], ["pvec"]); dma("sp", tokbc[:], IN("tokbc"), [], ["tokbc"]); dma("sp", cst[:], IN("cst"), [], ["cst"])
    dma("sp", hvec[:], IN("hvec"), [], ["hvec"]); dma("sp", lg[:], IN("lg"), [], ["lg"])
    dma("pool", selb[:], IN("selb"), [], ["selb"])
    import os
    KDBG = int(os.environ.get("KDBG", "9"))
    cp(identb[:], ident, ["cst"], ["identb"]); cp(onesb[:], ones, ["cst"], ["onesb"])
    if KDBG >= 1:
        act(hvec[:, 3:4], hvec[:, 1:2], AF.Exp, ["hvec"], ["hvec_a"])
        ts(hvec[:, 3:4], hvec[:, 3:4], -1.0, None, ALU.mult, None, ["hvec_a"], ["hvec_a"])
        act(tokbc[:, 64:128], tokbc[:, 64:128], AF.Exp, ["tokbc"], ["tokbc"])
        ts(tokbc[:, 64:128], tokbc[:, 64:128], -1.0, None, ALU.mult, None, ["tokbc"], ["tokbc"])
    a_bc = tokbc[:, 64:128]; d_bc = tokbc[:, 128:192]

    def in_transpose(rows_ap, R, C, dst, dst_keys_fn):
        si = state["stg"]; state["stg"] ^= 1
        st = stg[si]
        dma("sp", st[0:R, 0:C], rows_ap, [], [("stg", si)])
        nct = C // 128
        for c0 in range(0, nct, 4):
            n = min(4, nct - c0)
            b = bank()
            for j in range(n):
                tr(ps[:, b, j * 128:j * 128 + R], st[0:R, (c0 + j) * 128:(c0 + j + 1) * 128], ident[0:R, 0:R],
                   [("stg", si), "cst"], [("ps", b)])
            cpi[0] += 1
            eng_ = "act" if cpi[0] % 2 else "dve"
            for j in range(n):
                cp(dst(c0 + j), ps[:, b, j * 128:j * 128 + R], [("ps", b)], dst_keys_fn(c0 + j), eng=eng_)

    def out_transpose(src, src_keys_fn, R, C, out_ap, okey):
        si = state["stg"]; state["stg"] ^= 1
        st = stg[si]
        nct = C // 128
        for c0 in range(0, nct, 4):
            n = min(4, nct - c0)
            b = bank()
            for j in range(n):
                tr(ps[0:R, b, j * 128:(j + 1) * 128], src(c0 + j), ident, list(src_keys_fn(c0 + j)) + ["cst"], [("ps", b)])
            cp_alt(st[0:R, c0 * 128:(c0 + n) * 128], ps[0:R, b, 0:n * 128], [("ps", b)], [("stg", si)])
        dma("sp", out_ap, st[0:R, 0:C], [("stg", si)], [okey])
        outkeys.append(okey)

    def hkeys(kt, bi):
        return ("hT", kt, bi)

    def rmsnorm_cols(src, src_key, dstf, dst_key, gcol, ntile, blocks, scale_div):
        for bi, (c0, n) in blocks:
            b = bank()
            for kt in range(ntile):
                si = kt % 2
                act(sq[si][:, 0:n], src(kt, c0, n), AF.Square, [src_key(kt, bi)], [("sq", si)])
                mm(ps[:, b, 0:n], onesb[:], sq[si][:, 0:n], kt == 0, kt == ntile - 1, [("sq", si), "onesb"], [("ps", b)])
            ts(rstd[:, 0:n], ps[:, b, 0:n], 1.0 / scale_div, EPS, ALU.mult, ALU.add, [("ps", b)], ["rstd"])
            act(rstd[:, 0:n], rstd[:, 0:n], AF.Ln, ["rstd"], ["rstd"])
            act(rstd[:, 0:n], rstd[:, 0:n], AF.Exp, ["rstd"], ["rstd"], scale=-0.5)
            for kt in range(ntile):
                stt(dstf(kt, c0, n), src(kt, c0, n), pvec[:, gcol + kt:gcol + kt + 1], rstd[:, 0:n], ALU.mult, ALU.mult,
                    [src_key(kt, bi), "rstd", "pvec"], [dst_key(kt, bi)])

    EB = list(enumerate(BLKS))

    def norm_h(gcol):
        rmsnorm_cols(lambda kt, c0, n: hT[:, kt, c0:c0 + n], hkeys, lambda kt, c0, n: xn[:, kt, c0:c0 + n],
                     lambda kt, bi: ("xn", kt, bi), gcol, KT, EB, float(D))

    def load_w(pieces, nk):
        si = state["slot"]; state["slot"] = (si + 1) % NSLOT
        tot = sum(p.shape[1] for p in pieces)
        assert nk * tot <= SLOTW, (nk, tot)
        view = wsl[si][:, 0:nk * tot].rearrange("p (k n) -> p k n", k=nk)
        off = 0
        offs = []
        for p in pieces:
            n = p.shape[1]
            dma("pool", view[:, :, off:off + n], p.rearrange("(k p) n -> p k n", p=128), [], [("w", si)])
            offs.append(off); off += n
        return si, view, offs

    def proj_fm(view, si, col0, nk, rhs, rhs_keys, blocks, evac, M=128):
        for bi, (c0, n) in blocks:
            b = bank()
            for k in range(nk):
                mm(ps[0:M, b, 0:n], view[:, k, col0:col0 + M], rhs(k, c0, n), k == 0, k == nk - 1,
                   [("w", si)] + list(rhs_keys(k, bi)), [("ps", b)])
            evac(bi, c0, n, ps[0:M, b, 0:n], ("ps", b))

    def xn_rhs(k, c0, n):
        return xn[:, k, c0:c0 + n]

    def xn_keys(k, bi):
        return [("xn", k, bi)]

    def bB_rhs(k, c0, n):
        return bB[:, k, c0:c0 + n]

    def bB_keys(k, bi):
        return [("bB", k, bi)]

    def add_to_h(nt):
        def ev(bi, c0, n, p, pk):
            tt(hT[:, nt, c0:c0 + n], hT[:, nt, c0:c0 + n], p, ALU.add, [hkeys(nt, bi), pk], [hkeys(nt, bi)])
        return ev

    def out_proj(wfn, rhs, rhs_keys, nk, k0=0):
        ncc = SLOTW // nk // 128
        ncc = min(ncc, 4)
        for c in range(0, 16, ncc):
            si, view, offs = load_w([wfn(c * 128, ncc * 128)], nk)
            for t in range(ncc):
                proj_fm(view, si, t * 128, nk, rhs, rhs_keys, EB, add_to_h(c + t))

    def all_h_keys(ct):
        return [hkeys(ct, 0), hkeys(ct, 1), hkeys(ct, 2)]

    if KDBG >= 2:
        in_transpose(IN("xs"), NS, D, lambda ct: hT[:, ct, S0:S0 + NS], all_h_keys)
    if KDBG >= 3:
        in_transpose(IN("xh"), NH, D, lambda ct: hT[:, ct, H0:H0 + NH], all_h_keys)
    for i in range(8 if KDBG >= 4 else 0):
        in_transpose(IN("xp")[i * 128:(i + 1) * 128, :], 128, D, (lambda ct, i=i: hT[:, ct, P0 + i * 128:P0 + (i + 1) * 128]), all_h_keys)

    def mixer_sc():
        norm_h(PV_NMIX + 0)
        hist = arf(PX, 512).rearrange("p (k r) -> p k r", k=KT)
        uo_p = arf(PX + 512, 32).rearrange("p (k r) -> p k r", k=KT)
        uo_s = arf(PX + 544, 512).rearrange("p (k r) -> p k r", k=KT)
        u = arf(PX + 1056, NCOL); bg = arb(PX + 2104, NCOL // 2); acc = arf(PX + 2628, NCOL)
        assert 2628 + NCOL <= PXN
        in_transpose(IN("csc"), 2 * NS, D, lambda ct: hist[:, ct, :], lambda ct: ["hist"])
        for j in range(KT):
            for which in range(3):
                si, view, offs = load_w([IN("sc_w_in")[:, which * D + j * 128: which * D + (j + 1) * 128]], KT)
                if which == 0:
                    proj_fm(view, si, 0, KT, xn_rhs, xn_keys, EB, lambda bi, c0, n, p, pk: cp(bg[:, c0:c0 + n], p, [pk], ["bg"], eng="act"))
                elif which == 1:
                    proj_fm(view, si, 0, KT, xn_rhs, xn_keys, EB, lambda bi, c0, n, p, pk: cp(u[:, c0:c0 + n], p, [pk], ["u"], eng="act"))
                else:
                    proj_fm(view, si, 0, KT, xn_rhs, xn_keys, EB, lambda bi, c0, n, p, pk: tt(u[:, c0:c0 + n], u[:, c0:c0 + n], p, ALU.mult, ["u", pk], ["u"]))
            w0 = pvec[:, PV_SCW + j:PV_SCW + j + 1]; w1 = pvec[:, PV_SCW + 16 + j:PV_SCW + 17 + j]; w2 = pvec[:, PV_SCW + 32 + j:PV_SCW + 33 + j]
            L = NCOL - 18
            ts(acc[:, 18:NCOL], u[:, 16:16 + L], w0, None, ALU.mult, None, ["u", "pvec"], ["acc"])
            stt(acc[:, 18:NCOL], u[:, 17:17 + L], w1, acc[:, 18:NCOL], ALU.mult, ALU.add, ["u", "acc"], ["acc"])
            stt(acc[:, 18:NCOL], u[:, 18:18 + L], w2, acc[:, 18:NCOL], ALU.mult, ALU.add, ["u", "acc"], ["acc"])
            hv = hist[:, j, :].rearrange("p (b r) -> p b r", r=2)
            ts(acc[:, 0:NS], hv[:, :, 0], w0, None, ALU.mult, None, ["hist", "pvec"], ["acc"])
            stt(acc[:, 0:NS], hv[:, :, 1], w1, acc[:, 0:NS], ALU.mult, ALU.add, ["hist", "acc"], ["acc"])
            stt(acc[:, 0:NS], u[:, 0:NS], w2, acc[:, 0:NS], ALU.mult, ALU.add, ["u", "acc"], ["acc"])
            vop(lambda e, j=j: e.memset(bB[:, j, 16:18], 0.0), [], [("bB", j, 0)])
            tt(bB[:, j, 18:NCOL], acc[:, 18:NCOL], bg[:, 18:NCOL], ALU.mult, ["acc", "bg"], [("bB", j, 0), ("bB", j, 1), ("bB", j, 2)])
            tt(bB[:, j, 0:NS], acc[:, 0:NS], bg[:, 0:NS], ALU.mult, ["acc", "bg"], [("bB", j, 0)])
            cp(uo_p[:, j, :], u[:, NCOL - 2:NCOL], ["u"], ["uo_p"], eng="act")
            uov = uo_s[:, j, :].rearrange("p (b r) -> p b r", r=2)
            cp(uov[:, :, 0], hv[:, :, 1], ["hist"], ["uo_s"], eng="act")
            cp(uov[:, :, 1], u[:, 0:NS], ["u"], ["uo_s"], eng="act")
        out_transpose(lambda ct: uo_p[:, ct, :], lambda ct: ["uo_p"], 2, D, scp_o, "o_scp")
        out_transpose(lambda ct: uo_s[:, ct, :], lambda ct: ["uo_s"], 2 * NS, D, scs_o, "o_scs")
        out_proj(lambda c0, n: IN("sc_w_out")[:, c0:c0 + n], bB_rhs, bB_keys, KT)

    def xattn(layer, with_halo):
        barrier()
        memT = bBf[:, 0:8192].bitcast(F32).rearrange("p (k m) -> p k m", k=KT)
        mn = xnf[:, 0:4096].rearrange("p (k m) -> p k m", k=KT)
        KTb = arb(PX, 2048).rearrange("p (k m) -> p k m", k=KT)
        Vt = arb(PX + 2048, 2048).rearrange("p (m n) -> p m n", m=2)
        for mt in range(2):
            in_transpose(IN("mem")[mt * 128:(mt + 1) * 128, :], 128, D, (lambda ct, mt=mt: memT[:, ct, mt * 128:(mt + 1) * 128]),
                         lambda ct: [("memT", ct)] + BB_ALL[0:1])
        rmsnorm_cols(lambda kt, c0, n: memT[:, kt, c0:c0 + n], lambda kt, bi: ("memT", kt),
                     lambda kt, c0, n: mn[:, kt, c0:c0 + n], lambda kt, bi: ("mn", kt),
                     PV_NKV + 16 * layer, KT, [(0, (0, 256))], float(D))
        for which, wmat, oap in (("k", IN("xa_w_k"), mk_o), ("v", IN("xa_w_v"), mv_o)):
            for c in range(8):
                si, view, offs = load_w([wmat[layer, :, c * 256:(c + 1) * 256]], KT)
                for mt in range(2):
                    b = bank()
                    for k in range(KT):
                        mm(ps[:, b, 0:256], mn[:, k, mt * 128:(mt + 1) * 128], view[:, k, :], k == 0, k == KT - 1,
                           [("w", si), ("mn", k)], [("ps", b)])
                    sti = state["stg"]; state["stg"] ^= 1
                    cp(stg[sti][:, 0:256], ps[:, b, 0:256], [("ps", b)], [("stg", sti)], eng="act")
                    okey = ("o_m" + which, layer, c, mt)
                    dma("sp", oap[layer, mt * 128:(mt + 1) * 128, c * 256:(c + 1) * 256], stg[sti][:, 0:256], [("stg", sti)], [okey])
                    outkeys.append(okey)
                    if which == "v":
                        cp(Vt[:, mt, c * 256:(c + 1) * 256], stg[sti][:, 0:256], [("stg", sti)], ["Vt"])
                if which == "k":
                    for t in range(2):
                        b = bank()
                        for k in range(KT):
                            mm(ps[:, b, 0:256], view[:, k, t * 128:(t + 1) * 128], mn[:, k, :], k == 0, k == KT - 1,
                               [("w", si), ("mn", k)], [("ps", b)])
                        cp(KTb[:, c * 2 + t, :], ps[:, b, 0:256], [("ps", b)], [("KTb", c * 2 + t)])
        barrier()
        norm_h(PV_NQ + 16 * layer)
        qs = float(512 ** -0.5)
        for c in range(8):
            si, view, offs = load_w([IN("xa_w_q")[layer, :, c * 256:(c + 1) * 256]], KT)
            for t in range(2):
                nt = c * 2 + t
                proj_fm(view, si, t * 128, KT, xn_rhs, xn_keys, EB,
                        lambda bi, c0, n, p, pk, nt=nt: S.op("act", lambda e: e.mul(bB[:, nt, c0:c0 + n], p, qs),
                                                             [pk], [("bB", nt, bi)]))
        barrier()
        tiles = [(P0 + 128 * i, 128, 1 + (i // 4)) for i in range(8)]
        if with_halo:
            tiles.append((H0, NH, 0))
        Pf = sm[:, 0:256]; Pn = sm[:, 256:384].bitcast(BF16)
        PT = sm[:, 384:500].bitcast(BF16)[:, 0:256].rearrange("p (m t) -> p m t", m=2) if False else xnf[:, 0:256].rearrange("p (m t) -> p m t", m=2)
        st1 = sm[:, 500:504]
        for h in range(4):
            for (c0, nt_, bi) in tiles:
                b = bank()
                for j in range(4):
                    mm(ps[0:nt_, b, 0:256], bB[:, 4 * h + j, c0:c0 + nt_], KTb[:, 4 * h + j, :], j == 0, j == 3,
                       [("bB", 4 * h + j, bi), ("KTb", 4 * h + j)], [("ps", b)])
                vop(lambda e, b=b, nt_=nt_: e.tensor_reduce(out=st1[0:nt_, 0:1], in_=ps[0:nt_, b, 0:256], axis=AX.X, op=ALU.max), [("ps", b)], ["st1a"])
                ts(st1[0:nt_, 1:2], st1[0:nt_, 0:1], -1.0, None, ALU.mult, None, ["st1a"], ["st1b"])
                act(Pf[0:nt_, :], ps[0:nt_, b, 0:256], AF.Exp, [("ps", b), "st1b"], ["Pf", "st1c"], bias=st1[0:nt_, 1:2], accum=st1[0:nt_, 2:3])
                vop(lambda e, nt_=nt_: e.reciprocal(out=st1[0:nt_, 3:4], in_=st1[0:nt_, 2:3]), ["st1c"], ["st1d"])
                ts(Pn[0:nt_, :], Pf[0:nt_, :], st1[0:nt_, 3:4], None, ALU.mult, None, ["Pf", "st1d"], ["Pn"])
                pb = pbank()
                for mt in range(2):
                    tr(psb[:, pb, mt * 128:mt * 128 + nt_], Pn[0:nt_, mt * 128:(mt + 1) * 128], identb[0:nt_, 0:nt_], ["Pn", "identb"], [("psb", pb)])
                for mt in range(2):
                    cp(PT[:, mt, 0:nt_], psb[:, pb, mt * 128:mt * 128 + nt_], [("psb", pb)], ["PT"], eng="act")
                for j in range(4):
                    b2 = bank()
                    for mt in range(2):
                        mm(ps[:, b2, 0:nt_], Vt[:, mt, (4 * h + j) * 128:(4 * h + j + 1) * 128], PT[:, mt, 0:nt_], mt == 0, mt == 1,
                           ["Vt", "PT"], [("ps", b2)])
                    cp_alt(bB[:, 4 * h + j, c0:c0 + nt_], ps[:, b2, 0:nt_], [("ps", b2)], [("bB", 4 * h + j, bi)])
        qtok = xnf[0:16, 512:512 + 2048]
        kvst = [xnf[:, 4096 + i * 4096:4096 + (i + 1) * 4096].bitcast(F32) for i in range(2)]
        Pz = xnf[:, 12288:12288 + 4096].bitcast(F32).rearrange("p (f h b c) -> p f h b c", f=2, h=4, b=NS)
        scs = arf(PX, 128).rearrange("p (f b h) -> p f b h", f=2, b=NS)
        junk = arf(PX + 128, 512)
        sp_ = ar[0:64, PX + 640:PX + 896]
        Pm = arf(PX + 896, 128).rearrange("p (f b h) -> p f b h", f=2, b=NS)
        otok = ar[0:16, PX + 1024:PX + 3072]
        st2 = sm[0:64, 504:508]
        barrier()
        for c4 in range(4):
            pb = pbank()
            for j in range(4):
                tr(psb[0:16, pb, j * 128:(j + 1) * 128], bB[:, c4 * 4 + j, 0:NS], identb[:], [("bB", c4 * 4 + j, 0), "identb"], [("psb", pb)])
            cp(qtok[:, c4 * 512:(c4 + 1) * 512], psb[0:16, pb, 0:512], [("psb", pb)], ["qtok"])
        kv = [stg[0], stg[1], kvst[0], kvst[1]]
        kvk = [("stg", 0), ("stg", 1), "kvst0", "kvst1"]
        for bsmp in range(NS):
            o2 = (bsmp % 2) * 2
            for f in range(2):
                dma("sp", kv[o2 + f], IN("ck")[layer, bsmp, f * 128:(f + 1) * 128, :], [], [kvk[o2 + f]] + (XN_ALL if (bsmp < 2 and o2 == 2) else []))
            for h in range(4):
                b = bank()
                mm(ps[:, b, :], selb[:, bsmp * 128:(bsmp + 1) * 128], qtok[:, h * 512:(h + 1) * 512], True, True, ["selb", "qtok"], [("ps", b)])
                for f in range(2):
                    vop(lambda e, b=b, f=f, h=h, o2=o2, bsmp=bsmp: e.scalar_tensor_tensor(
                        out=junk, in0=kv[o2 + f][:, h * 512:(h + 1) * 512], scalar=1.0, in1=ps[:, b, :],
                        op0=ALU.mult, op1=ALU.mult, accum_out=scs[:, f, bsmp, h:h + 1]),
                        [kvk[o2 + f], ("ps", b)], ["junk", "scs"])
        b = bank()
        for f in range(2):
            tr(ps[0:64, b, f * 128:(f + 1) * 128], arf(PX + 64 * f, 64), ident, ["scs", "cst"], [("ps", b)])
        vop(lambda e, b=b: e.tensor_reduce(out=st2[:, 0:1], in_=ps[0:64, b, 0:256], axis=AX.X, op=ALU.max), [("ps", b)], ["st2a"])
        ts(st2[:, 1:2], st2[:, 0:1], -1.0, None, ALU.mult, None, ["st2a"], ["st2b"])
        act(sp_, ps[0:64, b, 0:256], AF.Exp, [("ps", b), "st2b"], ["sp_", "st2c"], bias=st2[:, 1:2], accum=st2[:, 2:3])
        vop(lambda e: e.reciprocal(out=st2[:, 3:4], in_=st2[:, 2:3]), ["st2c"], ["st2d"])
        ts(sp_, sp_, st2[:, 3:4], None, ALU.mult, None, ["sp_", "st2d"], ["sp_"])
        b = bank()
        for f in range(2):
            tr(ps[:, b, f * 64:(f + 1) * 64], sp_[:, f * 128:(f + 1) * 128], ident[0:64, 0:64], ["sp_", "cst"], [("ps", b)])
        cp(arf(PX + 896, 128), ps[:, b, 0:128], [("ps", b)], ["Pm"])
        for f in range(2):
            for h in range(4):
                tt(Pz[:, f, h, :, :], Pm[:, f, :, h].unsqueeze(2).to_broadcast([128, NS, NS]), d16, ALU.mult, ["Pm", "cst"], ["Pz"])
        ob = [bank() for _ in range(4)]
        for bsmp in range(NS):
            o2 = (bsmp % 2) * 2
            for f in range(2):
                dma("sp", kv[o2 + f], IN("cv")[layer, bsmp, f * 128:(f + 1) * 128, :], [], [kvk[o2 + f]])
            for h in range(4):
                for f in range(2):
                    mm(ps[0:16, ob[h], :], Pz[:, f, h, bsmp, :], kv[o2 + f][:, h * 512:(h + 1) * 512], bsmp == 0 and f == 0, bsmp == NS - 1 and f == 1,
                       ["Pz", kvk[o2 + f]], [("ps", ob[h])])
        for h in range(4):
            cp_alt(otok[:, h * 512:(h + 1) * 512], ps[0:16, ob[h], :], [("ps", ob[h])], ["otok"])
        for c4 in range(4):
            b = bank()
            for j in range(4):
                tr(ps[:, b, j * 16:(j + 1) * 16], otok[:, (c4 * 4 + j) * 128:(c4 * 4 + j + 1) * 128], ident[0:16, 0:16], ["otok", "cst"], [("ps", b)])
            for j in range(4):
                cp(bB[:, c4 * 4 + j, 0:NS], ps[:, b, j * 16:(j + 1) * 16], [("ps", b)], [("bB", c4 * 4 + j, 0)])
        out_proj(lambda c0, n: IN("xa_w_o")[layer, :, c0:c0 + n], bB_rhs, bB_keys, KT)
        barrier()

    def ffn(layer):
        norm_h(PV_NFFN + 16 * layer)
        gs = [arb(PX, NCOL // 2), arb(PX + 600, NCOL // 2)]
        f0 = 0
        while f0 < 44:
            nf = min(8, 44 - f0)
            for j0 in range(0, nf, 2):
                cg = (f0 + j0) * 128
                sg, vg, _ = load_w([IN("ffn_w_gate")[layer, :, cg:cg + 256]], KT)
                su, vu, _ = load_w([IN("ffn_w_up")[layer, :, cg:cg + 256]], KT)
                for t in range(2):
                    jj = j0 + t
                    gbuf = gs[jj % 2]; gk = ("gs", jj % 2)
                    proj_fm(vg, sg, t * 128, KT, xn_rhs, xn_keys, EB,
                            lambda bi, c0, n, p, pk, gbuf=gbuf, gk=gk: act(gbuf[:, c0:c0 + n], p, AF.Silu, [pk], [gk]))
                    proj_fm(vu, su, t * 128, KT, xn_rhs, xn_keys, EB,
                            lambda bi, c0, n, p, pk, gbuf=gbuf, gk=gk, jj=jj: tt(bB[:, jj, c0:c0 + n], gbuf[:, c0:c0 + n], p, ALU.mult, [gk, pk], [("bB", jj, bi)]))
            out_proj(lambda c0, n, f0=f0, nf=nf: IN("ffn_w_down")[layer, f0 * 128:(f0 + nf) * 128, c0:c0 + n], bB_rhs, bB_keys, nf)
            f0 += nf

    def mixer_ssd():
        barrier()
        norm_h(PV_NMIX + 16)
        for kt in range(KT):
            ts(xn[:, kt, H0:H0 + NH], xn[:, kt, H0:H0 + NH], flag, None, ALU.mult, None, [("xn", kt, 0), "cst"], [("xn", kt, 0)])
        xT = bB[:, 0:4, :]; BT = bB[:, 4, :]; CT = bB[:, 5, :]
        xtok = bBf[:, 6 * NCOL:6 * NCOL + 4096].rearrange("p (c n) -> p c n", c=8)
        Btok = bB[:, 10, 0:1024].rearrange("p (c n) -> p c n", c=8)
        EW = [bB[:, 11 + i, 0:1024].rearrange("p (h t) -> p h t", h=8) for i in range(3)]
        xdtb = [bB[:, 15, 0:512].rearrange("p (h d) -> p h d", h=8), bB[:, 15, 512:1024].rearrange("p (h d) -> p h d", h=8),
                bB[:, 14, 0:512].rearrange("p (h d) -> p h d", h=8)]
        o = PX
        raw = arf(o, NCOL); o += NCOL
        R = arf(o, 1024).rearrange("p (h t) -> p h t", h=8); o += 1024
        Sst = arf(o, 512); o += 512
        Sbf = arb(o, 256); o += 256
        tmpA = arf(o, 512); o += 512
        xdte = arb(o, 256).rearrange("p (h d) -> p h d", h=8); o += 256
        CBTb = [arb(o, 64), arb(o + 64, 64), bB[:, 14, 512:640]]; o += 128
        xs_conv = arb(o, 256).rearrange("p (j b) -> p j b", j=32); o += 256
        y_s = arf(o, 512).rearrange("p (j b) -> p j b", j=32); o += 512
        assert o <= ARW, o
        o = 1024
        tstat = arf(o, 16); o += 16
        cs_sb = [arf(o, 32), arf(o + 32, 32), arf(o + 64, 32)]; o += 96
        ncs = [arf(o, 8), arf(o + 8, 8), arf(o + 16, 8)]; o += 24
        dq = arf(o, 132); o += 132
        xdts = arf(o, 64).rearrange("p (b j) -> p b j", b=NS); o += 64
        dAs = arf(o, 16); o += 16
        bcs = arb(o, 128); o += 128
        assert o <= 1536, o
        o = 1536
        dt_tok = arf(o, 512).rearrange("p (c h) -> p c h", c=8); o += 512
        dta_tok = arf(o, 512).rearrange("p (c h) -> p c h", c=8); o += 512
        gg = arf(o, 512); o += 512
        gnb = [arb(o, 256), arb(o + 256, 256)]; o += 512
        histg = arf(o, 288).rearrange("p (t r) -> p t r", t=6); o += 288
        cvo_s = arf(o, 288).rearrange("p (t r) -> p t r", t=6); o += 288
        cvo_p = arf(o, 18).rearrange("p (t r) -> p t r", t=6); o += 18
        BsT = arb(o, 64).rearrange("p (g b) -> p g b", g=8); o += 64
        CsT = arb(o, 64).rearrange("p (g b) -> p g b", g=8); o += 64
        rallb = arf(o, 132); o += 132
        rg = arf(o, 132); o += 132
        stS = [arf(o, 512).rearrange("p (j n) -> p j n", j=4), arf(o + 512, 512).rearrange("p (j n) -> p j n", j=4)]
        cstg = arf(o, 512)
        o += 1024
        assert o <= 5632, o
        smid = arf(PX + NCOL, 512)
        stS4 = [stS[0], stS[1], gg.rearrange("p (j n) -> p j n", j=4), ar[:, 1536 + 1536:1536 + 2048].rearrange("p (j n) -> p j n", j=4)]
        t1 = tmpA.rearrange("p (j n) -> p j n", j=4)
        def fence():
            vop(lambda e: e.memset(tstat[:, 0:1], 0.0), [], [("stg", 0), ("stg", 1), "kvst0", "kvst1", "fence"])
            barrier()

        fence()

        dtT = ar[0:64, 3584 + 200:3584 + 200 + NCOL] if False else raw[0:64, :]
        sdt, vdt, _ = load_w([IN("ssd_w_in")[:, 10240:10304]], KT)
        for bi, (c0, n) in EB:
            b = bank()
            for k in range(KT):
                mm(ps[0:64, b, 0:n], vdt[:, k, 0:64], xn[:, k, c0:c0 + n], k == 0, k == KT - 1, [("w", sdt), ("xn", k, bi)], [("ps", b)])
            act(dtT[:, c0:c0 + n], ps[0:64, b, 0:n], AF.Exp, [("ps", b), "hvec"], ["dtT"], bias=hvec[:, 0:1])
        ts(dtT, dtT, 1.0, None, ALU.add, None, ["dtT"], ["dtT"])
        act(dtT, dtT, AF.Ln, ["dtT"], ["dtT"])
        for c4 in range(2):
            b = bank()
            for j in range(4):
                c = c4 * 4 + j
                tr(ps[:, b, j * 64:(j + 1) * 64], dtT[:, P0 + c * 128:P0 + (c + 1) * 128], ident[0:64, 0:64], ["dtT", "cst"], [("ps", b)])
            cp(ar[:, 1536 + c4 * 256:1536 + (c4 + 1) * 256], ps[:, b, 0:256], [("ps", b)], ["dt_tok"])
        tt(dta_tok, dt_tok, a_bc.unsqueeze(1).to_broadcast([128, 8, 64]), ALU.mult, ["dt_tok", "tokbc"], ["dta_tok"])
        M2 = lg[:, 128:132]
        dts = dtT[:, 0:NS]
        act(dAs[0:64, :], dts, AF.Exp, ["dtT", "hvec_a"], ["dAs"], scale=hvec[:, 3:4])
        tt(rallb[0:64, 0:64].rearrange("p (b j) -> p b j", b=NS), dts.unsqueeze(2).to_broadcast([64, NS, 4]), M2.unsqueeze(1).to_broadcast([64, NS, 4]), ALU.mult, ["dtT", "lg"], ["rall"])
        tt(rallb[0:64, 64:128].rearrange("p (b j) -> p b j", b=NS), dAs[0:64, :].unsqueeze(2).to_broadcast([64, NS, 4]), M2.unsqueeze(1).to_broadcast([64, NS, 4]), ALU.mult, ["dAs", "lg"], ["rall"])
        ts(rallb[0:64, 128:132], M2, hvec[:, 2:3], None, ALU.mult, None, ["lg", "hvec"], ["rall"])
        barrier()

        XCH0 = 0; BCH0 = 4096; CCH0 = 5120

        def do_conv(dst_tile, ct, hist_t, full, ck_):
            w = [pvec[:, PV_CW + 48 * r + ct:PV_CW + 48 * r + ct + 1] for r in range(4)]
            bia = pvec[:, PV_CB + ct:PV_CB + ct + 1]
            Rf = arf(PX + NCOL, 1024)
            for (c0, n) in ((19, 517), (536, 512)):
                a = Rf[:, 0:n]
                S.op("act", lambda e, a=a, c0=c0, n=n: e.activation(out=a, in_=raw[:, c0 - 3:c0 - 3 + n], func=AF.Identity, scale=w[0]), ["raw", "pvec"], ["R"])
                stt(a, raw[:, c0 - 2:c0 - 2 + n], w[1], a, ALU.mult, ALU.add, ["raw", "R"], ["R"])
                stt(a, raw[:, c0 - 1:c0 - 1 + n], w[2], a, ALU.mult, ALU.add, ["raw", "R"], ["R"])
                stt(a, raw[:, c0:c0 + n], w[3], a, ALU.mult, ALU.add, ["raw", "R"], ["R"])
                act(dst_tile[:, c0:c0 + n], a, AF.Silu, ["R", "pvec"], [ck_], bias=bia)
            hv = histg[:, hist_t, :].rearrange("p (b r) -> p b r", r=3)
            a = Rf[:, 0:NS]
            ts(a, hv[:, :, 0], w[0], None, ALU.mult, None, ["histg", "pvec"], ["R"])
            stt(a, hv[:, :, 1], w[1], a, ALU.mult, ALU.add, ["histg", "R"], ["R"])
            stt(a, hv[:, :, 2], w[2], a, ALU.mult, ALU.add, ["histg", "R"], ["R"])
            stt(a, raw[:, 0:NS], w[3], a, ALU.mult, ALU.add, ["raw", "R"], ["R"])
            act(dst_tile[:, 0:NS], a, AF.Silu, ["R", "pvec"], [ck_], bias=bia)
            if full:
                cp(cvo_p[:, hist_t, :], raw[:, NCOL - 3:NCOL], ["raw"], ["cvo_p"])
                ov = cvo_s[:, hist_t, :].rearrange("p (b r) -> p b r", r=3)
                cp(ov[:, :, 0:2], hv[:, :, 1:3], ["histg"], ["cvo_s"])
                cp(ov[:, :, 2], raw[:, 0:NS], ["raw"], ["cvo_s"])

        def load_hist(g):
            for (c0, ncol, t0) in ((XCH0 + 512 * g, 512, 0), (BCH0 + 128 * g, 128, 4), (CCH0 + 128 * g, 128, 5)):
                dma("sp", cstg[0:48, 0:ncol], IN("ccv")[:, c0:c0 + ncol], [], ["cstg"])
                b = bank()
                nct = ncol // 128
                for j in range(nct):
                    tr(ps[:, b, j * 48:(j + 1) * 48], cstg[0:48, j * 128:(j + 1) * 128], ident[0:48, 0:48], ["cstg", "cst"], [("ps", b)])
                cp(histg[:, t0:t0 + nct, :], ps[:, b, 0:nct * 48].rearrange("p (t r) -> p t r", t=nct), [("ps", b)], ["histg"])

        def store_conv(g):
            for (c0, ncol, t0) in ((XCH0 + 512 * g, 512, 0), (BCH0 + 128 * g, 128, 4), (CCH0 + 128 * g, 128, 5)):
                nct = ncol // 128
                b = bank()
                for j in range(nct):
                    tr(ps[0:48, b, j * 128:(j + 1) * 128], cvo_s[:, t0 + j, :], ident, ["cvo_s", "cst"], [("ps", b)])
                cp(cstg[0:48, 0:ncol], ps[0:48, b, 0:ncol], [("ps", b)], ["cstg"])
                ok = ("o_cvs", g, t0); outkeys.append(ok)
                dma("sp", cvs_o[:, c0:c0 + ncol], cstg[0:48, 0:ncol], ["cstg"], [ok])
                b = bank()
                for j in range(nct):
                    tr(ps[0:3, b, j * 128:(j + 1) * 128], cvo_p[:, t0 + j, :], ident, ["cvo_p", "cst"], [("ps", b)])
                cp(cstg[0:3, 0:ncol], ps[0:3, b, 0:ncol], [("ps", b)], ["cstg"])
                ok = ("o_cvp", g, t0); outkeys.append(ok)
                dma("sp", cvp_o[:, c0:c0 + ncol], cstg[0:3, 0:ncol], ["cstg"], [ok])

        def inproj_conv(g, full):
            load_hist(g)
            specs = [(4096 + 512 * g + 0, 2), (4096 + 512 * g + 256, 2), (8192 + 128 * g, 1)]
            specs.append((9216 + 128 * g, 1))
            tix = 0
            for (wc0, ntl) in specs:
                si, view, _ = load_w([IN("ssd_w_in")[:, wc0:wc0 + ntl * 128]], KT)
                for t in range(ntl):
                    proj_fm(view, si, t * 128, KT, xn_rhs, xn_keys, EB,
                            lambda bi, c0, n, p, pk: cp_alt(raw[:, c0:c0 + n], p, [pk], ["raw"]))
                    ch = wc0 - 4096 + t * 128
                    if tix < 4:
                        dst = xT[:, tix, :]; ck_ = "xTk"
                    elif tix == 4:
                        dst = BT; ck_ = "BTk"
                    else:
                        dst = CT; ck_ = "CTk"
                    do_conv(dst, ch // 128, tix, full, ck_)
                    tix += 1
            for c in range(8):
                pb = pbank()
                for j in range(4):
                    tr(psb[:, pb, j * 128:(j + 1) * 128], xT[:, j, P0 + c * 128:P0 + (c + 1) * 128], identb[:], ["xTk", "identb"], [("psb", pb)])
                cp(xtok[:, c, :], psb[:, pb, 0:512], [("psb", pb)], ["xtok"], eng="act")
            for c4 in range(2):
                pb = pbank()
                for j in range(4):
                    c = c4 * 4 + j
                    tr(psb[:, pb, j * 128:(j + 1) * 128], BT[:, P0 + c * 128:P0 + (c + 1) * 128], identb[:], ["BTk", "identb"], [("psb", pb)])
                cp(Btok[:, c4 * 4:(c4 + 1) * 4, :], psb[:, pb, 0:512].rearrange("p (c n) -> p c n", c=4), [("psb", pb)], ["Btok"], eng="act")
            cp(xs_conv[:, 4 * g:4 * g + 4, :], xT[:, :, 0:NS], ["xTk"], ["xs_conv"])
            cp(BsT[:, g, :], BT[:, 0:NS], ["BTk"], ["BsT"])
            cp(CsT[:, g, :], CT[:, 0:NS], ["CTk"], ["CsT"])

        def stageA(g, c, full):
            i = c % 3
            hs = slice(8 * g, 8 * g + 8)
            b = bank()
            mm(ps[:, b, 0:8], tri, dta_tok[:, c, hs], True, True, ["cst", "dta_tok"], [("ps", b)])
            mm(ps[:, b, 8:16], ones, dta_tok[:, c, hs], True, True, ["cst", "dta_tok"], [("ps", b)])
            if full:
                mm(ps[:, b, 128:256], BT[:, P0 + c * 128:P0 + (c + 1) * 128], CT[:, P0 + c * 128:P0 + (c + 1) * 128], True, True, ["BTk", "CTk"], [("ps", b)])
            cp(cs_sb[i][:, 0:16], ps[:, b, 0:16], [("ps", b)], [("cs", i)], eng="act")
            if full:
                cp(CBTb[i], ps[:, b, 128:256], [("ps", b)], [("CBT", i)], eng="act")
            tt(cs_sb[i][:, 16:24], cs_sb[i][:, 8:16], cs_sb[i][:, 0:8], ALU.subtract, [("cs", i)], [("cs", i)])
            act(cs_sb[i][:, 16:24], cs_sb[i][:, 16:24], AF.Exp, [("cs", i)], [("cs", i)])
            act(cs_sb[i][:, 24:32], cs_sb[i][:, 8:16], AF.Exp, [("cs", i)], [("cs", i)])
            tt(xdtb[i], xtok[:, c, :].rearrange("p (h d) -> p h d", h=8), dt_tok[:, c, hs].unsqueeze(2).to_broadcast([128, 8, 64]), ALU.mult,
               ["xtok", "dt_tok"], [("xdt", i)])
            if full:
                ts(ncs[i], cs_sb[i][:, 0:8], -1.0, None, ALU.mult, None, [("cs", i)], [("ncs", i)])
                tt(R, tri.unsqueeze(1).to_broadcast([128, 8, 128]), dta_tok[:, c, hs].unsqueeze(2).to_broadcast([128, 8, 128]), ALU.mult,
                   ["cst", "dta_tok"], ["R"])
                pbk = [bank(), bank()]
                Rf = arf(PX + NCOL, 1024)
                for hh in range(2):
                    mm(ps[:, pbk[hh], :], ones, Rf[:, hh * 512:(hh + 1) * 512], True, False, ["cst", "R"], [("ps", pbk[hh])])
                    mm(ps[:, pbk[hh], :], ident, mneg4, False, True, ["cst"], [("ps", pbk[hh])])
                for h in range(8):
                    act(EW[i][:, h, :], ps[:, pbk[h // 4], (h % 4) * 128:(h % 4 + 1) * 128], AF.Exp, [("ps", pbk[h // 4]), ("ncs", i)], [("EW", i)], bias=ncs[i][:, h:h + 1])

        def stageW(c):
            i = c % 3
            tt(EW[i], EW[i], CBTb[i].unsqueeze(1).to_broadcast([128, 8, 128]), ALU.mult, [("EW", i), ("CBT", i)], [("EW", i)])

        def stageT(c):
            k2 = c % 2
            pb = pbank()
            for j in range(4):
                tr(psb[:, pb, j * 128:(j + 1) * 128], gnb[k2][:, j * 128:(j + 1) * 128], identb[:], [("gn", k2), "identb"], [("psb", pb)])
            for j in range(4):
                cp(xT[:, j, P0 + c * 128:P0 + (c + 1) * 128], psb[:, pb, j * 128:(j + 1) * 128], [("psb", pb)], ["xTk"], eng="act")

        def stageB(g, c, full):
            i = c % 3
            if full:
                by = bank()
                for h in range(8):
                    mm(ps[:, by, h * 64:(h + 1) * 64], EW[i][:, h, :], xdtb[i][:, h, :], True, True, [("EW", i), ("xdt", i)], [("ps", by)])
                bo = bank()
                mm(ps[:, bo, :], CT[:, P0 + c * 128:P0 + (c + 1) * 128], Sbf, True, True, ["CTk", "Sbf"], [("ps", bo)])
                act(tstat[:, 0:8], cs_sb[i][:, 0:8], AF.Exp, [("cs", i)], ["expcs"])
                tt(gg.rearrange("p (h d) -> p h d", h=8), ps[:, bo, :].rearrange("p (h d) -> p h d", h=8),
                   tstat[:, 0:8].unsqueeze(2).to_broadcast([128, 8, 64]), ALU.mult, [("ps", bo), "expcs"], ["gg"])
                tt(gg, gg, ps[:, by, :], ALU.add, ["gg", ("ps", by)], ["gg"])
                tt(tmpA.rearrange("p (h d) -> p h d", h=8), xtok[:, c, :].rearrange("p (h d) -> p h d", h=8),
                   d_bc[:, 8 * g:8 * g + 8].unsqueeze(2).to_broadcast([128, 8, 64]), ALU.mult, ["xtok", "tokbc"], ["tmpA"])
                tt(gnb[c % 2], gg, tmpA, ALU.add, ["gg", "tmpA"], [("gn", c % 2)])
            tt(xdte, xdtb[i], cs_sb[i][:, 16:24].unsqueeze(2).to_broadcast([128, 8, 64]), ALU.mult, [("xdt", i), ("cs", i)], ["xdte"])
            bs_ = bank()
            mm(ps[:, bs_, :], Btok[:, c, :], xdte.rearrange("p h d -> p (h d)"), True, True, ["Btok", "xdte"], [("ps", bs_)])
            tt(Sst.rearrange("p (h d) -> p h d", h=8), Sst.rearrange("p (h d) -> p h d", h=8),
               cs_sb[i][:, 24:32].unsqueeze(2).to_broadcast([128, 8, 64]), ALU.mult, ["Sst", ("cs", i)], ["Sst"])
            tt(Sst, Sst, ps[:, bs_, :], ALU.add, ["Sst", ("ps", bs_)], ["Sst"])
            cp(Sbf, Sst, ["Sst"], ["Sbf"], eng="act")

        def scan(g, full):
            if full:
                dma("sp", smid, cc_dst[128 * g:128 * (g + 1), :], ["cc_dst"], ["smid", "R"])
                ts(Sst, smid, flag, None, ALU.mult, None, ["smid", "R", "cst"], ["Sst"])
            else:
                vop(lambda e: e.memset(Sst, 0.0), [], ["Sst"])
            cp(Sbf, Sst, ["Sst"], ["Sbf"], eng="act")
            if full:
                stageA(g, 0, True); stageA(g, 1, True); stageW(0)
                for c in range(8):
                    if c + 2 < 8:
                        stageA(g, c + 2, True)
                    stageB(g, c, True)
                    if c + 1 < 8:
                        stageW(c + 1)
                    if c >= 1:
                        stageT(c - 1)
                stageT(7)
            else:
                stageA(g, 0, False)
                for c in range(8):
                    if c + 1 < 8:
                        stageA(g, c + 1, False)
                    stageB(g, c, False)

        L64 = lg[:, 0:128]

        def sample_group(g):
            ts(rg[0:64, :], rallb[0:64, :], lg[:, 132 + g:133 + g], None, ALU.mult, None, ["rall", "lg"], ["rg"])
            b = bank()
            mm(ps[:, b, 0:132], L64, rg[0:64, :], True, True, ["lg", "rg"], [("ps", b)])
            cp(dq, ps[:, b, 0:132], [("ps", b)], ["dq"])
            tt(xdts, xs_conv[:, 4 * g:4 * g + 4, :].rearrange("p j b -> p b j"), dq[:, 0:64].rearrange("p (b j) -> p b j", b=NS), ALU.mult,
               ["xs_conv", "dq"], ["xdts"])
            pb = pbank()
            tr(psb[0:16, pb, 0:128], BsT[:, g, :], identb[:], ["BsT", "identb"], [("psb", pb)])
            tr(psb[0:16, pb, 128:256], CsT[:, g, :], identb[:], ["CsT", "identb"], [("psb", pb)])
            cp(bcs[0:16, :], psb[0:16, pb, 0:256], [("psb", pb)], ["bcs"])
            for bsm in range(NS):
                st = stS4[bsm % 4]; sk = ("cstg", ("stS", 1), ("stS", 2), ("stS", 3))[bsm % 4]
                dma("sp", st, IN("sst")[bsm, 512 * g:512 * (g + 1), :].rearrange("(j q) n -> q j n", q=128), [], [sk])
                b = bank()
                mm(ps[:, b, 0:256], selb[:, bsm * 128:(bsm + 1) * 128], bcs[0:16, :], True, True, ["selb", "bcs"], [("ps", b)])
                tt(t1, xdts[:, bsm, :].unsqueeze(2).to_broadcast([128, 4, 128]), ps[:, b, 0:128].unsqueeze(1).to_broadcast([128, 4, 128]), ALU.mult,
                   ["xdts", ("ps", b)], ["t1"])
                tt(st, st, dq[:, 64 + 4 * bsm:64 + 4 * bsm + 4].unsqueeze(2).to_broadcast([128, 4, 128]), ALU.mult, [sk, "dq"], [sk])
                tt(st, st, t1, ALU.add, [sk, "t1"], [sk])
                ok = ("o_sss", g, bsm); outkeys.append(ok)
                dma("sp", sss_o[bsm, 512 * g:512 * (g + 1), :].rearrange("(j q) n -> q j n", q=128), st, [sk], [ok])
                tt(t1, st, ps[:, b, 128:256].unsqueeze(1).to_broadcast([128, 4, 128]), ALU.mult, [sk, ("ps", b)], ["t1"])
                vop(lambda e: e.tensor_reduce(out=tstat[:, 12:16], in_=t1, axis=AX.X, op=ALU.add), ["t1"], ["ysr"])
                tt(tstat[:, 4:8], xs_conv[:, 4 * g:4 * g + 4, bsm], dq[:, 128:132], ALU.mult, ["xs_conv", "dq"], ["ydx"])
                tt(y_s[:, 4 * g:4 * g + 4, bsm], tstat[:, 12:16], tstat[:, 4:8], ALU.add, ["ysr", "ydx"], ["y_s"])

        for g in range(8):
            inproj_conv(g, False)
            if g > 0:
                sample_group(g - 1)
            scan(g, False)
            dma("sp", cc_src[128 * g:128 * (g + 1), :], Sst, ["Sst"], [("cc_src", g)])
        S.op("pool", lambda e: e.collective_compute("AllGather", ALU.bypass, replica_groups=[[0, 1], [2, 3], [4, 5], [6, 7]],
                                                     ins=[cc_src.opt()], outs=[cc_dst.opt()]),
             [("cc_src", g) for g in range(8)], ["cc_dst"])
        sample_group(7)
        vop(lambda e: e.memset(tstat[:, 2:3], 0.0), [], [("stS", 2), ("stS", 3), ("stS", 1), "cstg", "fence3"])
        barrier()

        for g in range(8):
            inproj_conv(g, True)
            store_conv(g)
            scan(g, True)
            b = bank()
            for j in range(4):
                tr(ps[:, b, j * 128:(j + 1) * 128], Sst[:, j * 128:(j + 1) * 128], ident, ["Sst", "cst"], [("ps", b)])
            cp(tmpA, ps[:, b, :], [("ps", b)], ["tmpA"])
            ok = ("o_ssp", g); outkeys.append(ok)
            dma("sp", ssp_o[512 * g:512 * (g + 1), :].rearrange("(j q) n -> q j n", q=128), tmpA.rearrange("p (j n) -> p j n", j=4), ["tmpA"], [ok])
            cp(xT[:, :, 0:NS], y_s[:, 4 * g:4 * g + 4, :], ["y_s"], ["xTk"])
            for half in range(2):
                si, view, _ = load_w([IN("ssd_w_in")[:, 512 * g + 256 * half:512 * g + 256 * (half + 1)]], KT)
                for t in range(2):
                    j = half * 2 + t
                    def ev(bi, c0, n, p, pk, j=j):
                        act(gg[:, 0:n], p, AF.Silu, [pk], ["gg"])
                        tt(xT[:, j, c0:c0 + n], xT[:, j, c0:c0 + n], gg[:, 0:n], ALU.mult, ["xTk", "gg"], [("gT", j, bi)])
                    proj_fm(view, si, t * 128, KT, xn_rhs, xn_keys, EB, ev)
            rmsnorm_cols(lambda kt, c0, n: xT[:, kt, c0:c0 + n], lambda kt, bi: ("gT", kt, bi),
                         lambda kt, c0, n: xT[:, kt, c0:c0 + n], lambda kt, bi: ("gT", kt, bi),
                         PV_SNORM + 4 * g, 4, EB, 512.0)
            out_proj(lambda c0, n, g=g: IN("ssd_w_out")[512 * g:512 * (g + 1), c0:c0 + n],
                     lambda k, c0, n: xT[:, k, c0:c0 + n], lambda k, bi: [("gT", k, bi), "xTk"], 4)
            vop(lambda e: e.memset(tstat[:, 1:2], 0.0), [("gT", k, bi) for k in range(4) for bi in range(3)], ["xTk"])
        barrier()

    import os
    kstop = int(os.environ.get("KSTOP", "7"))

    def ssd_phase():
        mixer_ssd()
        S.op("dve", lambda e: e.memset(sm[:, 508:509], 0.0), [], [("stg", 0), ("stg", 1), "fence2"])
        barrier()

    phases = [mixer_sc, lambda: xattn(0, True), lambda: ffn(0), ssd_phase, lambda: xattn(1, False), lambda: ffn(1)]
    for ph in phases[:kstop]:
        ph()
    if kstop >= 7:
        rmsnorm_cols(lambda kt, c0, n: hT[:, kt, c0:c0 + n], hkeys, lambda kt, c0, n: hT[:, kt, c0:c0 + n], hkeys, PV_NFIN, KT, EB, float(D))
    if KDBG >= 5:
        barrier()
    if KDBG >= 6:
        out_transpose(lambda ct: hT[:, ct, S0:S0 + NS], all_h_keys, NS, D, ys_o, "o_ys")
    for i in range(8 if KDBG >= 7 else 0):
        out_transpose((lambda ct, i=i: hT[:, ct, P0 + i * 128:P0 + (i + 1) * 128]), all_h_keys, 128, D, yp_o[i * 128:(i + 1) * 128, :], ("o_yp", i))
    S.op("sp", lambda e: None, outkeys, [], real=False)

    sems = {e: es.enter_context(nc.semaphore("sem_" + e)) for e in Sched.ENGS}
    dsems = {q: [es.enter_context(nc.semaphore(f"dsem_{q}{i}")) for i in range(S.NDS)] for q in ("pool", "sp")}
    block = es.enter_context(nc.Block())
    S.replay(nc, block, sems, dsems)
    es.close()
    nc._used_inputs = sorted(_in_aps.keys())
    return nc


def _fm(v, nt):
    return np.ascontiguousarray(np.asarray(v, np.float32).reshape(nt, 128).T)


def _consts():
    cst = np.zeros((128, 1160), np.float32)
    cst[:, 0:128] = np.eye(128, dtype=np.float32)
    s = np.arange(128)
    tri = (s[:, None] <= s[None, :]).astype(np.float32)
    cst[:, 128:256] = tri
    cst[:, 256:384] = 1.0
    cst[:, 384:896] = np.tile((1.0 - tri) * -30000.0, (1, 4))
    cst[:, 896:1152] = np.tile(np.eye(16, dtype=np.float32).reshape(1, 256), (128, 1))
    sel = np.zeros((16, 16, 128), np.float32)
    for b in range(16):
        sel[b, b, :] = 1.0
    lg = np.zeros((64, 140), np.float32)
    k = np.arange(64)
    q = np.arange(128)
    lg[:, 0:128] = ((k[:, None] % 2) == (q[None, :] // 64)).astype(np.float32)
    lg[:, 128:132] = (((k[:, None] % 8) // 2) == np.arange(4)[None, :]).astype(np.float32)
    lg[:, 132:140] = ((k[:, None] // 8) == np.arange(8)[None, :]).astype(np.float32)
    return cst, sel.reshape(16, 2048), lg


_PROG = {}


def kernel(x_prompt, x_sample, mem_prompt, cache_sc, state_ssd_conv, state_ssd, cache_mem_k, cache_mem_v,
           norm_mix, norm_mem_q, norm_mem_kv, norm_ffn, norm_final,
           sc_w_in, sc_w_conv, sc_w_out,
           ssd_w_in, ssd_conv_w, ssd_conv_b, ssd_dt_bias, ssd_a_log, ssd_d, ssd_norm, ssd_w_out,
           xa_w_q, xa_w_k, xa_w_v, xa_w_o, ffn_w_gate, ffn_w_up, ffn_w_down):
    f32 = np.float32
    A = lambda a: np.ascontiguousarray(np.asarray(a, f32))
    x_prompt = A(x_prompt); x_sample = A(x_sample); mem_prompt = A(mem_prompt)
    cache_sc = A(cache_sc); state_ssd_conv = A(state_ssd_conv); state_ssd = A(state_ssd)
    cache_mem_k = A(cache_mem_k); cache_mem_v = A(cache_mem_v)

    pvec = np.zeros((128, 512), f32)
    for l in range(2):
        pvec[:, PV_NMIX + 16 * l:PV_NMIX + 16 * (l + 1)] = _fm(norm_mix[l], 16)
        pvec[:, PV_NQ + 16 * l:PV_NQ + 16 * (l + 1)] = _fm(norm_mem_q[l], 16)
        pvec[:, PV_NKV + 16 * l:PV_NKV + 16 * (l + 1)] = _fm(norm_mem_kv[l], 16)
        pvec[:, PV_NFFN + 16 * l:PV_NFFN + 16 * (l + 1)] = _fm(norm_ffn[l], 16)
    pvec[:, PV_NFIN:PV_NFIN + 16] = _fm(norm_final, 16)
    for r in range(3):
        pvec[:, PV_SCW + 16 * r:PV_SCW + 16 * (r + 1)] = _fm(np.asarray(sc_w_conv)[0, r], 16)
    for r in range(4):
        pvec[:, PV_CW + 48 * r:PV_CW + 48 * (r + 1)] = _fm(np.asarray(ssd_conv_w)[0, r], 48)
    pvec[:, PV_CB:PV_CB + 48] = _fm(np.asarray(ssd_conv_b)[0], 48)
    pvec[:, PV_SNORM:PV_SNORM + 32] = _fm(np.asarray(ssd_norm)[0], 32)
    tokbc = np.zeros((128, 192), f32)
    tokbc[:, 0:64] = np.asarray(ssd_dt_bias, f32)[0][None, :]
    tokbc[:, 64:128] = np.asarray(ssd_a_log, f32)[0][None, :]
    tokbc[:, 128:192] = np.asarray(ssd_d, f32)[0][None, :]
    hvec = np.zeros((64, 8), f32)
    hvec[:, 0] = np.asarray(ssd_dt_bias, f32)[0]; hvec[:, 1] = np.asarray(ssd_a_log, f32)[0]; hvec[:, 2] = np.asarray(ssd_d, f32)[0]
    cst0, sel, lg = _consts()

    shared = {
        "pvec": pvec, "tokbc": tokbc, "hvec": hvec, "selb": sel, "lg": lg,
        "sc_w_in": A(sc_w_in)[0], "sc_w_out": A(sc_w_out)[0], "ssd_w_in": A(ssd_w_in)[0], "ssd_w_out": A(ssd_w_out)[0],
        "xa_w_q": A(xa_w_q), "xa_w_k": A(xa_w_k), "xa_w_v": A(xa_w_v), "xa_w_o": A(xa_w_o),
        "ffn_w_gate": A(ffn_w_gate), "ffn_w_up": A(ffn_w_up), "ffn_w_down": A(ffn_w_down),
    }
    in_maps = []
    for c in range(NCORES):
        seq, half = c // 2, c % 2
        st = half * NP
        cst = cst0.copy()
        cst[:, 1152] = float(half)
        xh = x_prompt[seq, st - NH:st] if half else np.zeros((NH, D), f32)
        sl = slice(NS * c, NS * (c + 1))
        m = dict(shared)
        m.update({
            "xp": np.ascontiguousarray(x_prompt[seq, st:st + NP]), "xh": np.ascontiguousarray(xh),
            "xs": np.ascontiguousarray(x_sample[sl, 0]), "mem": np.ascontiguousarray(mem_prompt[seq]),
            "csc": np.ascontiguousarray(cache_sc[0, sl].reshape(2 * NS, D)),
            "ccv": np.ascontiguousarray(state_ssd_conv[0, sl].reshape(3 * NS, CONV)),
            "sst": np.ascontiguousarray(state_ssd[0, sl].reshape(NS, 4096, 128)),
            "ck": np.ascontiguousarray(cache_mem_k[:, sl].reshape(2, NS, 256, D)),
            "cv": np.ascontiguousarray(cache_mem_v[:, sl].reshape(2, NS, 256, D)),
            "cst": cst,
        })
        in_maps.append(m)

    if "nc" not in _PROG:
        _PROG["nc"] = build_program()
    used = _PROG["nc"]._used_inputs
    in_maps = [{k: m[k] for k in used} for m in in_maps]
    res = run_bass_kernel_spmd(_PROG["nc"], in_maps, core_ids=list(range(NCORES)))
    R = res.results

    y_prompt = np.zeros((4, 2048, D), f32); y_sample = np.zeros((128, 1, D), f32)
    sc_p = np.zeros((1, 4, 2, D), f32); sc_s = np.zeros((1, 128, 2, D), f32)
    cv_p = np.zeros((1, 4, 3, CONV), f32); cv_s = np.zeros((1, 128, 3, CONV), f32)
    ss_p = np.zeros((1, 4, 64, 64, 128), f32); ss_s = np.zeros((1, 128, 64, 64, 128), f32)
    mk = np.zeros((2, 4, 256, 4, 512), f32); mv = np.zeros((2, 4, 256, 4, 512), f32)
    for c in range(NCORES):
        seq, half = c // 2, c % 2
        sl = slice(NS * c, NS * (c + 1))
        r = R[c]
        y_prompt[seq, half * NP:(half + 1) * NP] = r["y_p"]
        y_sample[sl, 0] = r["y_s"]
        sc_s[0, sl] = r["sc_s"].reshape(NS, 2, D)
        cv_s[0, sl] = r["cv_s"].reshape(NS, 3, CONV)
        ss_s[0, sl] = r["ss_s"].reshape(NS, 64, 64, 128)
        if half == 1:
            sc_p[0, seq] = r["sc_p"]
            cv_p[0, seq] = r["cv_p"]
            ss_p[0, seq] = r["ss_p"].reshape(64, 64, 128)
        else:
            mk[:, seq] = r["mk"].reshape(2, 256, 4, 512)
            mv[:, seq] = r["mv"].reshape(2, 256, 4, 512)
    return (y_prompt, y_sample, sc_p, sc_s, cv_p, cv_s, ss_p, ss_s, mk, mv)
```

```python
import numpy as np
import concourse.bass as bass
import concourse.mybir as mybir
from concourse.bass_utils import run_bass_kernel_spmd

F32 = mybir.dt.float32
BF16 = mybir.dt.bfloat16
AF = mybir.ActivationFunctionType
ALU = mybir.AluOpType
AX = mybir.AxisListType

D = 2048
KT = 16
NS, NH, NP = 16, 8, 1024
S0, H0, P0 = 0, 16, 24
NCOL = NS + NH + NP
BLKS = [(0, 24), (24, 512), (536, 512)]
DFF = 5632
DIN = 4096
CONV = 6144
EPS = 1e-5
NCORES = 8


class Op:
    __slots__ = ("eng", "idx", "fn", "waits", "signal", "dma", "rank", "real")

    def __init__(self, eng, idx, fn, dma):
        self.eng, self.idx, self.fn, self.dma = eng, idx, fn, dma
        self.waits = []
        self.signal = False
        self.rank = None
        self.real = True


class Sched:
    ENGS = ("pe", "act", "dve", "pool", "sp")

    def __init__(self):
        self.q = {e: [] for e in self.ENGS}
        self.lastw = {}
        self.readers = {}
        self.seen = {e: {} for e in self.ENGS}
        self.ndma = {"pool": 0, "sp": 0}
        self.NDS = 6

    def op(self, eng, fn, reads=(), writes=(), dma=False, real=True):
        o = Op(eng, len(self.q[eng]), fn, None)
        o.real = real
        deps = []
        for k in reads:
            w = self.lastw.get(k)
            if w is not None:
                deps.append((w, True))
        for k in writes:
            w = self.lastw.get(k)
            if w is not None:
                deps.append((w, False))
            for r in self.readers.get(k, ()):
                deps.append((r, False))
        if dma:
            i = self.ndma[eng]
            self.ndma[eng] += 1
            o.dma = (eng, i % self.NDS, 16 * (i // self.NDS + 1))
            if i >= self.NDS:
                o.waits.append(("dma", eng, i % self.NDS, 16 * (i // self.NDS)))
        for d, raw in deps:
            if d is o:
                continue
            if d.dma is not None:
                key = ("dma", d.dma[0], d.dma[1])
                if self.seen[eng].get(key, 0) >= d.dma[2]:
                    continue
                self.seen[eng][key] = d.dma[2]
                o.waits.append(("dma", d.dma[0], d.dma[1], d.dma[2]))
            else:
                if d.eng == eng and (eng == "pe" or not raw):
                    continue
                if self.seen[eng].get(d.eng, -1) >= d.idx:
                    continue
                self.seen[eng][d.eng] = d.idx
                d.signal = True
                o.waits.append(("op", d))
        for k in reads:
            self.readers.setdefault(k, []).append(o)
        for k in writes:
            self.lastw[k] = o
            self.readers[k] = []
        self.q[eng].append(o)
        return o

    def replay(self, nc, block, sems, dsems):
        for e in self.ENGS:
            r = 0
            for o in self.q[e]:
                if o.signal:
                    r += 1
                    o.rank = r
        names = {"pe": "tensor", "act": "scalar", "dve": "vector", "pool": "gpsimd", "sp": "sync"}
        fin = Op("sp", len(self.q["sp"]), (lambda e: None), None)
        fin.real = False
        for qn, n in self.ndma.items():
            for si in range(min(n, self.NDS)):
                last_i = ((n - 1 - si) // self.NDS) * self.NDS + si
                fin.waits.append(("dma", qn, si, 16 * (last_i // self.NDS + 1)))
        self.q["sp"].append(fin)

        def make(ename):
            def body(eng):
                for o in self.q[ename]:
                    for w in o.waits:
                        if w[0] == "dma":
                            eng.wait_ge(dsems[w[1]][w[2]], w[3])
                        else:
                            eng.wait_ge(sems[w[1].eng], w[1].rank)
                    ins = o.fn(eng)
                    if ins is None:
                        continue
                    if o.dma is not None:
                        ins.then_inc(dsems[o.dma[0]][o.dma[1]], 16)
                    elif o.signal:
                        ins.then_inc(sems[ename], 1)
            return body

        for ename in self.ENGS:
            getattr(block, names[ename])(make(ename))


PV_NMIX, PV_NQ, PV_NKV, PV_NFFN, PV_NFIN = 0, 32, 64, 96, 128
PV_SCW, PV_CW, PV_CB, PV_SNORM = 144, 192, 384, 432
NSLOT = 3
SLOTW = 4096


def build_program():
    import os
    nc = bass.Bass("TRN2", target_bir_lowering=False)
    S = Sched()
    from contextlib import ExitStack
    es = ExitStack()

    def din(name, shape):
        return nc.dram_tensor(name, list(shape), F32, kind="ExternalInput").ap()

    def dout(name, shape):
        return nc.dram_tensor(name, list(shape), F32, kind="ExternalOutput").ap()

    _in_shapes = {
        "xp": [NP, D], "xh": [NH, D], "xs": [NS, D], "mem": [256, D], "csc": [2 * NS, D], "ccv": [3 * NS, CONV],
        "sst": [NS, 4096, 128], "ck": [2, NS, 256, D], "cv": [2, NS, 256, D],
        "pvec": [128, 512], "tokbc": [128, 192], "cst": [128, 1160], "hvec": [64, 8], "selb": [16, 2048], "lg": [64, 140],
        "sc_w_in": [D, 3 * D], "sc_w_out": [D, D], "ssd_w_in": [D, 10304], "ssd_w_out": [DIN, D],
        "xa_w_q": [2, D, D], "xa_w_k": [2, D, D], "xa_w_v": [2, D, D], "xa_w_o": [2, D, D],
        "ffn_w_gate": [2, D, DFF], "ffn_w_up": [2, D, DFF], "ffn_w_down": [2, DFF, D],
    }
    _in_aps = {}

    def IN(name):
        if name not in _in_aps:
            _in_aps[name] = din(name, _in_shapes[name])
        return _in_aps[name]

    yp_o = dout("y_p", [NP, D]); ys_o = dout("y_s", [NS, D]); scp_o = dout("sc_p", [2, D]); scs_o = dout("sc_s", [2 * NS, D])
    cvp_o = dout("cv_p", [3, CONV]); cvs_o = dout("cv_s", [3 * NS, CONV]); ssp_o = dout("ss_p", [4096, 128])
    sss_o = dout("ss_s", [NS, 4096, 128]); mk_o = dout("mk", [2, 256, D]); mv_o = dout("mv", [2, 256, D])

    cc_src = nc.dram_tensor("cc_src", [1024, 512], F32, kind="Internal").ap()
    cc_dst = nc.dram_tensor("cc_dst", [2048, 512], F32, kind="Internal").ap()

    def sb(name, shape, dt=F32):
        return es.enter_context(nc.sbuf_tensor("sb_" + name, list(shape), dt))

    hT = sb("hT", [128, KT, NCOL])
    xn = sb("xn", [128, KT, NCOL], BF16)
    bB = sb("bufB", [128, KT, NCOL], BF16)
    wsl = [sb(f"wslot{i}", [128, SLOTW], BF16) for i in range(NSLOT)]
    pvec = sb("pvec", [128, 512]); tokbc = sb("tokbc", [128, 192]); cst = sb("cst", [128, 1160])
    hvec = sb("hvec", [64, 8]); selb = sb("selb", [16, 2048], BF16); lg = sb("lg", [64, 140])
    identb = sb("identb", [128, 128], BF16); onesb = sb("onesb", [128, 128], BF16)
    ARW = int(os.environ.get("ARW", "10160")) if True else 0
    ar = sb("arena", [128, ARW])
    ps = es.enter_context(nc.psum_tensor("ps", [128, 6, 512], F32))
    psb = es.enter_context(nc.psum_tensor("psb", [128, 2, 1024], BF16))

    xnf = xn[:].rearrange("p k n -> p (k n)")
    bBf = bB[:].rearrange("p k n -> p (k n)")
    XN_ALL = [("xn", k, bi) for k in range(KT) for bi in range(3)]
    BB_ALL = [("bB", k, bi) for k in range(KT) for bi in range(3)]

    def arf(o, n):
        return ar[:, o:o + n]

    def arb(o, n):
        return ar[:, o:o + n].bitcast(BF16)

    sq = [arb(0, 256), arb(256, 256)]
    rstd = arf(512, 512)
    sm = arf(1024, 512)
    stg = [arf(1536, 2048), arf(3584, 2048)]
    PX = 5632
    PXN = ARW - PX

    ident = cst[:, 0:128]; tri = cst[:, 128:256]; ones = cst[:, 256:384]; mneg4 = cst[:, 384:896]
    d16 = cst[:, 896:1152].rearrange("p (b c) -> p b c", b=NS)
    flag = cst[:, 1152:1153]

    state = {"bank": 0, "pb": 0, "slot": 0, "stg": 0}
    outkeys = []

    def bank():
        b = state["bank"]; state["bank"] = (b + 1) % 6
        return b

    def pbank():
        b = state["pb"]; state["pb"] = (b + 1) % 2
        return b

    def mm(out, lhsT, rhs, start, stop, reads, writes):
        S.op("pe", lambda e: e.matmul(out, lhsT=lhsT, rhs=rhs, start=start, stop=stop), reads, writes)

    def tr(out, in_, idn, reads, writes):
        S.op("pe", lambda e: e.transpose(out, in_, idn), reads, writes)

    def act(out, in_, func, reads, writes, bias=None, scale=None, accum=None):
        kw = {}
        if bias is not None: kw["bias"] = bias
        if scale is not None: kw["scale"] = scale
        if accum is not None: kw["accum_out"] = accum
        S.op("act", lambda e: e.activation(out=out, in_=in_, func=func, **kw), reads, writes)

    def tt(out, a, b, op, reads, writes, eng="dve"):
        S.op(eng, lambda e: e.tensor_tensor(out=out, in0=a, in1=b, op=op), reads, writes)

    def ts(out, a, s1, s2, op0, op1, reads, writes, eng="dve"):
        if op1 is None:
            S.op(eng, lambda e: e.tensor_scalar(out=out, in0=a, scalar1=s1, scalar2=None, op0=op0), reads, writes)
        else:
            S.op(eng, lambda e: e.tensor_scalar(out=out, in0=a, scalar1=s1, scalar2=s2, op0=op0, op1=op1), reads, writes)

    def stt(out, a, s, b, op0, op1, reads, writes):
        S.op("dve", lambda e: e.scalar_tensor_tensor(out=out, in0=a, scalar=s, in1=b, op0=op0, op1=op1), reads, writes)

    def cp(out, in_, reads, writes, eng="dve"):
        if eng == "act":
            S.op("act", lambda e: e.copy(out=out, in_=in_), reads, writes)
        else:
            S.op(eng, lambda e: e.tensor_copy(out=out, in_=in_), reads, writes)

    def vop(fn, reads, writes):
        S.op("dve", fn, reads, writes)

    def dma(q, out, in_, reads, writes):
        S.op(q, lambda e: e.dma_start(out=out, in_=in_), reads, writes, dma=True)

    cpi = [0]

    def cp_alt(out, in_, reads, writes):
        cpi[0] += 1
        cp(out, in_, reads, writes, eng=("act" if cpi[0] % 2 else "dve"))

    def barrier():
        engs = ("pe", "act", "dve")
        lasts = {}
        for e in engs:
            for o_ in reversed(S.q[e]):
                if o_.real:
                    lasts[e] = o_
                    break
        for e in engs:
            o = S.op(e, lambda eng: None, real=False)
            for e2, l in lasts.items():
                if e2 != e and S.seen[e].get(e2, -1) < l.idx:
                    S.seen[e][e2] = l.idx
                    l.signal = True
                    o.waits.append(("op", l))

    dma("sp", pvec[:], IN("pvec"), [], ["pvec"]); dma("sp", tokbc[:], IN("tokbc"), [], ["tokbc"]); dma("sp", cst[:], IN("cst"), [], ["cst"])
    dma("sp", hvec[:], IN("hvec"), [], ["hvec"]); dma("sp", lg[:], IN("lg"), [], ["lg"])
    dma("pool", selb[:], IN("selb"), [], ["selb"])
    import os
    KDBG = int(os.environ.get("KDBG", "9"))
    cp(identb[:], ident, ["cst"], ["identb"]); cp(onesb[:], ones, ["cst"], ["onesb"])
    if KDBG >= 1:
        act(hvec[:, 3:4], hvec[:, 1:2], AF.Exp, ["hvec"], ["hvec_a"])
        ts(hvec[:, 3:4], hvec[:, 3:4], -1.0, None, ALU.mult, None, ["hvec_a"], ["hvec_a"])
        act(tokbc[:, 64:128], tokbc[:, 64:128], AF.Exp, ["tokbc"], ["tokbc"])
        ts(tokbc[:, 64:128], tokbc[:, 64:128], -1.0, None, ALU.mult, None, ["tokbc"], ["tokbc"])
    a_bc = tokbc[:, 64:128]; d_bc = tokbc[:, 128:192]

    def in_transpose(rows_ap, R, C, dst, dst_keys_fn):
        si = state["stg"]; state["stg"] ^= 1
        st = stg[si]
        dma("sp", st[0:R, 0:C], rows_ap, [], [("stg", si)])
        nct = C // 128
        for c0 in range(0, nct, 4):
            n = min(4, nct - c0)
            b = bank()
            for j in range(n):
                tr(ps[:, b, j * 128:j * 128 + R], st[0:R, (c0 + j) * 128:(c0 + j + 1) * 128], ident[0:R, 0:R],
                   [("stg", si), "cst"], [("ps", b)])
            cpi[0] += 1
            eng_ = "act" if cpi[0] % 2 else "dve"
            for j in range(n):
                cp(dst(c0 + j), ps[:, b, j * 128:j * 128 + R], [("ps", b)], dst_keys_fn(c0 + j), eng=eng_)

    def out_transpose(src, src_keys_fn, R, C, out_ap, okey):
        si = state["stg"]; state["stg"] ^= 1
        st = stg[si]
        nct = C // 128
        for c0 in range(0, nct, 4):
            n = min(4, nct - c0)
            b = bank()
            for j in range(n):
                tr(ps[0:R, b, j * 128:(j + 1) * 128], src(c0 + j), ident, list(src_keys_fn(c0 + j)) + ["cst"], [("ps", b)])
            cp_alt(st[0:R, c0 * 128:(c0 + n) * 128], ps[0:R, b, 0:n * 128], [("ps", b)], [("stg", si)])
        dma("sp", out_ap, st[0:R, 0:C], [("stg", si)], [okey])
        outkeys.append(okey)

    def hkeys(kt, bi):
        return ("hT", kt, bi)

    def rmsnorm_cols(src, src_key, dstf, dst_key, gcol, ntile, blocks, scale_div):
        for bi, (c0, n) in blocks:
            b = bank()
            for kt in range(ntile):
                si = kt % 2
                act(sq[si][:, 0:n], src(kt, c0, n), AF.Square, [src_key(kt, bi)], [("sq", si)])
                mm(ps[:, b, 0:n], onesb[:], sq[si][:, 0:n], kt == 0, kt == ntile - 1, [("sq", si), "onesb"], [("ps", b)])
            ts(rstd[:, 0:n], ps[:, b, 0:n], 1.0 / scale_div, EPS, ALU.mult, ALU.add, [("ps", b)], ["rstd"])
            act(rstd[:, 0:n], rstd[:, 0:n], AF.Ln, ["rstd"], ["rstd"])
            act(rstd[:, 0:n], rstd[:, 0:n], AF.Exp, ["rstd"], ["rstd"], scale=-0.5)
            for kt in range(ntile):
                stt(dstf(kt, c0, n), src(kt, c0, n), pvec[:, gcol + kt:gcol + kt + 1], rstd[:, 0:n], ALU.mult, ALU.mult,
                    [src_key(kt, bi), "rstd", "pvec"], [dst_key(kt, bi)])

    EB = list(enumerate(BLKS))

    def norm_h(gcol):
        rmsnorm_cols(lambda kt, c0, n: hT[:, kt, c0:c0 + n], hkeys, lambda kt, c0, n: xn[:, kt, c0:c0 + n],
                     lambda kt, bi: ("xn", kt, bi), gcol, KT, EB, float(D))

    def load_w(pieces, nk):
        si = state["slot"]; state["slot"] = (si + 1) % NSLOT
        tot = sum(p.shape[1] for p in pieces)
        assert nk * tot <= SLOTW, (nk, tot)
        view = wsl[si][:, 0:nk * tot].rearrange("p (k n) -> p k n", k=nk)
        off = 0
        offs = []
        for p in pieces:
            n = p.shape[1]
            dma("pool", view[:, :, off:off + n], p.rearrange("(k p) n -> p k n", p=128), [], [("w", si)])
            offs.append(off); off += n
        return si, view, offs

    def proj_fm(view, si, col0, nk, rhs, rhs_keys, blocks, evac, M=128):
        for bi, (c0, n) in blocks:
            b = bank()
            for k in range(nk):
                mm(ps[0:M, b, 0:n], view[:, k, col0:col0 + M], rhs(k, c0, n), k == 0, k == nk - 1,
                   [("w", si)] + list(rhs_keys(k, bi)), [("ps", b)])
            evac(bi, c0, n, ps[0:M, b, 0:n], ("ps", b))

    def xn_rhs(k, c0, n):
        return xn[:, k, c0:c0 + n]

    def xn_keys(k, bi):
        return [("xn", k, bi)]

    def bB_rhs(k, c0, n):
        return bB[:, k, c0:c0 + n]

    def bB_keys(k, bi):
        return [("bB", k, bi)]

    def add_to_h(nt):
        def ev(bi, c0, n, p, pk):
            tt(hT[:, nt, c0:c0 + n], hT[:, nt, c0:c0 + n], p, ALU.add, [hkeys(nt, bi), pk], [hkeys(nt, bi)])
        return ev

    def out_proj(wfn, rhs, rhs_keys, nk, k0=0):
        ncc = SLOTW // nk // 128
        ncc = min(ncc, 4)
        for c in range(0, 16, ncc):
            si, view, offs = load_w([wfn(c * 128, ncc * 128)], nk)
            for t in range(ncc):
                proj_fm(view, si, t * 128, nk, rhs, rhs_keys, EB, add_to_h(c + t))

    def all_h_keys(ct):
        return [hkeys(ct, 0), hkeys(ct, 1), hkeys(ct, 2)]

    if KDBG >= 2:
        in_transpose(IN("xs"), NS, D, lambda ct: hT[:, ct, S0:S0 + NS], all_h_keys)
    if KDBG >= 3:
        in_transpose(IN("xh"), NH, D, lambda ct: hT[:, ct, H0:H0 + NH], all_h_keys)
    for i in range(8 if KDBG >= 4 else 0):
        in_transpose(IN("xp")[i * 128:(i + 1) * 128, :], 128, D, (lambda ct, i=i: hT[:, ct, P0 + i * 128:P0 + (i + 1) * 128]), all_h_keys)

    def mixer_sc():
        norm_h(PV_NMIX + 0)
        hist = arf(PX, 512).rearrange("p (k r) -> p k r", k=KT)
        uo_p = arf(PX + 512, 32).rearrange("p (k r) -> p k r", k=KT)
        uo_s = arf(PX + 544, 512).rearrange("p (k r) -> p k r", k=KT)
        u = arf(PX + 1056, NCOL); bg = arb(PX + 2104, NCOL // 2); acc = arf(PX + 2628, NCOL)
        assert 2628 + NCOL <= PXN
        in_transpose(IN("csc"), 2 * NS, D, lambda ct: hist[:, ct, :], lambda ct: ["hist"])
        for j in range(KT):
            for which in range(3):
                si, view, offs = load_w([IN("sc_w_in")[:, which * D + j * 128: which * D + (j + 1) * 128]], KT)
                if which == 0:
                    proj_fm(view, si, 0, KT, xn_rhs, xn_keys, EB, lambda bi, c0, n, p, pk: cp(bg[:, c0:c0 + n], p, [pk], ["bg"], eng="act"))
                elif which == 1:
                    proj_fm(view, si, 0, KT, xn_rhs, xn_keys, EB, lambda bi, c0, n, p, pk: cp(u[:, c0:c0 + n], p, [pk], ["u"], eng="act"))
                else:
                    proj_fm(view, si, 0, KT, xn_rhs, xn_keys, EB, lambda bi, c0, n, p, pk: tt(u[:, c0:c0 + n], u[:, c0:c0 + n], p, ALU.mult, ["u", pk], ["u"]))
            w0 = pvec[:, PV_SCW + j:PV_SCW + j + 1]; w1 = pvec[:, PV_SCW + 16 + j:PV_SCW + 17 + j]; w2 = pvec[:, PV_SCW + 32 + j:PV_SCW + 33 + j]
            L = NCOL - 18
            ts(acc[:, 18:NCOL], u[:, 16:16 + L], w0, None, ALU.mult, None, ["u", "pvec"], ["acc"])
            stt(acc[:, 18:NCOL], u[:, 17:17 + L], w1, acc[:, 18:NCOL], ALU.mult, ALU.add, ["u", "acc"], ["acc"])
            stt(acc[:, 18:NCOL], u[:, 18:18 + L], w2, acc[:, 18:NCOL], ALU.mult, ALU.add, ["u", "acc"], ["acc"])
            hv = hist[:, j, :].rearrange("p (b r) -> p b r", r=2)
            ts(acc[:, 0:NS], hv[:, :, 0], w0, None, ALU.mult, None, ["hist", "pvec"], ["acc"])
            stt(acc[:, 0:NS], hv[:, :, 1], w1, acc[:, 0:NS], ALU.mult, ALU.add, ["hist", "acc"], ["acc"])
            stt(acc[:, 0:NS], u[:, 0:NS], w2, acc[:, 0:NS], ALU.mult, ALU.add, ["u", "acc"], ["acc"])
            vop(lambda e, j=j: e.memset(bB[:, j, 16:18], 0.0), [], [("bB", j, 0)])
            tt(bB[:, j, 18:NCOL], acc[:, 18:NCOL], bg[:, 18:NCOL], ALU.mult, ["acc", "bg"], [("bB", j, 0), ("bB", j, 1), ("bB", j, 2)])
            tt(bB[:, j, 0:NS], acc[:, 0:NS], bg[:, 0:NS], ALU.mult, ["acc", "bg"], [("bB", j, 0)])
            cp(uo_p[:, j, :], u[:, NCOL - 2:NCOL], ["u"], ["uo_p"], eng="act")
            uov = uo_s[:, j, :].rearrange("p (b r) -> p b r", r=2)
            cp(uov[:, :, 0], hv[:, :, 1], ["hist"], ["uo_s"], eng="act")
            cp(uov[:, :, 1], u[:, 0:NS], ["u"], ["uo_s"], eng="act")
        out_transpose(lambda ct: uo_p[:, ct, :], lambda ct: ["uo_p"], 2, D, scp_o, "o_scp")
        out_transpose(lambda ct: uo_s[:, ct, :], lambda ct: ["uo_s"], 2 * NS, D, scs_o, "o_scs")
        out_proj(lambda c0, n: IN("sc_w_out")[:, c0:c0 + n], bB_rhs, bB_keys, KT)

    def xattn(layer, with_halo):
        barrier()
        memT = bBf[:, 0:8192].bitcast(F32).rearrange("p (k m) -> p k m", k=KT)
        mn = xnf[:, 0:4096].rearrange("p (k m) -> p k m", k=KT)
        KTb = arb(PX, 2048).rearrange("p (k m) -> p k m", k=KT)
        Vt = arb(PX + 2048, 2048).rearrange("p (m n) -> p m n", m=2)
        for mt in range(2):
            in_transpose(IN("mem")[mt * 128:(mt + 1) * 128, :], 128, D, (lambda ct, mt=mt: memT[:, ct, mt * 128:(mt + 1) * 128]),
                         lambda ct: [("memT", ct)] + BB_ALL[0:1])
        rmsnorm_cols(lambda kt, c0, n: memT[:, kt, c0:c0 + n], lambda kt, bi: ("memT", kt),
                     lambda kt, c0, n: mn[:, kt, c0:c0 + n], lambda kt, bi: ("mn", kt),
                     PV_NKV + 16 * layer, KT, [(0, (0, 256))], float(D))
        for which, wmat, oap in (("k", IN("xa_w_k"), mk_o), ("v", IN("xa_w_v"), mv_o)):
            for c in range(8):
                si, view, offs = load_w([wmat[layer, :, c * 256:(c + 1) * 256]], KT)
                for mt in range(2):
                    b = bank()
                    for k in range(KT):
                        mm(ps[:, b, 0:256], mn[:, k, mt * 128:(mt + 1) * 128], view[:, k, :], k == 0, k == KT - 1,
                           [("w", si), ("mn", k)], [("ps", b)])
                    sti = state["stg"]; state["stg"] ^= 1
                    cp(stg[sti][:, 0:256], ps[:, b, 0:256], [("ps", b)], [("stg", sti)], eng="act")
                    okey = ("o_m" + which, layer, c, mt)
                    dma("sp", oap[layer, mt * 128:(mt + 1) * 128, c * 256:(c + 1) * 256], stg[sti][:, 0:256], [("stg", sti)], [okey])
                    outkeys.append(okey)
                    if which == "v":
                        cp(Vt[:, mt, c * 256:(c + 1) * 256], stg[sti][:, 0:256], [("stg", sti)], ["Vt"])
                if which == "k":
                    for t in range(2):
                        b = bank()
                        for k in range(KT):
                            mm(ps[:, b, 0:256], view[:, k, t * 128:(t + 1) * 128], mn[:, k, :], k == 0, k == KT - 1,
                               [("w", si), ("mn", k)], [("ps", b)])
                        cp(KTb[:, c * 2 + t, :], ps[:, b, 0:256], [("ps", b)], [("KTb", c * 2 + t)])
        barrier()
        norm_h(PV_NQ + 16 * layer)
        qs = float(512 ** -0.5)
        for c in range(8):
            si, view, offs = load_w([IN("xa_w_q")[layer, :, c * 256:(c + 1) * 256]], KT)
            for t in range(2):
                nt = c * 2 + t
                proj_fm(view, si, t * 128, KT, xn_rhs, xn_keys, EB,
                        lambda bi, c0, n, p, pk, nt=nt: S.op("act", lambda e: e.mul(bB[:, nt, c0:c0 + n], p, qs),
                                                             [pk], [("bB", nt, bi)]))
        barrier()
        tiles = [(P0 + 128 * i, 128, 1 + (i // 4)) for i in range(8)]
        if with_halo:
            tiles.append((H0, NH, 0))
        Pf = sm[:, 0:256]; Pn = sm[:, 256:384].bitcast(BF16)
        PT = sm[:, 384:500].bitcast(BF16)[:, 0:256].rearrange("p (m t) -> p m t", m=2) if False else xnf[:, 0:256].rearrange("p (m t) -> p m t", m=2)
        st1 = sm[:, 500:504]
        for h in range(4):
            for (c0, nt_, bi) in tiles:
                b = bank()
                for j in range(4):
                    mm(ps[0:nt_, b, 0:256], bB[:, 4 * h + j, c0:c0 + nt_], KTb[:, 4 * h + j, :], j == 0, j == 3,
                       [("bB", 4 * h + j, bi), ("KTb", 4 * h + j)], [("ps", b)])
                vop(lambda e, b=b, nt_=nt_: e.tensor_reduce(out=st1[0:nt_, 0:1], in_=ps[0:nt_, b, 0:256], axis=AX.X, op=ALU.max), [("ps", b)], ["st1a"])
                ts(st1[0:nt_, 1:2], st1[0:nt_, 0:1], -1.0, None, ALU.mult, None, ["st1a"], ["st1b"])
                act(Pf[0:nt_, :], ps[0:nt_, b, 0:256], AF.Exp, [("ps", b), "st1b"], ["Pf", "st1c"], bias=st1[0:nt_, 1:2], accum=st1[0:nt_, 2:3])
                vop(lambda e, nt_=nt_: e.reciprocal(out=st1[0:nt_, 3:4], in_=st1[0:nt_, 2:3]), ["st1c"], ["st1d"])
                ts(Pn[0:nt_, :], Pf[0:nt_, :], st1[0:nt_, 3:4], None, ALU.mult, None, ["Pf", "st1d"], ["Pn"])
                pb = pbank()
                for mt in range(2):
                    tr(psb[:, pb, mt * 128:mt * 128 + nt_], Pn[0:nt_, mt * 128:(mt + 1) * 128], identb[0:nt_, 0:nt_], ["Pn", "identb"], [("psb", pb)])
                for mt in range(2):
                    cp(PT[:, mt, 0:nt_], psb[:, pb, mt * 128:mt * 128 + nt_], [("psb", pb)], ["PT"], eng="act")
                for j in range(4):
                    b2 = bank()
                    for mt in range(2):
                        mm(ps[:, b2, 0:nt_], Vt[:, mt, (4 * h + j) * 128:(4 * h + j + 1) * 128], PT[:, mt, 0:nt_], mt == 0, mt == 1,
                           ["Vt", "PT"], [("ps", b2)])
                    cp_alt(bB[:, 4 * h + j, c0:c0 + nt_], ps[:, b2, 0:nt_], [("ps", b2)], [("bB", 4 * h + j, bi)])
        qtok = xnf[0:16, 512:512 + 2048]
        kvst = [xnf[:, 4096 + i * 4096:4096 + (i + 1) * 4096].bitcast(F32) for i in range(2)]
        Pz = xnf[:, 12288:12288 + 4096].bitcast(F32).rearrange("p (f h b c) -> p f h b c", f=2, h=4, b=NS)
        scs = arf(PX, 128).rearrange("p (f b h) -> p f b h", f=2, b=NS)
        junk = arf(PX + 128, 512)
        sp_ = ar[0:64, PX + 640:PX + 896]
        Pm = arf(PX + 896, 128).rearrange("p (f b h) -> p f b h", f=2, b=NS)
        otok = ar[0:16, PX + 1024:PX + 3072]
        st2 = sm[0:64, 504:508]
        barrier()
        for c4 in range(4):
            pb = pbank()
            for j in range(4):
                tr(psb[0:16, pb, j * 128:(j + 1) * 128], bB[:, c4 * 4 + j, 0:NS], identb[:], [("bB", c4 * 4 + j, 0), "identb"], [("psb", pb)])
            cp(qtok[:, c4 * 512:(c4 + 1) * 512], psb[0:16, pb, 0:512], [("psb", pb)], ["qtok"])
        kv = [stg[0], stg[1], kvst[0], kvst[1]]
        kvk = [("stg", 0), ("stg", 1), "kvst0", "kvst1"]
        for bsmp in range(NS):
            o2 = (bsmp % 2) * 2
            for f in range(2):
                dma("sp", kv[o2 + f], IN("ck")[layer, bsmp, f * 128:(f + 1) * 128, :], [], [kvk[o2 + f]] + (XN_ALL if (bsmp < 2 and o2 == 2) else []))
            for h in range(4):
                b = bank()
                mm(ps[:, b, :], selb[:, bsmp * 128:(bsmp + 1) * 128], qtok[:, h * 512:(h + 1) * 512], True, True, ["selb", "qtok"], [("ps", b)])
                for f in range(2):
                    vop(lambda e, b=b, f=f, h=h, o2=o2, bsmp=bsmp: e.scalar_tensor_tensor(
                        out=junk, in0=kv[o2 + f][:, h * 512:(h + 1) * 512], scalar=1.0, in1=ps[:, b, :],
                        op0=ALU.mult, op1=ALU.mult, accum_out=scs[:, f, bsmp, h:h + 1]),
                        [kvk[o2 + f], ("ps", b)], ["junk", "scs"])
        b = bank()
        for f in range(2):
            tr(ps[0:64, b, f * 128:(f + 1) * 128], arf(PX + 64 * f, 64), ident, ["scs", "cst"], [("ps", b)])
        vop(lambda e, b=b: e.tensor_reduce(out=st2[:, 0:1], in_=ps[0:64, b, 0:256], axis=AX.X, op=ALU.max), [("ps", b)], ["st2a"])
        ts(st2[:, 1:2], st2[:, 0:1], -1.0, None, ALU.mult, None, ["st2a"], ["st2b"])
        act(sp_, ps[0:64, b, 0:256], AF.Exp, [("ps", b), "st2b"], ["sp_", "st2c"], bias=st2[:, 1:2], accum=st2[:, 2:3])
        vop(lambda e: e.reciprocal(out=st2[:, 3:4], in_=st2[:, 2:3]), ["st2c"], ["st2d"])
        ts(sp_, sp_, st2[:, 3:4], None, ALU.mult, None, ["sp_", "st2d"], ["sp_"])
        b = bank()
        for f in range(2):
            tr(ps[:, b, f * 64:(f + 1) * 64], sp_[:, f * 128:(f + 1) * 128], ident[0:64, 0:64], ["sp_", "cst"], [("ps", b)])
        cp(arf(PX + 896, 128), ps[:, b, 0:128], [("ps", b)], ["Pm"])
        for f in range(2):
            for h in range(4):
                tt(Pz[:, f, h, :, :], Pm[:, f, :, h].unsqueeze(2).to_broadcast([128, NS, NS]), d16, ALU.mult, ["Pm", "cst"], ["Pz"])
        ob = [bank() for _ in range(4)]
        for bsmp in range(NS):
            o2 = (bsmp % 2) * 2
            for f in range(2):
                dma("sp", kv[o2 + f], IN("cv")[layer, bsmp, f * 128:(f + 1) * 128, :], [], [kvk[o2 + f]])
            for h in range(4):
                for f in range(2):
                    mm(ps[0:16, ob[h], :], Pz[:, f, h, bsmp, :], kv[o2 + f][:, h * 512:(h + 1) * 512], bsmp == 0 and f == 0, bsmp == NS - 1 and f == 1,
                       ["Pz", kvk[o2 + f]], [("ps", ob[h])])
        for h in range(4):
            cp_alt(otok[:, h * 512:(h + 1) * 512], ps[0:16, ob[h], :], [("ps", ob[h])], ["otok"])
        for c4 in range(4):
            b = bank()
            for j in range(4):
                tr(ps[:, b, j * 16:(j + 1) * 16], otok[:, (c4 * 4 + j) * 128:(c4 * 4 + j + 1) * 128], ident[0:16, 0:16], ["otok", "cst"], [("ps", b)])
            for j in range(4):
                cp(bB[:, c4 * 4 + j, 0:NS], ps[:, b, j * 16:(j + 1) * 16], [("ps", b)], [("bB", c4 * 4 + j, 0)])
        out_proj(lambda c0, n: IN("xa_w_o")[layer, :, c0:c0 + n], bB_rhs, bB_keys, KT)
        barrier()

    def ffn(layer):
        norm_h(PV_NFFN + 16 * layer)
        gs = [arb(PX, NCOL // 2), arb(PX + 600, NCOL // 2)]
        f0 = 0
        while f0 < 44:
            nf = min(8, 44 - f0)
            for j0 in range(0, nf, 2):
                cg = (f0 + j0) * 128
                sg, vg, _ = load_w([IN("ffn_w_gate")[layer, :, cg:cg + 256]], KT)
                su, vu, _ = load_w([IN("ffn_w_up")[layer, :, cg:cg + 256]], KT)
                for t in range(2):
                    jj = j0 + t
                    gbuf = gs[jj % 2]; gk = ("gs", jj % 2)
                    proj_fm(vg, sg, t * 128, KT, xn_rhs, xn_keys, EB,
                            lambda bi, c0, n, p, pk, gbuf=gbuf, gk=gk: act(gbuf[:, c0:c0 + n], p, AF.Silu, [pk], [gk]))
                    proj_fm(vu, su, t * 128, KT, xn_rhs, xn_keys, EB,
                            lambda bi, c0, n, p, pk, gbuf=gbuf, gk=gk, jj=jj: tt(bB[:, jj, c0:c0 + n], gbuf[:, c0:c0 + n], p, ALU.mult, [gk, pk], [("bB", jj, bi)]))
            out_proj(lambda c0, n, f0=f0, nf=nf: IN("ffn_w_down")[layer, f0 * 128:(f0 + nf) * 128, c0:c0 + n], bB_rhs, bB_keys, nf)
            f0 += nf

    def mixer_ssd():
        barrier()
        norm_h(PV_NMIX + 16)
        for kt in range(KT):
            ts(xn[:, kt, H0:H0 + NH], xn[:, kt, H0:H0 + NH], flag, None, ALU.mult, None, [("xn", kt, 0), "cst"], [("xn", kt, 0)])
        xT = bB[:, 0:4, :]; BT = bB[:, 4, :]; CT = bB[:, 5, :]
        xtok = bBf[:, 6 * NCOL:6 * NCOL + 4096].rearrange("p (c n) -> p c n", c=8)
        Btok = bB[:, 10, 0:1024].rearrange("p (c n) -> p c n", c=8)
        EW = [bB[:, 11 + i, 0:1024].rearrange("p (h t) -> p h t", h=8) for i in range(3)]
        xdtb = [bB[:, 15, 0:512].rearrange("p (h d) -> p h d", h=8), bB[:, 15, 512:1024].rearrange("p (h d) -> p h d", h=8),
                bB[:, 14, 0:512].rearrange("p (h d) -> p h d", h=8)]
        o = PX
        raw = arf(o, NCOL); o += NCOL
        R = arf(o, 1024).rearrange("p (h t) -> p h t", h=8); o += 1024
        Sst = arf(o, 512); o += 512
        Sbf = arb(o, 256); o += 256
        tmpA = arf(o, 512); o += 512
        xdte = arb(o, 256).rearrange("p (h d) -> p h d", h=8); o += 256
        CBTb = [arb(o, 64), arb(o + 64, 64), bB[:, 14, 512:640]]; o += 128
        xs_conv = arb(o, 256).rearrange("p (j b) -> p j b", j=32); o += 256
        y_s = arf(o, 512).rearrange("p (j b) -> p j b", j=32); o += 512
        assert o <= ARW, o
        o = 1024
        tstat = arf(o, 16); o += 16
        cs_sb = [arf(o, 32), arf(o + 32, 32), arf(o + 64, 32)]; o += 96
        ncs = [arf(o, 8), arf(o + 8, 8), arf(o + 16, 8)]; o += 24
        dq = arf(o, 132); o += 132
        xdts = arf(o, 64).rearrange("p (b j) -> p b j", b=NS); o += 64
        dAs = arf(o, 16); o += 16
        bcs = arb(o, 128); o += 128
        assert o <= 1536, o
        o = 1536
        dt_tok = arf(o, 512).rearrange("p (c h) -> p c h", c=8); o += 512
        dta_tok = arf(o, 512).rearrange("p (c h) -> p c h", c=8); o += 512
        gg = arf(o, 512); o += 512
        gnb = [arb(o, 256), arb(o + 256, 256)]; o += 512
        histg = arf(o, 288).rearrange("p (t r) -> p t r", t=6); o += 288
        cvo_s = arf(o, 288).rearrange("p (t r) -> p t r", t=6); o += 288
        cvo_p = arf(o, 18).rearrange("p (t r) -> p t r", t=6); o += 18
        BsT = arb(o, 64).rearrange("p (g b) -> p g b", g=8); o += 64
        CsT = arb(o, 64).rearrange("p (g b) -> p g b", g=8); o += 64
        rallb = arf(o, 132); o += 132
        rg = arf(o, 132); o += 132
        stS = [arf(o, 512).rearrange("p (j n) -> p j n", j=4), arf(o + 512, 512).rearrange("p (j n) -> p j n", j=4)]
        cstg = arf(o, 512)
        o += 1024
        assert o <= 5632, o
        smid = arf(PX + NCOL, 512)
        t1 = tmpA.rearrange("p (j n) -> p j n", j=4)
        def fence():
            vop(lambda e: e.memset(tstat[:, 0:1], 0.0), [], [("stg", 0), ("stg", 1), "kvst0", "kvst1", "fence"])
            barrier()

        fence()

        dtT = ar[0:64, 3584 + 200:3584 + 200 + NCOL] if False else raw[0:64, :]
        sdt, vdt, _ = load_w([IN("ssd_w_in")[:, 10240:10304]], KT)
        for bi, (c0, n) in EB:
            b = bank()
            for k in range(KT):
                mm(ps[0:64, b, 0:n], vdt[:, k, 0:64], xn[:, k, c0:c0 + n], k == 0, k == KT - 1, [("w", sdt), ("xn", k, bi)], [("ps", b)])
            act(dtT[:, c0:c0 + n], ps[0:64, b, 0:n], AF.Exp, [("ps", b), "hvec"], ["dtT"], bias=hvec[:, 0:1])
        ts(dtT, dtT, 1.0, None, ALU.add, None, ["dtT"], ["dtT"])
        act(dtT, dtT, AF.Ln, ["dtT"], ["dtT"])
        for c4 in range(2):
            b = bank()
            for j in range(4):
                c = c4 * 4 + j
                tr(ps[:, b, j * 64:(j + 1) * 64], dtT[:, P0 + c * 128:P0 + (c + 1) * 128], ident[0:64, 0:64], ["dtT", "cst"], [("ps", b)])
            cp(ar[:, 1536 + c4 * 256:1536 + (c4 + 1) * 256], ps[:, b, 0:256], [("ps", b)], ["dt_tok"])
        tt(dta_tok, dt_tok, a_bc.unsqueeze(1).to_broadcast([128, 8, 64]), ALU.mult, ["dt_tok", "tokbc"], ["dta_tok"])
        M2 = lg[:, 128:132]
        dts = dtT[:, 0:NS]
        act(dAs[0:64, :], dts, AF.Exp, ["dtT", "hvec_a"], ["dAs"], scale=hvec[:, 3:4])
        tt(rallb[0:64, 0:64].rearrange("p (b j) -> p b j", b=NS), dts.unsqueeze(2).to_broadcast([64, NS, 4]), M2.unsqueeze(1).to_broadcast([64, NS, 4]), ALU.mult, ["dtT", "lg"], ["rall"])
        tt(rallb[0:64, 64:128].rearrange("p (b j) -> p b j", b=NS), dAs[0:64, :].unsqueeze(2).to_broadcast([64, NS, 4]), M2.unsqueeze(1).to_broadcast([64, NS, 4]), ALU.mult, ["dAs", "lg"], ["rall"])
        ts(rallb[0:64, 128:132], M2, hvec[:, 2:3], None, ALU.mult, None, ["lg", "hvec"], ["rall"])
        barrier()

        XCH0 = 0; BCH0 = 4096; CCH0 = 5120

        def do_conv(dst_tile, ct, hist_t, full, ck_):
            w = [pvec[:, PV_CW + 48 * r + ct:PV_CW + 48 * r + ct + 1] for r in range(4)]
            bia = pvec[:, PV_CB + ct:PV_CB + ct + 1]
            Rf = arf(PX + NCOL, 1024)
            for (c0, n) in ((19, 517), (536, 512)):
                a = Rf[:, 0:n]
                ts(a, raw[:, c0 - 3:c0 - 3 + n], w[0], None, ALU.mult, None, ["raw", "pvec"], ["R"])
                stt(a, raw[:, c0 - 2:c0 - 2 + n], w[1], a, ALU.mult, ALU.add, ["raw", "R"], ["R"])
                stt(a, raw[:, c0 - 1:c0 - 1 + n], w[2], a, ALU.mult, ALU.add, ["raw", "R"], ["R"])
                stt(a, raw[:, c0:c0 + n], w[3], a, ALU.mult, ALU.add, ["raw", "R"], ["R"])
                act(dst_tile[:, c0:c0 + n], a, AF.Silu, ["R", "pvec"], [ck_], bias=bia)
            hv = histg[:, hist_t, :].rearrange("p (b r) -> p b r", r=3)
            a = Rf[:, 0:NS]
            ts(a, hv[:, :, 0], w[0], None, ALU.mult, None, ["histg", "pvec"], ["R"])
            stt(a, hv[:, :, 1], w[1], a, ALU.mult, ALU.add, ["histg", "R"], ["R"])
            stt(a, hv[:, :, 2], w[2], a, ALU.mult, ALU.add, ["histg", "R"], ["R"])
            stt(a, raw[:, 0:NS], w[3], a, ALU.mult, ALU.add, ["raw", "R"], ["R"])
            act(dst_tile[:, 0:NS], a, AF.Silu, ["R", "pvec"], [ck_], bias=bia)
            if full:
                cp(cvo_p[:, hist_t, :], raw[:, NCOL - 3:NCOL], ["raw"], ["cvo_p"])
                ov = cvo_s[:, hist_t, :].rearrange("p (b r) -> p b r", r=3)
                cp(ov[:, :, 0:2], hv[:, :, 1:3], ["histg"], ["cvo_s"])
                cp(ov[:, :, 2], raw[:, 0:NS], ["raw"], ["cvo_s"])

        def load_hist(g):
            for (c0, ncol, t0) in ((XCH0 + 512 * g, 512, 0), (BCH0 + 128 * g, 128, 4), (CCH0 + 128 * g, 128, 5)):
                dma("sp", cstg[0:48, 0:ncol], IN("ccv")[:, c0:c0 + ncol], [], ["cstg"])
                b = bank()
                nct = ncol // 128
                for j in range(nct):
                    tr(ps[:, b, j * 48:(j + 1) * 48], cstg[0:48, j * 128:(j + 1) * 128], ident[0:48, 0:48], ["cstg", "cst"], [("ps", b)])
                cp(histg[:, t0:t0 + nct, :], ps[:, b, 0:nct * 48].rearrange("p (t r) -> p t r", t=nct), [("ps", b)], ["histg"])

        def store_conv(g):
            for (c0, ncol, t0) in ((XCH0 + 512 * g, 512, 0), (BCH0 + 128 * g, 128, 4), (CCH0 + 128 * g, 128, 5)):
                nct = ncol // 128
                b = bank()
                for j in range(nct):
                    tr(ps[0:48, b, j * 128:(j + 1) * 128], cvo_s[:, t0 + j, :], ident, ["cvo_s", "cst"], [("ps", b)])
                cp(cstg[0:48, 0:ncol], ps[0:48, b, 0:ncol], [("ps", b)], ["cstg"])
                ok = ("o_cvs", g, t0); outkeys.append(ok)
                dma("sp", cvs_o[:, c0:c0 + ncol], cstg[0:48, 0:ncol], ["cstg"], [ok])
                b = bank()
                for j in range(nct):
                    tr(ps[0:3, b, j * 128:(j + 1) * 128], cvo_p[:, t0 + j, :], ident, ["cvo_p", "cst"], [("ps", b)])
                cp(cstg[0:3, 0:ncol], ps[0:3, b, 0:ncol], [("ps", b)], ["cstg"])
                ok = ("o_cvp", g, t0); outkeys.append(ok)
                dma("sp", cvp_o[:, c0:c0 + ncol], cstg[0:3, 0:ncol], ["cstg"], [ok])

        def inproj_conv(g, full):
            load_hist(g)
            specs = [(4096 + 512 * g + 0, 2), (4096 + 512 * g + 256, 2), (8192 + 128 * g, 1)]
            specs.append((9216 + 128 * g, 1))
            tix = 0
            for (wc0, ntl) in specs:
                si, view, _ = load_w([IN("ssd_w_in")[:, wc0:wc0 + ntl * 128]], KT)
                for t in range(ntl):
                    proj_fm(view, si, t * 128, KT, xn_rhs, xn_keys, EB,
                            lambda bi, c0, n, p, pk: cp_alt(raw[:, c0:c0 + n], p, [pk], ["raw"]))
                    ch = wc0 - 4096 + t * 128
                    if tix < 4:
                        dst = xT[:, tix, :]; ck_ = "xTk"
                    elif tix == 4:
                        dst = BT; ck_ = "BTk"
                    else:
                        dst = CT; ck_ = "CTk"
                    do_conv(dst, ch // 128, tix, full, ck_)
                    tix += 1
            for c in range(8):
                pb = pbank()
                for j in range(4):
                    tr(psb[:, pb, j * 128:(j + 1) * 128], xT[:, j, P0 + c * 128:P0 + (c + 1) * 128], identb[:], ["xTk", "identb"], [("psb", pb)])
                cp_alt(xtok[:, c, :], psb[:, pb, 0:512], [("psb", pb)], ["xtok"])
            for c4 in range(2):
                pb = pbank()
                for j in range(4):
                    c = c4 * 4 + j
                    tr(psb[:, pb, j * 128:(j + 1) * 128], BT[:, P0 + c * 128:P0 + (c + 1) * 128], identb[:], ["BTk", "identb"], [("psb", pb)])
                cp_alt(Btok[:, c4 * 4:(c4 + 1) * 4, :], psb[:, pb, 0:512].rearrange("p (c n) -> p c n", c=4), [("psb", pb)], ["Btok"])
            cp(xs_conv[:, 4 * g:4 * g + 4, :], xT[:, :, 0:NS], ["xTk"], ["xs_conv"])
            cp(BsT[:, g, :], BT[:, 0:NS], ["BTk"], ["BsT"])
            cp(CsT[:, g, :], CT[:, 0:NS], ["CTk"], ["CsT"])

        def stageA(g, c, full):
            i = c % 3
            hs = slice(8 * g, 8 * g + 8)
            b = bank()
            mm(ps[:, b, 0:8], tri, dta_tok[:, c, hs], True, True, ["cst", "dta_tok"], [("ps", b)])
            mm(ps[:, b, 8:16], ones, dta_tok[:, c, hs], True, True, ["cst", "dta_tok"], [("ps", b)])
            if full:
                mm(ps[:, b, 128:256], BT[:, P0 + c * 128:P0 + (c + 1) * 128], CT[:, P0 + c * 128:P0 + (c + 1) * 128], True, True, ["BTk", "CTk"], [("ps", b)])
            cp(cs_sb[i][:, 0:16], ps[:, b, 0:16], [("ps", b)], [("cs", i)], eng="act")
            if full:
                cp(CBTb[i], ps[:, b, 128:256], [("ps", b)], [("CBT", i)], eng="act")
            tt(cs_sb[i][:, 16:24], cs_sb[i][:, 8:16], cs_sb[i][:, 0:8], ALU.subtract, [("cs", i)], [("cs", i)])
            act(cs_sb[i][:, 16:24], cs_sb[i][:, 16:24], AF.Exp, [("cs", i)], [("cs", i)])
            act(cs_sb[i][:, 24:32], cs_sb[i][:, 8:16], AF.Exp, [("cs", i)], [("cs", i)])
            tt(xdtb[i], xtok[:, c, :].rearrange("p (h d) -> p h d", h=8), dt_tok[:, c, hs].unsqueeze(2).to_broadcast([128, 8, 64]), ALU.mult,
               ["xtok", "dt_tok"], [("xdt", i)])
            if full:
                ts(ncs[i], cs_sb[i][:, 0:8], -1.0, None, ALU.mult, None, [("cs", i)], [("ncs", i)])
                tt(R, tri.unsqueeze(1).to_broadcast([128, 8, 128]), dta_tok[:, c, hs].unsqueeze(2).to_broadcast([128, 8, 128]), ALU.mult,
                   ["cst", "dta_tok"], ["R"])
                pbk = [bank(), bank()]
                Rf = arf(PX + NCOL, 1024)
                for hh in range(2):
                    mm(ps[:, pbk[hh], :], ones, Rf[:, hh * 512:(hh + 1) * 512], True, False, ["cst", "R"], [("ps", pbk[hh])])
                    mm(ps[:, pbk[hh], :], ident, mneg4, False, True, ["cst"], [("ps", pbk[hh])])
                for h in range(8):
                    act(EW[i][:, h, :], ps[:, pbk[h // 4], (h % 4) * 128:(h % 4 + 1) * 128], AF.Exp, [("ps", pbk[h // 4]), ("ncs", i)], [("EW", i)], bias=ncs[i][:, h:h + 1])

        def stageW(c):
            i = c % 3
            tt(EW[i], EW[i], CBTb[i].unsqueeze(1).to_broadcast([128, 8, 128]), ALU.mult, [("EW", i), ("CBT", i)], [("EW", i)])

        def stageT(c):
            k2 = c % 2
            pb = pbank()
            for j in range(4):
                tr(psb[:, pb, j * 128:(j + 1) * 128], gnb[k2][:, j * 128:(j + 1) * 128], identb[:], [("gn", k2), "identb"], [("psb", pb)])
            for j in range(4):
                cp(xT[:, j, P0 + c * 128:P0 + (c + 1) * 128], psb[:, pb, j * 128:(j + 1) * 128], [("psb", pb)], ["xTk"], eng="act")

        def stageB(g, c, full):
            i = c % 3
            if full:
                by = bank()
                for h in range(8):
                    mm(ps[:, by, h * 64:(h + 1) * 64], EW[i][:, h, :], xdtb[i][:, h, :], True, True, [("EW", i), ("xdt", i)], [("ps", by)])
                bo = bank()
                mm(ps[:, bo, :], CT[:, P0 + c * 128:P0 + (c + 1) * 128], Sbf, True, True, ["CTk", "Sbf"], [("ps", bo)])
                act(tstat[:, 0:8], cs_sb[i][:, 0:8], AF.Exp, [("cs", i)], ["expcs"])
                tt(gg.rearrange("p (h d) -> p h d", h=8), ps[:, bo, :].rearrange("p (h d) -> p h d", h=8),
                   tstat[:, 0:8].unsqueeze(2).to_broadcast([128, 8, 64]), ALU.mult, [("ps", bo), "expcs"], ["gg"])
                tt(gg, gg, ps[:, by, :], ALU.add, ["gg", ("ps", by)], ["gg"])
                tt(tmpA.rearrange("p (h d) -> p h d", h=8), xtok[:, c, :].rearrange("p (h d) -> p h d", h=8),
                   d_bc[:, 8 * g:8 * g + 8].unsqueeze(2).to_broadcast([128, 8, 64]), ALU.mult, ["xtok", "tokbc"], ["tmpA"])
                tt(gnb[c % 2], gg, tmpA, ALU.add, ["gg", "tmpA"], [("gn", c % 2)])
            tt(xdte, xdtb[i], cs_sb[i][:, 16:24].unsqueeze(2).to_broadcast([128, 8, 64]), ALU.mult, [("xdt", i), ("cs", i)], ["xdte"])
            bs_ = bank()
            mm(ps[:, bs_, :], Btok[:, c, :], xdte.rearrange("p h d -> p (h d)"), True, True, ["Btok", "xdte"], [("ps", bs_)])
            tt(Sst.rearrange("p (h d) -> p h d", h=8), Sst.rearrange("p (h d) -> p h d", h=8),
               cs_sb[i][:, 24:32].unsqueeze(2).to_broadcast([128, 8, 64]), ALU.mult, ["Sst", ("cs", i)], ["Sst"])
            tt(Sst, Sst, ps[:, bs_, :], ALU.add, ["Sst", ("ps", bs_)], ["Sst"])
            cp(Sbf, Sst, ["Sst"], ["Sbf"], eng="act")

        def scan(g, full):
            if full:
                dma("sp", smid, cc_dst[128 * g:128 * (g + 1), :], ["cc_dst"], ["smid", "R"])
                ts(Sst, smid, flag, None, ALU.mult, None, ["smid", "R", "cst"], ["Sst"])
            else:
                vop(lambda e: e.memset(Sst, 0.0), [], ["Sst"])
            cp(Sbf, Sst, ["Sst"], ["Sbf"], eng="act")
            if full:
                stageA(g, 0, True); stageA(g, 1, True); stageW(0)
                for c in range(8):
                    if c + 2 < 8:
                        stageA(g, c + 2, True)
                    stageB(g, c, True)
                    if c + 1 < 8:
                        stageW(c + 1)
                    if c >= 1:
                        stageT(c - 1)
                stageT(7)
            else:
                stageA(g, 0, False)
                for c in range(8):
                    if c + 1 < 8:
                        stageA(g, c + 1, False)
                    stageB(g, c, False)

        L64 = lg[:, 0:128]

        def sample_group(g):
            ts(rg[0:64, :], rallb[0:64, :], lg[:, 132 + g:133 + g], None, ALU.mult, None, ["rall", "lg"], ["rg"])
            b = bank()
            mm(ps[:, b, 0:132], L64, rg[0:64, :], True, True, ["lg", "rg"], [("ps", b)])
            cp(dq, ps[:, b, 0:132], [("ps", b)], ["dq"])
            tt(xdts, xs_conv[:, 4 * g:4 * g + 4, :].rearrange("p j b -> p b j"), dq[:, 0:64].rearrange("p (b j) -> p b j", b=NS), ALU.mult,
               ["xs_conv", "dq"], ["xdts"])
            pb = pbank()
            tr(psb[0:16, pb, 0:128], BsT[:, g, :], identb[:], ["BsT", "identb"], [("psb", pb)])
            tr(psb[0:16, pb, 128:256], CsT[:, g, :], identb[:], ["CsT", "identb"], [("psb", pb)])
            cp(bcs[0:16, :], psb[0:16, pb, 0:256], [("psb", pb)], ["bcs"])
            for bsm in range(NS):
                st = stS[bsm % 2]; sk = "cstg" if bsm % 2 == 0 else ("stS", 1)
                dma("sp", st, IN("sst")[bsm, 512 * g:512 * (g + 1), :].rearrange("(j q) n -> q j n", q=128), [], [sk])
                b = bank()
                mm(ps[:, b, 0:256], selb[:, bsm * 128:(bsm + 1) * 128], bcs[0:16, :], True, True, ["selb", "bcs"], [("ps", b)])
                tt(t1, xdts[:, bsm, :].unsqueeze(2).to_broadcast([128, 4, 128]), ps[:, b, 0:128].unsqueeze(1).to_broadcast([128, 4, 128]), ALU.mult,
                   ["xdts", ("ps", b)], ["t1"])
                tt(st, st, dq[:, 64 + 4 * bsm:64 + 4 * bsm + 4].unsqueeze(2).to_broadcast([128, 4, 128]), ALU.mult, [sk, "dq"], [sk])
                tt(st, st, t1, ALU.add, [sk, "t1"], [sk])
                ok = ("o_sss", g, bsm); outkeys.append(ok)
                dma("sp", sss_o[bsm, 512 * g:512 * (g + 1), :].rearrange("(j q) n -> q j n", q=128), st, [sk], [ok])
                tt(t1, st, ps[:, b, 128:256].unsqueeze(1).to_broadcast([128, 4, 128]), ALU.mult, [sk, ("ps", b)], ["t1"])
                vop(lambda e: e.tensor_reduce(out=tstat[:, 12:16], in_=t1, axis=AX.X, op=ALU.add), ["t1"], ["ysr"])
                tt(tstat[:, 4:8], xs_conv[:, 4 * g:4 * g + 4, bsm], dq[:, 128:132], ALU.mult, ["xs_conv", "dq"], ["ydx"])
                tt(y_s[:, 4 * g:4 * g + 4, bsm], tstat[:, 12:16], tstat[:, 4:8], ALU.add, ["ysr", "ydx"], ["y_s"])

        xw = ar[:, 2560:3584].bitcast(BF16).rearrange("p (c n) -> p c n", c=4)

        def state_only(g):
            hs = slice(8 * g, 8 * g + 8)
            b = bank()
            for c in range(8):
                mm(ps[:, b, c * 8:(c + 1) * 8], tri, dta_tok[:, c, hs], True, c == 0, ["cst", "dta_tok"], [("ps", b)])
                for c2 in range(c):
                    mm(ps[:, b, c * 8:(c + 1) * 8], ones, dta_tok[:, c2, hs], False, c2 == c - 1, ["cst", "dta_tok"], [("ps", b)])
            for c in range(8):
                mm(ps[:, b, 64:72], ones, dta_tok[:, c, hs], c == 0, c == 7, ["cst", "dta_tok"], [("ps", b)])
            wg = R.rearrange("p h t -> p (h t)")[:, 0:72]
            cp(wg, ps[:, b, 0:72], [("ps", b)], ["R"], eng="act")
            wg3 = wg[:, 0:64].rearrange("p (c h) -> p c h", c=8)
            tt(wg3, wg[:, 64:72].unsqueeze(1).to_broadcast([128, 8, 8]), wg3, ALU.subtract, ["R"], ["R"])
            act(wg[:, 0:64], wg[:, 0:64], AF.Exp, ["R"], ["R"])
            tt(wg3, wg3, dt_tok[:, :, hs], ALU.mult, ["R", "dt_tok"], ["R"])
            bs_ = bank()
            for half in range(2):
                tt(xw.rearrange("p c (h d) -> p c h d", h=8), xtok[:, half * 4:(half + 1) * 4, :].rearrange("p c (h d) -> p c h d", h=8),
                   wg3[:, half * 4:(half + 1) * 4, :].unsqueeze(3).to_broadcast([128, 4, 8, 64]), ALU.mult, ["xtok", "R"], ["xw"])
                for cc in range(4):
                    c = half * 4 + cc
                    mm(ps[:, bs_, :], Btok[:, c, :], xw[:, cc, :], c == 0, c == 7, ["Btok", "xw"], [("ps", bs_)])
            cp(Sst, ps[:, bs_, :], [("ps", bs_)], ["Sst"], eng="act")

        for g in range(8):
            inproj_conv(g, False)
            if g > 0:
                sample_group(g - 1)
            state_only(g)
            dma("sp", cc_src[128 * g:128 * (g + 1), :], Sst, ["Sst"], [("cc_src", g)])
        S.op("pool", lambda e: e.collective_compute("AllGather", ALU.bypass, replica_groups=[[0, 1], [2, 3], [4, 5], [6, 7]],
                                                     ins=[cc_src.opt()], outs=[cc_dst.opt()]),
             [("cc_src", g) for g in range(8)], ["cc_dst"])
        sample_group(7)
        barrier()

        for g in range(8):
            inproj_conv(g, True)
            store_conv(g)
            scan(g, True)
            b = bank()
            for j in range(4):
                tr(ps[:, b, j * 128:(j + 1) * 128], Sst[:, j * 128:(j + 1) * 128], ident, ["Sst", "cst"], [("ps", b)])
            cp(tmpA, ps[:, b, :], [("ps", b)], ["tmpA"])
            ok = ("o_ssp", g); outkeys.append(ok)
            dma("sp", ssp_o[512 * g:512 * (g + 1), :].rearrange("(j q) n -> q j n", q=128), tmpA.rearrange("p (j n) -> p j n", j=4), ["tmpA"], [ok])
            cp(xT[:, :, 0:NS], y_s[:, 4 * g:4 * g + 4, :], ["y_s"], ["xTk"])
            for half in range(2):
                si, view, _ = load_w([IN("ssd_w_in")[:, 512 * g + 256 * half:512 * g + 256 * (half + 1)]], KT)
                for t in range(2):
                    j = half * 2 + t
                    def ev(bi, c0, n, p, pk, j=j):
                        act(gg[:, 0:n], p, AF.Silu, [pk], ["gg"])
                        tt(xT[:, j, c0:c0 + n], xT[:, j, c0:c0 + n], gg[:, 0:n], ALU.mult, ["xTk", "gg"], [("gT", j, bi)])
                    proj_fm(view, si, t * 128, KT, xn_rhs, xn_keys, EB, ev)
            rmsnorm_cols(lambda kt, c0, n: xT[:, kt, c0:c0 + n], lambda kt, bi: ("gT", kt, bi),
                         lambda kt, c0, n: xT[:, kt, c0:c0 + n], lambda kt, bi: ("gT", kt, bi),
                         PV_SNORM + 4 * g, 4, EB, 512.0)
            out_proj(lambda c0, n, g=g: IN("ssd_w_out")[512 * g:512 * (g + 1), c0:c0 + n],
                     lambda k, c0, n: xT[:, k, c0:c0 + n], lambda k, bi: [("gT", k, bi), "xTk"], 4)
            vop(lambda e: e.memset(tstat[:, 1:2], 0.0), [("gT", k, bi) for k in range(4) for bi in range(3)], ["xTk"])
        barrier()

    import os
    kstop = int(os.environ.get("KSTOP", "7"))

    def ssd_phase():
        mixer_ssd()
        S.op("dve", lambda e: e.memset(sm[:, 508:509], 0.0), [], [("stg", 0), ("stg", 1), "fence2"])
        barrier()

    phases = [mixer_sc, lambda: xattn(0, True), lambda: ffn(0), ssd_phase, lambda: xattn(1, False), lambda: ffn(1)]
    for ph in phases[:kstop]:
        ph()
    if kstop >= 7:
        rmsnorm_cols(lambda kt, c0, n: hT[:, kt, c0:c0 + n], hkeys, lambda kt, c0, n: hT[:, kt, c0:c0 + n], hkeys, PV_NFIN, KT, EB, float(D))
    if KDBG >= 5:
        barrier()
    if KDBG >= 6:
        out_transpose(lambda ct: hT[:, ct, S0:S0 + NS], all_h_keys, NS, D, ys_o, "o_ys")
    for i in range(8 if KDBG >= 7 else 0):
        out_transpose((lambda ct, i=i: hT[:, ct, P0 + i * 128:P0 + (i + 1) * 128]), all_h_keys, 128, D, yp_o[i * 128:(i + 1) * 128, :], ("o_yp", i))
    S.op("sp", lambda e: None, outkeys, [], real=False)

    sems = {e: es.enter_context(nc.semaphore("sem_" + e)) for e in Sched.ENGS}
    dsems = {q: [es.enter_context(nc.semaphore(f"dsem_{q}{i}")) for i in range(S.NDS)] for q in ("pool", "sp")}
    block = es.enter_context(nc.Block())
    S.replay(nc, block, sems, dsems)
    es.close()
    nc._used_inputs = sorted(_in_aps.keys())
    return nc


def _fm(v, nt):
    return np.ascontiguousarray(np.asarray(v, np.float32).reshape(nt, 128).T)


def _consts():
    cst = np.zeros((128, 1160), np.float32)
    cst[:, 0:128] = np.eye(128, dtype=np.float32)
    s = np.arange(128)
    tri = (s[:, None] <= s[None, :]).astype(np.float32)
    cst[:, 128:256] = tri
    cst[:, 256:384] = 1.0
    cst[:, 384:896] = np.tile((1.0 - tri) * -30000.0, (1, 4))
    cst[:, 896:1152] = np.tile(np.eye(16, dtype=np.float32).reshape(1, 256), (128, 1))
    sel = np.zeros((16, 16, 128), np.float32)
    for b in range(16):
        sel[b, b, :] = 1.0
    lg = np.zeros((64, 140), np.float32)
    k = np.arange(64)
    q = np.arange(128)
    lg[:, 0:128] = ((k[:, None] % 2) == (q[None, :] // 64)).astype(np.float32)
    lg[:, 128:132] = (((k[:, None] % 8) // 2) == np.arange(4)[None, :]).astype(np.float32)
    lg[:, 132:140] = ((k[:, None] // 8) == np.arange(8)[None, :]).astype(np.float32)
    return cst, sel.reshape(16, 2048), lg


_PROG = {}


def kernel(x_prompt, x_sample, mem_prompt, cache_sc, state_ssd_conv, state_ssd, cache_mem_k, cache_mem_v,
           norm_mix, norm_mem_q, norm_mem_kv, norm_ffn, norm_final,
           sc_w_in, sc_w_conv, sc_w_out,
           ssd_w_in, ssd_conv_w, ssd_conv_b, ssd_dt_bias, ssd_a_log, ssd_d, ssd_norm, ssd_w_out,
           xa_w_q, xa_w_k, xa_w_v, xa_w_o, ffn_w_gate, ffn_w_up, ffn_w_down):
    f32 = np.float32
    A = lambda a: np.ascontiguousarray(np.asarray(a, f32))
    x_prompt = A(x_prompt); x_sample = A(x_sample); mem_prompt = A(mem_prompt)
    cache_sc = A(cache_sc); state_ssd_conv = A(state_ssd_conv); state_ssd = A(state_ssd)
    cache_mem_k = A(cache_mem_k); cache_mem_v = A(cache_mem_v)

    pvec = np.zeros((128, 512), f32)
    for l in range(2):
        pvec[:, PV_NMIX + 16 * l:PV_NMIX + 16 * (l + 1)] = _fm(norm_mix[l], 16)
        pvec[:, PV_NQ + 16 * l:PV_NQ + 16 * (l + 1)] = _fm(norm_mem_q[l], 16)
        pvec[:, PV_NKV + 16 * l:PV_NKV + 16 * (l + 1)] = _fm(norm_mem_kv[l], 16)
        pvec[:, PV_NFFN + 16 * l:PV_NFFN + 16 * (l + 1)] = _fm(norm_ffn[l], 16)
    pvec[:, PV_NFIN:PV_NFIN + 16] = _fm(norm_final, 16)
    for r in range(3):
        pvec[:, PV_SCW + 16 * r:PV_SCW + 16 * (r + 1)] = _fm(np.asarray(sc_w_conv)[0, r], 16)
    for r in range(4):
        pvec[:, PV_CW + 48 * r:PV_CW + 48 * (r + 1)] = _fm(np.asarray(ssd_conv_w)[0, r], 48)
    pvec[:, PV_CB:PV_CB + 48] = _fm(np.asarray(ssd_conv_b)[0], 48)
    pvec[:, PV_SNORM:PV_SNORM + 32] = _fm(np.asarray(ssd_norm)[0], 32)
    tokbc = np.zeros((128, 192), f32)
    tokbc[:, 0:64] = np.asarray(ssd_dt_bias, f32)[0][None, :]
    tokbc[:, 64:128] = np.asarray(ssd_a_log, f32)[0][None, :]
    tokbc[:, 128:192] = np.asarray(ssd_d, f32)[0][None, :]
    hvec = np.zeros((64, 8), f32)
    hvec[:, 0] = np.asarray(ssd_dt_bias, f32)[0]; hvec[:, 1] = np.asarray(ssd_a_log, f32)[0]; hvec[:, 2] = np.asarray(ssd_d, f32)[0]
    cst0, sel, lg = _consts()

    shared = {
        "pvec": pvec, "tokbc": tokbc, "hvec": hvec, "selb": sel, "lg": lg,
        "sc_w_in": A(sc_w_in)[0], "sc_w_out": A(sc_w_out)[0], "ssd_w_in": A(ssd_w_in)[0], "ssd_w_out": A(ssd_w_out)[0],
        "xa_w_q": A(xa_w_q), "xa_w_k": A(xa_w_k), "xa_w_v": A(xa_w_v), "xa_w_o": A(xa_w_o),
        "ffn_w_gate": A(ffn_w_gate), "ffn_w_up": A(ffn_w_up), "ffn_w_down": A(ffn_w_down),
    }
    in_maps = []
    for c in range(NCORES):
        seq, half = c // 2, c % 2
        st = half * NP
        cst = cst0.copy()
        cst[:, 1152] = float(half)
        xh = x_prompt[seq, st - NH:st] if half else np.zeros((NH, D), f32)
        sl = slice(NS * c, NS * (c + 1))
        m = dict(shared)
        m.update({
            "xp": np.ascontiguousarray(x_prompt[seq, st:st + NP]), "xh": np.ascontiguousarray(xh),
            "xs": np.ascontiguousarray(x_sample[sl, 0]), "mem": np.ascontiguousarray(mem_prompt[seq]),
            "csc": np.ascontiguousarray(cache_sc[0, sl].reshape(2 * NS, D)),
            "ccv": np.ascontiguousarray(state_ssd_conv[0, sl].reshape(3 * NS, CONV)),
            "sst": np.ascontiguousarray(state_ssd[0, sl].reshape(NS, 4096, 128)),
            "ck": np.ascontiguousarray(cache_mem_k[:, sl].reshape(2, NS, 256, D)),
            "cv": np.ascontiguousarray(cache_mem_v[:, sl].reshape(2, NS, 256, D)),
            "cst": cst,
        })
        in_maps.append(m)

    if "nc" not in _PROG:
        _PROG["nc"] = build_program()
    used = _PROG["nc"]._used_inputs
    in_maps = [{k: m[k] for k in used} for m in in_maps]
    res = run_bass_kernel_spmd(_PROG["nc"], in_maps, core_ids=list(range(NCORES)))
    R = res.results

    y_prompt = np.zeros((4, 2048, D), f32); y_sample = np.zeros((128, 1, D), f32)
    sc_p = np.zeros((1, 4, 2, D), f32); sc_s = np.zeros((1, 128, 2, D), f32)
    cv_p = np.zeros((1, 4, 3, CONV), f32); cv_s = np.zeros((1, 128, 3, CONV), f32)
    ss_p = np.zeros((1, 4, 64, 64, 128), f32); ss_s = np.zeros((1, 128, 64, 64, 128), f32)
    mk = np.zeros((2, 4, 256, 4, 512), f32); mv = np.zeros((2, 4, 256, 4, 512), f32)
    for c in range(NCORES):
        seq, half = c // 2, c % 2
        sl = slice(NS * c, NS * (c + 1))
        r = R[c]
        y_prompt[seq, half * NP:(half + 1) * NP] = r["y_p"]
        y_sample[sl, 0] = r["y_s"]
        sc_s[0, sl] = r["sc_s"].reshape(NS, 2, D)
        cv_s[0, sl] = r["cv_s"].reshape(NS, 3, CONV)
        ss_s[0, sl] = r["ss_s"].reshape(NS, 64, 64, 128)
        if half == 1:
            sc_p[0, seq] = r["sc_p"]
            cv_p[0, seq] = r["cv_p"]
            ss_p[0, seq] = r["ss_p"].reshape(64, 64, 128)
        else:
            mk[:, seq] = r["mk"].reshape(2, 256, 4, 512)
            mv[:, seq] = r["mv"].reshape(2, 256, 4, 512)
    return (y_prompt, y_sample, sc_p, sc_s, cv_p, cv_s, ss_p, ss_s, mk, mv)
```
